# Optimizing a Trainium2 kernel written in Bass

```python
import math
import jax, jax.numpy as jnp
from jax import lax
import numpy as np

D_MODEL = 1024
BATCH = 4
SEQ = 8192
DEPTH = 4

D_MIX = 1024
A_HEADS = 8
A_HEAD_DIM = 64
A_WIDTH = A_HEADS * A_HEAD_DIM
IDX_HEADS = 8
IDX_DIM = 64
TOPK_MAX = 256
Q_BLOCK = 128
B_WIDTH = 256
CONV_WIDTH = 31
C_HEADS = 4
C_KEY_DIM = 32
C_VAL_DIM = 64
C_WIDTH = C_HEADS * C_VAL_DIM
GATE_RANK = 16
GATE_TAU = 16.0
GLA_CHUNK = 64
ROPE_THETA = 10000.0
ROPE_DIM = 64
DEEPNORM_ALPHA = (2 * DEPTH) ** 0.25
DEEPNORM_BETA = (8 * DEPTH) ** -0.25
EPS = 1e-5

IN_SIZES = (A_WIDTH, A_WIDTH, A_WIDTH, A_WIDTH,
            IDX_HEADS * IDX_DIM, IDX_DIM, IDX_HEADS,
            2 * B_WIDTH, B_WIDTH,
            C_HEADS * C_KEY_DIM, C_HEADS * C_KEY_DIM,
            C_WIDTH, C_WIDTH, GATE_RANK)
D_IN = sum(IN_SIZES)

kernel_name = "hybrid_dsa_conformer_gla_deepnorm"


def _split_in(h):
    points, acc = [], 0
    for s in IN_SIZES[:-1]:
        acc += s
        points.append(acc)
    return jnp.split(h, points, axis=-1)


def _layernorm(x, g, b):
    xf = x.astype(jnp.float32)
    mu = jnp.mean(xf, axis=-1, keepdims=True)
    var = jnp.mean(jnp.square(xf - mu), axis=-1, keepdims=True)
    y = (xf - mu) * lax.rsqrt(var + EPS) * g.astype(jnp.float32) + b.astype(jnp.float32)
    return y.astype(x.dtype)


def _rope_tables(positions):
    inv = ROPE_THETA ** (-jnp.arange(0, ROPE_DIM, 2, dtype=jnp.float32) / ROPE_DIM)
    ang = positions.astype(jnp.float32)[..., None] * inv
    return jnp.cos(ang), jnp.sin(ang)


def _rope(x, cos, sin):
    x1, x2 = jnp.split(x, 2, axis=-1)
    c = cos[:, :, None, :].astype(x.dtype)
    s = sin[:, :, None, :].astype(x.dtype)
    return jnp.concatenate([x1 * c - x2 * s, x2 * c + x1 * s], axis=-1)


def _dsa_attention(q, k, v, qi, ki, wi):
    B, T, H, dh = q.shape
    topk = min(TOPK_MAX, T // 4)
    nb = T // Q_BLOCK
    scale = dh ** -0.5
    s_pos = jnp.arange(T)

    def to_blocks(a):
        return jnp.moveaxis(a.reshape((B, nb, Q_BLOCK) + a.shape[2:]), 1, 0)

    def block(args):
        qb, qib, wib, start = args
        t_pos = start + jnp.arange(Q_BLOCK)
        causal = s_pos[None, :] <= t_pos[:, None]
        idx_logits = jnp.einsum('bqhd,bsd->bqhs', qib, ki)
        score = jnp.einsum('bqh,bqhs->bqs', wib, jax.nn.relu(idx_logits)).astype(jnp.float32)
        score = jnp.where(causal[None], score, -jnp.inf)
        _, sel = lax.top_k(score, topk)
        k_sel = jax.vmap(lambda kb, ib: kb[ib])(k, sel)
        v_sel = jax.vmap(lambda vb, ib: vb[ib])(v, sel)
        logits = jnp.einsum('bqhd,bqkhd->bhqk', qb, k_sel).astype(jnp.float32) * scale
        valid = sel <= t_pos[None, :, None]
        logits = jnp.where(valid[:, None], logits, -jnp.inf)
        p = jax.nn.softmax(logits, axis=-1).astype(v.dtype)
        return jnp.einsum('bhqk,bqkhd->bqhd', p, v_sel)

    starts = jnp.arange(nb) * Q_BLOCK
    out = lax.map(block, (to_blocks(q), to_blocks(qi), to_blocks(wi), starts))
    return jnp.moveaxis(out, 0, 1).reshape(B, T, H, dh)


def _conformer_conv(u, conv_w, conv_b, ln_g, ln_b, pw_w, pw_b):
    val, gate = jnp.split(u, 2, axis=-1)
    h = val * jax.nn.sigmoid(gate)
    h = lax.conv_general_dilated(h, conv_w[:, None, :].astype(h.dtype), window_strides=(1,),
                                 padding=((CONV_WIDTH - 1, 0),),
                                 dimension_numbers=('NWC', 'WIO', 'NWC'),
                                 feature_group_count=B_WIDTH) + conv_b
    h = jax.nn.silu(_layernorm(h, ln_g, ln_b))
    return h @ pw_w + pw_b


def _gla(q, k, v, log_a):
    B, T, H, dk = q.shape
    dv = v.shape[-1]
    C = GLA_CHUNK
    n = T // C

    def to_chunks(a):
        return a.reshape(B, n, C, H, a.shape[-1]).transpose(1, 0, 3, 2, 4).astype(jnp.float32)

    qc, kc, vc, gc = to_chunks(q * (dk ** -0.5)), to_chunks(k), to_chunks(v), to_chunks(log_a)
    causal = jnp.tril(jnp.ones((C, C), dtype=bool))[None, None, :, :, None]

    def step(S, inp):
        qb, kb, vb, gb = inp
        b = jnp.cumsum(gb, axis=2)
        o_inter = jnp.einsum('bhcd,bhde->bhce', qb * jnp.exp(b), S)
        diff = b[:, :, :, None, :] - b[:, :, None, :, :]
        decay = jnp.exp(jnp.where(causal, diff, -jnp.inf))
        A = jnp.einsum('bhid,bhjd,bhijd->bhij', qb, kb, decay)
        o_intra = jnp.einsum('bhij,bhje->bhie', A, vb)
        b_last = b[:, :, -1:, :]
        S_new = jnp.exp(b_last[:, :, 0, :])[..., None] * S + \
            jnp.einsum('bhcd,bhce->bhde', kb * jnp.exp(b_last - b), vb)
        return S_new, o_inter + o_intra

    S0 = jnp.zeros((B, H, dk, dv), jnp.float32)
    _, o = lax.scan(step, S0, (qc, kc, vc, gc))
    return o.transpose(1, 0, 3, 2, 4).reshape(B, T, H, dv)


def _layer(x, cos, sin, w_in, conv_w, conv_b, cln_g, cln_b, pw_w, pw_b,
           gate_w2, gate_b, gnorm_g, w_out, ln_g, ln_b):
    B, T, _ = x.shape
    h = x @ w_in
    (a_q, a_k, a_v, a_g, i_q, i_k, i_w, b_glu, b_g,
     c_q, c_k, c_v, c_g, c_lr) = _split_in(h)

    q = _rope(a_q.reshape(B, T, A_HEADS, A_HEAD_DIM), cos, sin)
    k = _rope(a_k.reshape(B, T, A_HEADS, A_HEAD_DIM), cos, sin)
    v = a_v.reshape(B, T, A_HEADS, A_HEAD_DIM)
    qi = _rope(i_q.reshape(B, T, IDX_HEADS, IDX_DIM), cos, sin)
    ki = _rope(i_k[:, :, None, :], cos, sin)[:, :, 0, :]
    wi = i_w * (IDX_HEADS ** -0.5 * IDX_DIM ** -0.5)
    y_a = _dsa_attention(q, k, v, qi, ki, wi).reshape(B, T, A_WIDTH) * jax.nn.silu(a_g)

    y_b = _conformer_conv(b_glu, conv_w, conv_b, cln_g, cln_b, pw_w, pw_b) * jax.nn.silu(b_g)

    log_a = jax.nn.log_sigmoid((c_lr @ gate_w2 + gate_b).astype(jnp.float32)) / GATE_TAU
    o = _gla(c_q.reshape(B, T, C_HEADS, C_KEY_DIM), c_k.reshape(B, T, C_HEADS, C_KEY_DIM),
             c_v.reshape(B, T, C_HEADS, C_VAL_DIM), log_a.reshape(B, T, C_HEADS, C_KEY_DIM))
    o = o * lax.rsqrt(jnp.mean(jnp.square(o), axis=-1, keepdims=True) + EPS) * \
        gnorm_g.astype(jnp.float32).reshape(C_HEADS, C_VAL_DIM)
    y_c = o.reshape(B, T, C_WIDTH).astype(x.dtype) * jax.nn.silu(c_g)

    y = jnp.concatenate([y_a.astype(x.dtype), y_b.astype(x.dtype), y_c], axis=-1) @ w_out
    return _layernorm(DEEPNORM_ALPHA * x + y, ln_g, ln_b)


def setup_inputs(seed: int = 0) -> dict:
    key = jax.random.key(seed)
    ks = jax.random.split(key, 16)

    def nrm(k, shape, scale):
        return jax.random.normal(k, shape, jnp.float32) * scale

    x = nrm(ks[0], (BATCH, SEQ, D_MODEL), 1.0)
    positions = jnp.broadcast_to(jnp.arange(SEQ, dtype=jnp.int32), (BATCH, SEQ))
    w_in = nrm(ks[1], (DEPTH, D_MODEL, D_IN), D_MODEL ** -0.5)
    conv_w = nrm(ks[2], (DEPTH, CONV_WIDTH, B_WIDTH), CONV_WIDTH ** -0.5)
    conv_b = nrm(ks[3], (DEPTH, B_WIDTH), 0.02)
    cln_g = 1.0 + nrm(ks[4], (DEPTH, B_WIDTH), 0.05)
    cln_b = nrm(ks[5], (DEPTH, B_WIDTH), 0.02)
    pw_w = nrm(ks[6], (DEPTH, B_WIDTH, B_WIDTH), B_WIDTH ** -0.5)
    pw_b = nrm(ks[7], (DEPTH, B_WIDTH), 0.02)
    gate_w2 = nrm(ks[8], (DEPTH, GATE_RANK, C_HEADS * C_KEY_DIM), GATE_RANK ** -0.5)
    gate_b = nrm(ks[9], (DEPTH, C_HEADS * C_KEY_DIM), 0.1)
    gnorm_g = 1.0 + nrm(ks[10], (DEPTH, C_WIDTH), 0.05)
    w_out = nrm(ks[11], (DEPTH, D_MIX, D_MODEL), DEEPNORM_BETA * D_MIX ** -0.5)
    ln_g = 1.0 + nrm(ks[12], (DEPTH, D_MODEL), 0.05)
    ln_b = nrm(ks[13], (DEPTH, D_MODEL), 0.02)
    return {"x": x, "positions": positions, "w_in": w_in, "conv_w": conv_w, "conv_b": conv_b,
            "cln_g": cln_g, "cln_b": cln_b, "pw_w": pw_w, "pw_b": pw_b,
            "gate_w2": gate_w2, "gate_b": gate_b, "gnorm_g": gnorm_g,
            "w_out": w_out, "ln_g": ln_g, "ln_b": ln_b}


def reference(x, positions, w_in, conv_w, conv_b, cln_g, cln_b, pw_w, pw_b,
              gate_w2, gate_b, gnorm_g, w_out, ln_g, ln_b):
    cos, sin = _rope_tables(positions)
    for i in range(DEPTH):
        x = _layer(x, cos, sin, w_in[i], conv_w[i], conv_b[i], cln_g[i], cln_b[i],
                   pw_w[i], pw_b[i], gate_w2[i], gate_b[i], gnorm_g[i],
                   w_out[i], ln_g[i], ln_b[i])
    return x
```

```python
import math
import os
from contextlib import ExitStack
import numpy as np
import concourse.bass as bass
import concourse.mybir as mybir
from concourse.bass_utils import run_bass_kernel_spmd

F32 = mybir.dt.float32
BF16 = mybir.dt.bfloat16
I32 = mybir.dt.int32
ALU = mybir.AluOpType
AF = mybir.ActivationFunctionType
AX = mybir.AxisListType

D_MODEL = 1024
D_IN = 4184
TOPK_MAX = 256
CONVW = 31
EPS = 1e-5
NEG = -30000.0
NBIS = 16


class Buf:
    __slots__ = ("writers", "readers", "old", "name", "excl")

    def __init__(self, name="", excl=False):
        self.excl = excl
        self.writers = []
        self.readers = []
        self.old = []
        self.name = name


class DSem:
    def __init__(self, sem):
        self.sem = sem
        self.count = 0


class Eng:
    def __init__(self, name, sem):
        self.name = name
        self.sem = sem
        self.count = 0
        self.known = {}
        self.prog = []


class _Rec:
    def __init__(self):
        self.calls = []

    def __getattr__(self, name):
        def f(*a, **k):
            self.calls.append((name, a, k))
            return self
        return f


class Sched:
    def __init__(self, nc, es):
        self.nc = nc
        self.es = es
        self.engs = {}
        for n in ("pe", "act", "dve", "pool", "sp"):
            self.engs[n] = Eng(n, es.enter_context(nc.semaphore("s_" + n)))
        self.dsems = {}
        self.nops = 0

    def dsem(self, key):
        if key not in self.dsems:
            self.dsems[key] = DSem(self.es.enter_context(self.nc.semaphore("d_%d" % len(self.dsems))))
        return self.dsems[key]

    def _compress(self, lst):
        if len(lst) > 6:
            best = {}
            for (s, v) in lst:
                k = id(s)
                if k not in best or best[k][1] < v:
                    best[k] = (s, v)
            lst[:] = list(best.values())

    def op(self, eng, fn, reads=(), writes=(), appends=(), dsem=None, ndma=1):
        e = self.engs[eng]
        deps = {}

        def need(t):
            k = id(t[0])
            if k not in deps or deps[k][1] < t[1]:
                deps[k] = t

        for b in reads:
            for t in b.writers:
                need(t)
            if b.excl:
                for t in b.readers:
                    if t[0] is not e.sem:
                        need(t)
        for b in writes:
            for t in b.writers:
                need(t)
            for t in b.readers:
                need(t)
        for b in appends:
            for t in b.readers:
                need(t)
            for t in b.old:
                need(t)
        waits = []
        for k, (s, v) in deps.items():
            if eng == "pe" and s is e.sem:
                continue
            if e.known.get(k, 0) < v:
                waits.append((s, v))
                e.known[k] = v
        if dsem is None:
            e.count += 1
            t = (e.sem, e.count)
            inc = (e.sem, 1)
        else:
            dsem.count += 16 * ndma
            t = (dsem.sem, dsem.count)
            inc = (dsem.sem, 16)
        rec = _Rec()
        fn(rec)
        e.prog.append((waits, rec.calls, inc))
        self.nops += 1
        for b in reads:
            b.readers.append(t)
            self._compress(b.readers)
        for b in writes:
            b.old = b.writers + b.readers
            self._compress(b.old)
            b.writers = [t]
            b.readers = []
        for b in appends:
            b.writers.append(t)
            self._compress(b.writers)
        return t

    def barrier(self):
        for e in self.engs.values():
            waits = []
            for f in self.engs.values():
                if f is e or f.count == 0:
                    continue
                if e.known.get(id(f.sem), 0) < f.count:
                    waits.append((f.sem, f.count))
                    e.known[id(f.sem)] = f.count
            for d in self.dsems.values():
                if d.count and e.known.get(id(d.sem), 0) < d.count:
                    waits.append((d.sem, d.count))
                    e.known[id(d.sem)] = d.count
            if waits:
                e.prog.append((waits, None, None))

    def emit(self):
        nc = self.nc
        hmap = {"pe": "tensor", "act": "scalar", "dve": "vector", "pool": "gpsimd", "sp": "sync"}
        with nc.Block() as block:
            for n, e in self.engs.items():
                def body(h, e=e):
                    for waits, fn, inc in e.prog:
                        for (s, v) in waits:
                            h.wait_ge(s, v)
                        if fn is None:
                            continue
                        for (name, a, k) in fn:
                            getattr(h, name)(*a, **k).then_inc(inc[0], inc[1])
                getattr(block, hmap[n])(body)


class Ring:
    def __init__(self, items):
        self.items = items
        self.i = 0

    def next(self):
        it = self.items[self.i % len(self.items)]
        self.i += 1
        return it


def build_program(T, DEPTH, dbg=False):
    NT = T // 128
    topk = min(TOPK_MAX, T // 4)
    alpha = (2 * DEPTH) ** 0.25
    nc = bass.Bass("TRN2", target_bir_lowering=False)
    es = ExitStack()
    S = Sched(nc, es)

    def din(name, shape, dt=F32):
        return nc.dram_tensor(name, list(shape), dt, kind="ExternalInput").ap()

    def dscr(name, shape, dt):
        return nc.dram_tensor(name, list(shape), dt, kind="ExternalOutput" if dbg else "Internal").ap()

    x_in = din("x", [T, D_MODEL])
    pos_in = din("positions", [128, T // 128], I32)
    w_in = din("w_in", [DEPTH, D_MODEL, D_IN])
    conv_w = din("conv_w", [DEPTH, 2, 128, CONVW])
    conv_b = din("conv_b", [DEPTH, 128, 2])
    cln_g = din("cln_g", [DEPTH, 128, 2])
    cln_b = din("cln_b", [DEPTH, 128, 2])
    pw_w = din("pw_w", [DEPTH, 256, 256])
    pw_b = din("pw_b", [DEPTH, 128, 2])
    gate_w2 = din("gate_w2", [DEPTH, 16, 128])
    gate_b = din("gate_b", [DEPTH, 128])
    gnorm_g = din("gnorm_g", [DEPTH, 128, 256])
    w_out = din("w_out", [DEPTH, 1024, 1024])
    ln_g = din("ln_g", [DEPTH, 128, 1024])
    ln_b = din("ln_b", [DEPTH, 128, 1024])
    c_ident = din("c_ident", [128, 128])
    c_tri = din("c_tri", [128, 128])
    c_cbias = din("c_cbias", [128, 128])
    c_hmask = din("c_hmask", [128, 4])
    c_inv8 = din("c_inv8", [128, 256])
    c_pow2 = din("c_pow2", [128, 32])
    out_d = nc.dram_tensor("out", [T, D_MODEL], F32, kind="ExternalOutput").ap()

    xcur = dscr("xcur", [T, D_MODEL], F32)
    xT_d = dscr("xT", [D_MODEL, T], BF16)
    cs_d = dscr("cs", [T, 512], F32)
    qT_d = dscr("qT", [4, 128, T], BF16)
    kT_d = dscr("kT", [4, 128, T], BF16)
    qiT_d = dscr("qiT", [4, 128, T], BF16)
    kiT_d = dscr("kiT", [64, T], BF16)
    wi_d = dscr("wi", [T, 8], F32)
    V_d = dscr("V", [T, 520], BF16)
    ag_d = dscr("ag", [T, 512], F32)
    ybc_d = dscr("ybcT", [4, 128, T], BF16)

    dbufs = {}

    def DB(name, i):
        k = (name, i)
        if k not in dbufs:
            dbufs[k] = Buf("%s%d" % (name, i))
        return dbufs[k]

    def sb(name, shape, dt):
        return es.enter_context(nc.sbuf_tensor(name, list(shape), dt))

    def ps(name, shape, dt):
        return es.enter_context(nc.psum_tensor(name, list(shape), dt))

    def slot(name, shape, dt, n=2):
        items = []
        for i in range(n):
            t = sb("%s_%d" % (name, i), shape, dt)
            items.append((t, Buf("%s_%d" % (name, i)), S.dsem("%s_%d" % (name, i))))
        return Ring(items)

    def dma(eng, out_ap, in_ap, reads, writes, dsem, appends=()):
        return S.op(eng, lambda h: h.dma_start(out=out_ap, in_=in_ap), reads=reads, writes=writes,
                    appends=appends, dsem=dsem)

    def dma_multi(eng, pairs, reads, writes, dsem):
        return S.op(eng, lambda h: [h.dma_start(out=o, in_=i) for (o, i) in pairs], reads=reads, writes=writes,
                    dsem=dsem, ndma=len(pairs))

    identf = sb("identf", [128, 128], F32)
    identb = sb("identb", [128, 128], BF16)
    trif = sb("trif", [128, 128], F32)
    onesf = sb("onesf", [128, 128], F32)
    cbias = sb("cbias", [128, 128], F32)
    hmask = sb("hmask", [128, 4], F32)
    cmaskb = sb("cmaskb", [128, 4, 128], F32)
    inv8 = sb("inv8", [128, 256], F32)
    pow2 = sb("pow2", [128, 32], F32)
    B_const = Buf("const")
    dsc = S.dsem("const")
    dma("sp", identf[:], c_ident, [], [], dsc, appends=[B_const])
    dma("sp", trif[:], c_tri, [], [], dsc, appends=[B_const])
    dma("sp", cbias[:], c_cbias, [], [], dsc, appends=[B_const])
    dma("sp", hmask[:], c_hmask, [], [], dsc, appends=[B_const])
    dma("sp", inv8[:], c_inv8, [], [], dsc, appends=[B_const])
    dma("sp", pow2[:], c_pow2, [], [], dsc, appends=[B_const])
    B_c2 = Buf("const2")
    S.op("dve", lambda h: h.tensor_copy(out=identb[:], in_=identf[:]), reads=[B_const], appends=[B_c2])
    S.op("dve", lambda h: h.memset(onesf[:], 1.0), appends=[B_c2])
    for hh in range(4):
        S.op("dve", lambda h, hh=hh: h.tensor_copy(out=cmaskb[:, hh, :], in_=trif[:]), reads=[B_const], appends=[B_c2])
    CONSTS = [B_const, B_c2]

    xb_ring = slot("xb", [128, 1024], BF16, 2)
    xts_ring = slot("xts", [128, 8, 128], BF16, 2)
    pT_ring = Ring([(ps("pT%d" % i, [128, 1024], BF16), Buf("pT%d" % i, True)) for i in range(2)])
    pA_ring = Ring([(ps("pA%d" % i, [128, 512], F32), Buf("pA%d" % i, True)) for i in range(4)])
    pS_ring = Ring([(ps("pS%d" % i, [128, 512], F32), Buf("pS%d" % i, True)) for i in range(2)])

    xT_v = xT_d.rearrange("(kc p) t -> p kc t", p=128)

    def emit_xT(blk, x_sb, x_buf):
        xb, xbB, _ = xb_ring.next()
        S.op("act", lambda h: h.activation(out=xb[:], in_=x_sb, func=AF.Copy), reads=[x_buf], writes=[xbB])
        pT, pTB = pT_ring.next()
        for kc in range(8):
            S.op("pe", lambda h, kc=kc: h.transpose(out=pT[:, kc * 128:(kc + 1) * 128], in_=xb[:, kc * 128:(kc + 1) * 128],
                                                    identity=identb[:]),
                 reads=[xbB] + CONSTS, writes=[pTB] if kc == 0 else [], appends=[pTB] if kc else [])
        xts, xtsB, xtsD = xts_ring.next()
        S.op("dve", lambda h: h.tensor_copy(out=xts[:].rearrange("p a b -> p (a b)"), in_=pT[:]), reads=[pTB], writes=[xtsB])
        dma("sp", xT_v[:, :, blk * 128:(blk + 1) * 128], xts[:], [xtsB], [DB("xT", blk)], xtsD)

    TWO_PI = 2.0 * math.pi
    MAGIC = 12582912.0
    posi = sb("posi", [128, NT], I32)
    posf = sb("posf", [128, NT], F32)
    B_pos = Buf("pos")
    dma("sp", posi[:], pos_in, [], [B_pos], S.dsem("pos"))
    S.op("dve", lambda h: h.tensor_copy(out=posf[:], in_=posi[:]), reads=[B_pos], writes=[B_pos])
    pp = ExitStack()

    def sbp(name, shape, dt):
        return pp.enter_context(nc.sbuf_tensor(name, list(shape), dt))

    def slotp(name, shape, dt, n=2):
        return Ring([(sbp("%s_%d" % (name, i), shape, dt), Buf("%s_%d" % (name, i)), S.dsem("%s_%d" % (name, i))) for i in range(n)])

    cs_ring = slotp("cs", [128, 512], F32, 2)
    ang_ring = Ring([(sbp("ang%d" % i, [128, 256], F32), Buf("ang%d" % i)) for i in range(2)])
    rr_ring = Ring([(sbp("rr%d" % i, [128, 256], F32), Buf("rr%d" % i)) for i in range(4)])
    xl_ring = slotp("xl", [128, 1024], F32, 2)
    for blk in range(NT):
        ang, angB = ang_ring.next()
        S.op("dve", lambda h, blk=blk, ang=ang: h.tensor_scalar(out=ang[:], in0=inv8[:], scalar1=posf[:, blk:blk + 1], scalar2=None,
                                                       op0=ALU.mult), reads=[B_pos] + CONSTS, writes=[angB])
        cst, cstB, cstD = cs_ring.next()
        for which, off in ((0, 0.25), (1, 0.0)):
            yo, yoB = rr_ring.next()
            rr, rrB = rr_ring.next()
            S.op("dve", lambda h, off=off, yo=yo: h.tensor_scalar(out=yo[:], in0=ang[:], scalar1=off, scalar2=None, op0=ALU.add), reads=[angB], writes=[yoB])
            S.op("dve", lambda h, yo=yo, rr=rr: h.tensor_scalar(out=rr[:], in0=yo[:], scalar1=MAGIC, scalar2=MAGIC,
                                                                op0=ALU.add, op1=ALU.subtract), reads=[yoB], writes=[rrB])
            S.op("dve", lambda h, yo=yo, rr=rr: h.tensor_tensor(out=rr[:], in0=yo[:], in1=rr[:], op=ALU.subtract), reads=[yoB, rrB], writes=[rrB])
            S.op("act", lambda h, which=which, rr=rr, cst=cst: h.activation(out=cst[:, which * 256:(which + 1) * 256], in_=rr[:], func=AF.Sin,
                                                            scale=TWO_PI), reads=[rrB],
                 writes=[cstB] if which == 0 else [], appends=[cstB] if which else [])
        dma("sp", cs_d[blk * 128:(blk + 1) * 128, :], cst[:], [cstB], [DB("cs", blk)], cstD)
        xl, xlB, xlD = xl_ring.next()
        dma("sp", xl[:], x_in[blk * 128:(blk + 1) * 128, :], [], [xlB], xlD)
        emit_xT(blk, xl[:], xlB)

    S.barrier()
    pp.close()
    convw_sb = sb("convw_sb", [128, DEPTH, 2, CONVW], F32)
    convb_sb = sb("convb_sb", [128, DEPTH, 2], F32)
    clng_sb = sb("clng_sb", [128, DEPTH, 2], F32)
    clnb_sb = sb("clnb_sb", [128, DEPTH, 2], F32)
    pwb_sb = sb("pwb_sb", [128, DEPTH, 2], F32)
    B_par = Buf("par")
    dpar = S.dsem("par")
    for l in range(DEPTH):
        for ct in range(2):
            dma("sp", convw_sb[:, l, ct, :], conv_w[l, ct], [], [], dpar, appends=[B_par])
        dma("sp", convb_sb[:, l, :], conv_b[l], [], [], dpar, appends=[B_par])
        dma("sp", clng_sb[:, l, :], cln_g[l], [], [], dpar, appends=[B_par])
        dma("sp", clnb_sb[:, l, :], cln_b[l], [], [], dpar, appends=[B_par])
        dma("sp", pwb_sb[:, l, :], pw_b[l], [], [], dpar, appends=[B_par])

    STOP = os.environ.get('KSTOP', '')
    SL = int(os.environ.get('KSL', '99'))
    SUB = int(os.environ.get('KSUB', '99'))
    G = int(os.environ.get('KG', '99'))
    for l in range(DEPTH if STOP != 'pro' else 0):
        x_src = x_in if l == 0 else xcur
        x_dst = out_d if l == DEPTH - 1 else xcur
        S.barrier()
        with ExitStack() as pa:
            def sba(name, shape, dt):
                return pa.enter_context(nc.sbuf_tensor("%s_a%d" % (name, l), list(shape), dt))

            def slota(name, shape, dt, n=2):
                items = []
                for i in range(n):
                    t = sba("%s_%d" % (name, i), shape, dt)
                    items.append((t, Buf("%s_%d" % (name, i)), S.dsem("a%s_%d" % (name, i))))
                return Ring(items)

            wbf = sba("wbf", [128, 8, D_IN], BF16)
            B_w = Buf("wbf")
            wst_ring = slota("wst", [128, 8, 256], F32, 2)
            w_v = w_in[l].rearrange("(kc p) n -> p kc n", p=128)
            c0 = 0
            ci = 0
            while c0 < D_IN:
                cw = min(256, D_IN - c0)
                wst, wstB, wstD = wst_ring.next()
                dma("sp", wst[:, :, 0:cw], w_v[:, :, c0:c0 + cw], [], [wstB], wstD)
                eng = "act" if ci % 2 == 0 else "dve"
                if eng == "act":
                    S.op("act", lambda h, c0=c0, cw=cw, wst=wst: h.activation(out=wbf[:, :, c0:c0 + cw], in_=wst[:, :, 0:cw], func=AF.Copy),
                         reads=[wstB], appends=[B_w])
                else:
                    S.op("dve", lambda h, c0=c0, cw=cw, wst=wst: h.tensor_copy(out=wbf[:, :, c0:c0 + cw], in_=wst[:, :, 0:cw]),
                         reads=[wstB], appends=[B_w])
                c0 += cw
                ci += 1
            gw2f = sba("gw2f", [16, 128], F32)
            gw2b = sba("gw2b", [16, 128], BF16)
            gbf = sba("gbf", [1, 128], F32)
            gbb = sba("gbb", [1, 128], BF16)
            onesb = sba("onesb", [1, 128], BF16)
            gng = sba("gng", [128, 256], F32)
            pwf = sba("pwf", [128, 2, 256], F32)
            pwb16 = sba("pwb16", [128, 2, 256], BF16)
            diag = sba("diag", [128, 2, CONVW, 128], BF16)
            onesm = sba("onesm", [128, 128], BF16)
            B_lw = Buf("lw")
            dlw = S.dsem("lw")
            dma("sp", gw2f[:], gate_w2[l], [], [], dlw, appends=[B_lw])
            dma("sp", gbf[:], gate_b[l:l + 1, :], [], [], dlw, appends=[B_lw])
            dma("sp", gng[:], gnorm_g[l], [], [], dlw, appends=[B_lw])
            dma("sp", pwf[:], pw_w[l].rearrange("(ct c) n -> c ct n", c=128), [], [], dlw, appends=[B_lw])
            B_lw2 = Buf("lw2")
            S.op("dve", lambda h: h.tensor_copy(out=gw2b[:], in_=gw2f[:]), reads=[B_lw], appends=[B_lw2])
            S.op("dve", lambda h: h.tensor_copy(out=gbb[:], in_=gbf[:]), reads=[B_lw], appends=[B_lw2])
            S.op("dve", lambda h: h.memset(onesb[:], 1.0), appends=[B_lw2])
            S.op("dve", lambda h: h.memset(onesm[:], 1.0 / 256.0), appends=[B_lw2])
            S.op("dve", lambda h: h.tensor_copy(out=pwb16[:].rearrange("p a b -> p (a b)"), in_=pwf[:].rearrange("p a b -> p (a b)")),
                 reads=[B_lw], appends=[B_lw2])
            for ct in range(2):
                for j in range(CONVW):
                    S.op("pool" if (j % 2) else "dve",
                         lambda h, ct=ct, j=j: h.tensor_scalar(out=diag[:, ct, j, :], in0=identf[:], scalar1=convw_sb[:, l, ct, j:j + 1],
                                                               scalar2=None, op0=ALU.mult),
                         reads=[B_par] + CONSTS, appends=[B_lw2])
            LW = [B_lw, B_lw2, B_par] + CONSTS

            hT = sba("hT", [128, 2, T + 32], BF16)
            B_h0 = Buf("h0")
            S.op("pool", lambda h: h.memset(hT[:, :, 0:32], 0.0), writes=[B_h0])
            hB = [Buf("hT%d" % i) for i in range(NT)]
            Sst = sba("Sst", [128, 64], F32)
            Sbf = sba("Sbf", [128, 64], BF16)
            B_S = Buf("S")
            B_Sb = Buf("Sb")
            S.op("dve", lambda h: h.memset(Sst[:], 0.0), writes=[B_S])
            S.op("dve", lambda h: h.memset(Sbf[:], 0.0), writes=[B_Sb])
            Kbd_ring = Ring([(sba("Kbd%d" % i, [128, 4, 128], BF16), Buf("Kbd%d" % i)) for i in range(2)])
            for (kb_t, kb_B) in Kbd_ring.items:
                S.op("pool", lambda h, kb_t=kb_t: h.memset(kb_t[:].rearrange("p a b -> p (a b)"), 0.0), writes=[kb_B])

            xt_ring = slota("xt", [128, 8, 128], BF16, 2)
            csl_ring = slota("csl", [128, 512], F32, 2)
            ev_ring = Ring([(sba("ev%d" % i, [128, 512], F32), Buf("ev%d" % i)) for i in range(3)])
            t_ring = Ring([(sba("tt%d" % i, [128, 256], F32), Buf("tt%d" % i)) for i in range(4)])
            rb_ring = Ring([(sba("rb%d" % i, [128, 512], BF16), Buf("rb%d" % i)) for i in range(2)])
            qts_ring = slota("qts", [128, 4, 128], BF16, 3)
            kis_ring = slota("kis", [64, 128], BF16, 2)
            wis_ring = slota("wis", [128, 8], F32, 2)
            vs_ring = slota("vs", [128, 8, 65], BF16, 2)
            ags_ring = slota("ags", [128, 512], F32, 2)
            ybc_ring = slota("ybc", [128, 4, 128], BF16, 2)
            fm_ring = Ring([(sba("fm%d" % i, [128, 128], F32), Buf("fm%d" % i)) for i in range(6)])
            sm_ring = Ring([(sba("sm%d" % i, [128, 256], F32), Buf("sm%d" % i)) for i in range(6)])
            smb_ring = Ring([(sba("smb%d" % i, [128, 512], BF16), Buf("smb%d" % i)) for i in range(6)])
            col_ring = Ring([(sba("col%d" % i, [128, 8], F32), Buf("col%d" % i)) for i in range(6)])
            clr_ring = Ring([(sba("clr%d" % i, [16, 128], BF16), Buf("clr%d" % i)) for i in range(2)])

            def proj_tm(xt, xtB, c0, cw):
                pA, pAB = pA_ring.next()
                for kc in range(8):
                    S.op("pe", lambda h, kc=kc, pA=pA: h.matmul(pA[:, 0:cw], lhsT=xt[:, kc, :], rhs=wbf[:, kc, c0:c0 + cw],
                                                                start=(kc == 0), stop=(kc == 7)),
                         reads=[xtB, B_w], writes=[pAB] if kc == 0 else [], appends=[pAB] if kc else [])
                return pA, pAB

            def proj_fm(xt, xtB, c0, cw):
                pA, pAB = pA_ring.next()
                for kc in range(8):
                    S.op("pe", lambda h, kc=kc, pA=pA: h.matmul(pA[0:cw, 0:128], lhsT=wbf[:, kc, c0:c0 + cw], rhs=xt[:, kc, :],
                                                                start=(kc == 0), stop=(kc == 7)),
                         reads=[xtB, B_w], writes=[pAB] if kc == 0 else [], appends=[pAB] if kc else [])
                return pA, pAB

            def rope_tm(src, srcB, csl, cslB, nh, dst, dstB, first=True):
                sv = src.rearrange("p (h d) -> p h d", d=64)
                dv = dst.rearrange("p (h d) -> p h d", d=64)
                C = csl[:, 0:nh * 32].rearrange("p (h d) -> p h d", d=32)
                Sn = csl[:, 256:256 + nh * 32].rearrange("p (h d) -> p h d", d=32)
                t1, t1B = t_ring.next()
                t2, t2B = t_ring.next()
                a1 = t1[:, 0:nh * 32].rearrange("p (h d) -> p h d", d=32)
                a2 = t2[:, 0:nh * 32].rearrange("p (h d) -> p h d", d=32)
                S.op("dve", lambda h: h.tensor_tensor(out=a1, in0=sv[:, :, 0:32], in1=C, op=ALU.mult), reads=[srcB, cslB], writes=[t1B])
                S.op("pool", lambda h: h.tensor_tensor(out=a2, in0=sv[:, :, 32:64], in1=Sn, op=ALU.mult), reads=[srcB, cslB], writes=[t2B])
                S.op("dve", lambda h: h.tensor_tensor(out=dv[:, :, 0:32], in0=a1, in1=a2, op=ALU.subtract), reads=[t1B, t2B],
                     writes=[dstB] if first else [], appends=[] if first else [dstB])
                t3, t3B = t_ring.next()
                t4, t4B = t_ring.next()
                a3 = t3[:, 0:nh * 32].rearrange("p (h d) -> p h d", d=32)
                a4 = t4[:, 0:nh * 32].rearrange("p (h d) -> p h d", d=32)
                S.op("dve", lambda h: h.tensor_tensor(out=a3, in0=sv[:, :, 32:64], in1=C, op=ALU.mult), reads=[srcB, cslB], writes=[t3B])
                S.op("pool", lambda h: h.tensor_tensor(out=a4, in0=sv[:, :, 0:32], in1=Sn, op=ALU.mult), reads=[srcB, cslB], writes=[t4B])
                S.op("dve", lambda h: h.tensor_tensor(out=dv[:, :, 32:64], in0=a3, in1=a4, op=ALU.add), reads=[t3B, t4B], appends=[dstB])

            def evac(pA, pAB, cw, rows=128):
                ev, evB = ev_ring.next()
                S.op("act", lambda h: h.activation(out=ev[0:rows, 0:cw], in_=pA[0:rows, 0:cw], func=AF.Copy), reads=[pAB], writes=[evB])
                return ev, evB

            def tr_to(dst3, dstB, srcb, srcB, ntile, first=True):
                pT, pTB = pT_ring.next()
                for i in range(ntile):
                    S.op("pe", lambda h, i=i: h.transpose(out=pT[:, i * 128:(i + 1) * 128], in_=srcb[:, i * 128:(i + 1) * 128], identity=identb[:]),
                         reads=[srcB] + CONSTS, writes=[pTB] if i == 0 else [], appends=[pTB] if i else [])
                S.op("dve", lambda h: h.tensor_copy(out=dst3, in_=pT[:, 0:ntile * 128]), reads=[pTB],
                     writes=[dstB] if first else [], appends=[] if first else [dstB])

            for blk in range(NT if SL >= 1 else 0):
                t0 = blk * 128
                xt, xtB, xtD = xt_ring.next()
                dma("sp", xt[:], xT_v[:, :, t0:t0 + 128], [DB("xT", blk)], [xtB], xtD)
                csl, cslB, cslD = csl_ring.next()
                dma("sp", csl[:], cs_d[t0:t0 + 128, :], [DB("cs", blk)], [cslB], cslD)

                for (c0, dst_d, nm) in ((0, qT_d, "qT"), (512, kT_d, "kT"), (2048, qiT_d, "qiT")):
                    pA, pAB = proj_tm(xt, xtB, c0, 512)
                    ev, evB = evac(pA, pAB, 512)
                    rb, rbB = rb_ring.next()
                    rope_tm(ev[:, 0:512], evB, csl, cslB, 8, rb[:, 0:512], rbB)
                    qts, qtsB, qtsD = qts_ring.next()
                    tr_to(qts[:].rearrange("p a b -> p (a b)"), qtsB, rb, rbB, 4)
                    dma("sp", dst_d[:, :, t0:t0 + 128].rearrange("a p t -> p a t"), qts[:], [qtsB], [DB(nm, blk)], qtsD)
                if SL < 2:
                    continue
                pA, pAB = proj_tm(xt, xtB, 2560, 72)
                ev, evB = evac(pA, pAB, 72)
                rb, rbB = rb_ring.next()
                rope_tm(ev[:, 0:64], evB, csl, cslB, 1, rb[:, 0:64], rbB)
                pT, pTB = pT_ring.next()
                S.op("pe", lambda h: h.transpose(out=pT[0:64, 0:128], in_=rb[:, 0:64], identity=identb[:]), reads=[rbB] + CONSTS, writes=[pTB])
                kis, kisB, kisD = kis_ring.next()
                S.op("dve", lambda h: h.tensor_copy(out=kis[:], in_=pT[0:64, 0:128]), reads=[pTB], writes=[kisB])
                dma("sp", kiT_d[:, t0:t0 + 128], kis[:], [kisB], [DB("kiT", blk)], kisD)
                wis, wisB, wisD = wis_ring.next()
                S.op("dve", lambda h: h.tensor_scalar(out=wis[:], in0=ev[:, 64:72], scalar1=(8 ** -0.5) * (64 ** -0.5), scalar2=None, op0=ALU.mult),
                     reads=[evB], writes=[wisB])
                dma("sp", wi_d[t0:t0 + 128, :], wis[:], [wisB], [DB("wi", blk)], wisD)
                if SL < 3:
                    continue
                pA, pAB = proj_tm(xt, xtB, 1024, 512)
                vs, vsB, vsD = vs_ring.next()
                S.op("act", lambda h: h.activation(out=vs[:, :, 0:64], in_=pA[:, 0:512].rearrange("p (h d) -> p h d", d=64), func=AF.Copy),
                     reads=[pAB], writes=[vsB])
                S.op("pool", lambda h: h.memset(vs[:, :, 64:65], 1.0), appends=[vsB])
                dma("sp", V_d[t0:t0 + 128, :], vs[:].rearrange("p a b -> p (a b)"), [vsB], [DB("V", blk)], vsD)
                pA, pAB = proj_tm(xt, xtB, 1536, 512)
                ags, agsB, agsD = ags_ring.next()
                S.op("act", lambda h: h.activation(out=ags[:], in_=pA[:, 0:512], func=AF.Silu), reads=[pAB], writes=[agsB])
                dma("sp", ag_d[t0:t0 + 128, :], ags[:], [agsB], [DB("ag", blk)], agsD)

                if SL < 4:
                    continue
                ybc, ybcB, ybcD = ybc_ring.next()
                for ct in range(2):
                    pv_, pvB = proj_fm(xt, xtB, 2632 + ct * 128, 128)
                    pg_, pgB = proj_fm(xt, xtB, 2888 + ct * 128, 128)
                    sg, sgB = fm_ring.next()
                    S.op("act", lambda h, sg=sg, pg_=pg_: h.activation(out=sg[:], in_=pg_[:, 0:128], func=AF.Sigmoid), reads=[pgB], writes=[sgB])
                    S.op("dve", lambda h, sg=sg, pv_=pv_, ct=ct: h.tensor_tensor(out=hT[:, ct, 32 + t0:32 + t0 + 128], in0=pv_[:, 0:128], in1=sg[:], op=ALU.mult),
                         reads=[pvB, sgB], writes=[hB[blk]] if ct == 0 else [], appends=[hB[blk]] if ct else [])
                if SUB < 1:
                    continue
                hdeps = [hB[blk]] + ([hB[blk - 1]] if blk else [B_h0])
                cv = []
                for ct in range(2):
                    pC, pCB = pA_ring.next()
                    for j in range(CONVW):
                        S.op("pe", lambda h, ct=ct, j=j, pC=pC: h.matmul(pC[:, 0:128], lhsT=diag[:, ct, j, :],
                                                                         rhs=hT[:, ct, 32 + t0 - 30 + j:32 + t0 - 30 + j + 128],
                                                                         start=(j == 0), stop=(j == CONVW - 1)),
                             reads=hdeps + LW, writes=[pCB] if j == 0 else [], appends=[pCB] if j else [])
                    cvt, cvB = fm_ring.next()
                    S.op("act", lambda h, cvt=cvt, pC=pC, ct=ct: h.activation(out=cvt[:], in_=pC[:, 0:128], func=AF.Identity,
                                                                             bias=convb_sb[:, l, ct:ct + 1]), reads=[pCB] + LW, writes=[cvB])
                    cv.append((cvt, cvB))
                if SUB < 2:
                    continue
                cb16, cb16B = smb_ring.next()
                sq16, sq16B = smb_ring.next()
                for ct in range(2):
                    cvt, cvB = cv[ct]
                    S.op("dve", lambda h, cvt=cvt, ct=ct: h.tensor_copy(out=cb16[:, ct * 256:ct * 256 + 128], in_=cvt[:]), reads=[cvB],
                         writes=[cb16B] if ct == 0 else [], appends=[cb16B] if ct else [])
                    S.op("dve", lambda h, cvt=cvt, ct=ct: h.tensor_tensor(out=cb16[:, ct * 256 + 128:ct * 256 + 256], in0=cvt[:],
                                                                         in1=cb16[:, ct * 256:ct * 256 + 128], op=ALU.subtract),
                         reads=[cvB, cb16B], appends=[cb16B])
                    sqf, sqfB = fm_ring.next()
                    S.op("pool", lambda h, cvt=cvt, sqf=sqf: h.tensor_tensor(out=sqf[:], in0=cvt[:], in1=cvt[:], op=ALU.mult), reads=[cvB], writes=[sqfB])
                    S.op("dve", lambda h, sqf=sqf, ct=ct: h.tensor_copy(out=sq16[:, ct * 256:ct * 256 + 128], in_=sqf[:]), reads=[sqfB],
                         writes=[sq16B] if ct == 0 else [], appends=[sq16B] if ct else [])
                    S.op("dve", lambda h, sqf=sqf, ct=ct: h.tensor_tensor(out=sq16[:, ct * 256 + 128:ct * 256 + 256], in0=sqf[:],
                                                                         in1=sq16[:, ct * 256:ct * 256 + 128], op=ALU.subtract),
                         reads=[sqfB, sq16B], appends=[sq16B])
                pM, pMB = pS_ring.next()
                n = 0
                for ct in range(2):
                    for part in range(2):
                        S.op("pe", lambda h, ct=ct, part=part, n=n: h.matmul(pM[:, 0:128], lhsT=onesm[:], rhs=cb16[:, ct * 256 + part * 128:ct * 256 + part * 128 + 128],
                                                                             start=(n == 0), stop=(n == 3)),
                             reads=[cb16B, B_lw2], writes=[pMB] if n == 0 else [], appends=[pMB] if n else [])
                        n += 1
                n = 0
                for ct in range(2):
                    for part in range(2):
                        S.op("pe", lambda h, ct=ct, part=part, n=n: h.matmul(pM[:, 128:256], lhsT=onesm[:], rhs=sq16[:, ct * 256 + part * 128:ct * 256 + part * 128 + 128],
                                                                             start=(n == 0), stop=(n == 3)),
                             reads=[sq16B, B_lw2], appends=[pMB])
                        n += 1
                if SUB < 3:
                    continue
                st, stB = sm_ring.next()
                S.op("act", lambda h: h.activation(out=st[:, 0:256], in_=pM[:, 0:256], func=AF.Copy), reads=[pMB], writes=[stB])
                m2, m2B = fm_ring.next()
                S.op("dve", lambda h: h.tensor_tensor(out=m2[:], in0=st[:, 0:128], in1=st[:, 0:128], op=ALU.mult), reads=[stB], writes=[m2B])
                S.op("dve", lambda h: h.tensor_tensor(out=m2[:], in0=st[:, 128:256], in1=m2[:], op=ALU.subtract), reads=[stB, m2B], writes=[m2B])
                S.op("dve", lambda h: h.tensor_scalar(out=m2[:], in0=m2[:], scalar1=EPS, scalar2=None, op0=ALU.add), reads=[m2B], writes=[m2B])
                S.op("act", lambda h: h.activation(out=m2[:], in_=m2[:], func=AF.Sqrt), reads=[m2B], writes=[m2B])
                S.op("dve", lambda h: h.reciprocal(out=st[:, 128:256], in_=m2[:]), reads=[m2B, stB], writes=[stB])
                hs16, hs16B = smb_ring.next()
                for ct in range(2):
                    cvt, cvB = cv[ct]
                    S.op("dve", lambda h, cvt=cvt: h.tensor_tensor(out=cvt[:], in0=cvt[:], in1=st[:, 0:128], op=ALU.subtract), reads=[cvB, stB], writes=[cvB])
                    S.op("dve", lambda h, cvt=cvt: h.tensor_tensor(out=cvt[:], in0=cvt[:], in1=st[:, 128:256], op=ALU.mult), reads=[cvB, stB], writes=[cvB])
                    S.op("act", lambda h, cvt=cvt, ct=ct: h.activation(out=hs16[:, ct * 128:(ct + 1) * 128], in_=cvt[:], func=AF.Silu,
                                                                      scale=clng_sb[:, l, ct:ct + 1], bias=clnb_sb[:, l, ct:ct + 1]),
                         reads=[cvB] + LW, writes=[hs16B] if ct == 0 else [], appends=[hs16B] if ct else [])
                if SUB < 4:
                    continue
                for co in range(2):
                    pP, pPB = pA_ring.next()
                    for ct in range(2):
                        S.op("pe", lambda h, co=co, ct=ct, pP=pP: h.matmul(pP[:, 0:128], lhsT=pwb16[:, ct, co * 128:(co + 1) * 128],
                                                                           rhs=hs16[:, ct * 128:(ct + 1) * 128], start=(ct == 0), stop=(ct == 1)),
                             reads=[hs16B] + LW, writes=[pPB] if ct == 0 else [], appends=[pPB] if ct else [])
                    pG, pGB = proj_fm(xt, xtB, 3144 + co * 128, 128)
                    sgl, sglB = fm_ring.next()
                    S.op("act", lambda h, sgl=sgl, pG=pG: h.activation(out=sgl[:], in_=pG[:, 0:128], func=AF.Silu), reads=[pGB], writes=[sglB])
                    yb, ybB_ = fm_ring.next()
                    S.op("act", lambda h, yb=yb, pP=pP, co=co: h.activation(out=yb[:], in_=pP[:, 0:128], func=AF.Identity, bias=pwb_sb[:, l, co:co + 1]),
                         reads=[pPB] + LW, writes=[ybB_])
                    S.op("dve", lambda h, yb=yb, sgl=sgl, co=co: h.tensor_tensor(out=ybc[:, co, :], in0=yb[:], in1=sgl[:], op=ALU.mult),
                         reads=[ybB_, sglB], writes=[ybcB] if co == 0 else [], appends=[ybcB] if co else [])

                if SL < 5:
                    continue
                pK, pKB = proj_tm(xt, xtB, 3528, 384)
                kv, kvB = evac(pK, pKB, 384)
                vb, vbB = smb_ring.next()
                S.op("dve", lambda h: h.tensor_copy(out=vb[:, 0:256], in_=kv[:, 128:384]), reads=[kvB], writes=[vbB])
                pQ, pQB = proj_tm(xt, xtB, 3400, 128)
                qf, qfB = fm_ring.next()
                S.op("act", lambda h: h.activation(out=qf[:], in_=pQ[:, 0:128], func=AF.Copy, scale=32 ** -0.5), reads=[pQB], writes=[qfB])
                pL, pLB = proj_fm(xt, xtB, 4168, 16)
                clr, clrB = clr_ring.next()
                S.op("act", lambda h: h.activation(out=clr[:], in_=pL[0:16, 0:128], func=AF.Copy), reads=[pLB], writes=[clrB])
                if G < 1:
                    continue
                pZ, pZB = pS_ring.next()
                S.op("pe", lambda h: h.matmul(pZ[:, 0:128], lhsT=clr[:], rhs=gw2b[:], start=True, stop=False), reads=[clrB, B_lw2], writes=[pZB])
                S.op("pe", lambda h: h.matmul(pZ[:, 0:128], lhsT=onesb[:], rhs=gbb[:], start=False, stop=True), reads=[B_lw2], appends=[pZB])
                la, laB = fm_ring.next()
                S.op("act", lambda h: h.activation(out=la[:], in_=pZ[:, 0:128], func=AF.Exp, scale=-1.0), reads=[pZB], writes=[laB])
                S.op("act", lambda h: h.activation(out=la[:], in_=la[:], func=AF.Ln, bias=1.0), reads=[laB], writes=[laB])
                S.op("dve", lambda h: h.tensor_scalar(out=la[:], in0=la[:], scalar1=-1.0 / 16.0, scalar2=None, op0=ALU.mult), reads=[laB], writes=[laB])
                if G < 2:
                    continue
                pB_, pBB = pS_ring.next()
                S.op("pe", lambda h: h.matmul(pB_[:, 0:128], lhsT=trif[:], rhs=la[:], start=True, stop=True), reads=[laB] + CONSTS, writes=[pBB])
                S.op("pe", lambda h: h.matmul(pB_[:, 128:256], lhsT=onesf[:], rhs=la[:], start=True, stop=True), reads=[laB] + CONSTS, appends=[pBB])
                S.op("pe", lambda h: h.matmul(pB_[:, 256:257], lhsT=la[:], rhs=onesf[:, 0:1], start=True, stop=True), reads=[laB] + CONSTS, appends=[pBB])
                bb, bbB = sm_ring.next()
                S.op("act", lambda h: h.activation(out=bb[:, 0:256], in_=pB_[:, 0:256], func=AF.Copy), reads=[pBB], writes=[bbB])
                dec, decB = col_ring.next()
                S.op("act", lambda h: h.activation(out=dec[:, 0:1], in_=pB_[:, 256:257], func=AF.Exp), reads=[pBB], writes=[decB])
                eb, ebB = sm_ring.next()
                S.op("act", lambda h: h.activation(out=eb[:, 0:128], in_=bb[:, 0:128], func=AF.Exp), reads=[bbB], writes=[ebB])
                S.op("act", lambda h: h.activation(out=eb[:, 128:256], in_=bb[:, 0:128], func=AF.Exp, scale=-1.0), reads=[bbB], appends=[ebB])
                el, elB = fm_ring.next()
                S.op("dve", lambda h: h.tensor_tensor(out=el[:], in0=bb[:, 128:256], in1=bb[:, 0:128], op=ALU.subtract), reads=[bbB], writes=[elB])
                S.op("act", lambda h: h.activation(out=el[:], in_=el[:], func=AF.Exp), reads=[elB], writes=[elB])
                if G < 3:
                    continue
                qk16, qk16B = smb_ring.next()
                S.op("dve", lambda h: h.tensor_tensor(out=qk16[:, 0:128], in0=qf[:], in1=eb[:, 0:128], op=ALU.mult), reads=[qfB, ebB], writes=[qk16B])
                S.op("dve", lambda h: h.tensor_tensor(out=qk16[:, 128:256], in0=kv[:, 0:128], in1=eb[:, 128:256], op=ALU.mult), reads=[kvB, ebB], appends=[qk16B])
                Kbd, KbdB = Kbd_ring.next()
                for hh in range(4):
                    S.op("dve", lambda h, hh=hh: h.tensor_tensor(out=Kbd[:, hh, hh * 32:(hh + 1) * 32], in0=kv[:, hh * 32:(hh + 1) * 32],
                                                                 in1=el[:, hh * 32:(hh + 1) * 32], op=ALU.mult),
                         reads=[kvB, elB], writes=[KbdB] if hh == 0 else [], appends=[KbdB] if hh else [])
                pT, pTB = pT_ring.next()
                S.op("pe", lambda h: h.transpose(out=pT[:, 0:128], in_=qk16[:, 0:128], identity=identb[:]), reads=[qk16B] + CONSTS, writes=[pTB])
                S.op("pe", lambda h: h.transpose(out=pT[:, 128:256], in_=qk16[:, 128:256], identity=identb[:]), reads=[qk16B] + CONSTS, appends=[pTB])
                ktT, ktTB = smb_ring.next()
                S.op("act", lambda h: h.activation(out=ktT[:, 0:128], in_=pT[:, 128:256], func=AF.Copy), reads=[pTB], writes=[ktTB])
                Qbd, QbdB = smb_ring.next()
                for hh in range(4):
                    S.op("dve", lambda h, hh=hh: h.tensor_scalar(out=Qbd[:, hh * 128:(hh + 1) * 128], in0=pT[:, 0:128], scalar1=hmask[:, hh:hh + 1],
                                                                 scalar2=None, op0=ALU.mult),
                         reads=[pTB] + CONSTS, writes=[QbdB] if hh == 0 else [], appends=[QbdB] if hh else [])
                if G < 4:
                    continue
                pAt, pAtB = pA_ring.next()
                S.op("pe", lambda h: h.matmul(pAt[:, 0:512], lhsT=ktT[:, 0:128], rhs=Qbd[:, 0:512], start=True, stop=True), reads=[ktTB, QbdB], writes=[pAtB])
                AT, ATB = smb_ring.next()
                S.op("dve", lambda h: h.tensor_tensor(out=AT[:, 0:512], in0=pAt[:, 0:512], in1=cmaskb[:].rearrange("p a b -> p (a b)"), op=ALU.mult),
                     reads=[pAtB] + CONSTS, writes=[ATB])
                pO, pOB = pS_ring.next()
                for hh in range(4):
                    S.op("pe", lambda h, hh=hh: h.matmul(pO[:, hh * 64:(hh + 1) * 64], lhsT=AT[:, hh * 128:(hh + 1) * 128], rhs=vb[:, hh * 64:(hh + 1) * 64],
                                                         start=True, stop=False), reads=[ATB, vbB], writes=[pOB] if hh == 0 else [], appends=[pOB] if hh else [])
                    S.op("pe", lambda h, hh=hh: h.matmul(pO[:, hh * 64:(hh + 1) * 64], lhsT=Qbd[:, hh * 128:(hh + 1) * 128], rhs=Sbf[:],
                                                         start=False, stop=True), reads=[QbdB, B_Sb], appends=[pOB])
                pN, pNB = pS_ring.next()
                for hh in range(4):
                    S.op("pe", lambda h, hh=hh: h.matmul(pN[:, 0:64], lhsT=Kbd[:, hh, :], rhs=vb[:, hh * 64:(hh + 1) * 64], start=(hh == 0), stop=(hh == 3)),
                         reads=[KbdB, vbB], writes=[pNB] if hh == 0 else [], appends=[pNB] if hh else [])
                S.op("dve", lambda h: h.scalar_tensor_tensor(out=Sst[:], in0=Sst[:], scalar=dec[:, 0:1], in1=pN[:, 0:64], op0=ALU.mult, op1=ALU.add),
                     reads=[decB, pNB, B_S], writes=[B_S])
                S.op("dve", lambda h: h.tensor_copy(out=Sbf[:], in_=Sst[:]), reads=[B_S], writes=[B_Sb])
                if G < 5:
                    continue
                of, ofB = sm_ring.next()
                S.op("act", lambda h: h.activation(out=of[:, 0:256], in_=pO[:, 0:256], func=AF.Copy), reads=[pOB], writes=[ofB])
                osq, osqB = sm_ring.next()
                S.op("pool", lambda h: h.tensor_tensor(out=osq[:, 0:256], in0=of[:, 0:256], in1=of[:, 0:256], op=ALU.mult), reads=[ofB], writes=[osqB])
                ss, ssB = col_ring.next()
                S.op("dve", lambda h: h.tensor_reduce(out=ss[:, 0:4], in_=osq[:, 0:256].rearrange("p (h e) -> p h e", e=64), axis=AX.X, op=ALU.add),
                     reads=[osqB], writes=[ssB])
                S.op("dve", lambda h: h.tensor_scalar(out=ss[:, 0:4], in0=ss[:, 0:4], scalar1=1.0 / 64.0, scalar2=EPS, op0=ALU.mult, op1=ALU.add),
                     reads=[ssB], writes=[ssB])
                S.op("act", lambda h: h.activation(out=ss[:, 0:4], in_=ss[:, 0:4], func=AF.Sqrt), reads=[ssB], writes=[ssB])
                S.op("dve", lambda h: h.reciprocal(out=ss[:, 0:4], in_=ss[:, 0:4]), reads=[ssB], writes=[ssB])
                for hh in range(4):
                    S.op("dve", lambda h, hh=hh: h.tensor_scalar(out=of[:, hh * 64:(hh + 1) * 64], in0=of[:, hh * 64:(hh + 1) * 64], scalar1=ss[:, hh:hh + 1],
                                                                 scalar2=None, op0=ALU.mult), reads=[ofB, ssB], writes=[ofB])
                S.op("dve", lambda h: h.tensor_tensor(out=of[:, 0:256], in0=of[:, 0:256], in1=gng[:], op=ALU.mult), reads=[ofB] + LW, writes=[ofB])
                pCg, pCgB = proj_tm(xt, xtB, 3912, 256)
                cgs, cgsB = sm_ring.next()
                S.op("act", lambda h: h.activation(out=cgs[:, 0:256], in_=pCg[:, 0:256], func=AF.Silu), reads=[pCgB], writes=[cgsB])
                yc16, yc16B = smb_ring.next()
                S.op("dve", lambda h: h.tensor_tensor(out=yc16[:, 0:256], in0=of[:, 0:256], in1=cgs[:, 0:256], op=ALU.mult), reads=[ofB, cgsB], writes=[yc16B])
                tr_to(ybc[:, 2:4, :].rearrange("p a b -> p (a b)"), ybcB, yc16, yc16B, 2, first=False)
                dma("sp", ybc_d[:, :, t0:t0 + 128].rearrange("a p t -> p a t"), ybc[:], [ybcB], [DB("ybc", blk)], ybcD)

        S.barrier()
        if SL < 6:
            continue
        with ExitStack() as pd:
            def sbd(name, shape, dt):
                return pd.enter_context(nc.sbuf_tensor("%s_d%d" % (name, l), list(shape), dt))

            def slotd(name, shape, dt, n=2):
                items = []
                for i in range(n):
                    t = sbd("%s_%d" % (name, i), shape, dt)
                    items.append((t, Buf("%s_%d" % (name, i)), S.dsem("d%s_%d" % (name, i))))
                return Ring(items)

            kiT = sbd("kiT_sb", [128, T], BF16)
            kiB = Buf("kiT_sb")
            dma_multi("sp", [(kiT[0:64, :], kiT_d[:, :]), (kiT[64:128, :], kiT_d[:, :])],
                      [DB("kiT", b) for b in range(NT)], [kiB], S.dsem("kiT_sb"))
            wob = sbd("wob", [128, 8, 1024], BF16)
            B_wo = Buf("wob")
            rl_ring = Ring([(sbd("rl%d" % i, [128, 1024], F32), Buf("rl%d" % i), S.dsem("drl%d" % i)) for i in range(3)])
            wo_v = w_out[l].rearrange("(kc p) n -> p kc n", p=128)
            for hf in range(8):
                wos, wosB, wosD = rl_ring.next()
                wv = wos[:].rearrange("p (a b) -> p a b", b=128)
                dma("sp", wv, wo_v[:, :, hf * 128:(hf + 1) * 128], [], [wosB], wosD)
                if hf % 2:
                    S.op("act", lambda h: h.activation(out=wob[:, :, hf * 128:(hf + 1) * 128], in_=wv, func=AF.Copy), reads=[wosB], appends=[B_wo])
                else:
                    S.op("dve", lambda h: h.tensor_copy(out=wob[:, :, hf * 128:(hf + 1) * 128], in_=wv), reads=[wosB], appends=[B_wo])
            rl_ring = Ring([(a, b) for (a, b, c) in rl_ring.items])
            lng = sbd("lng", [128, 1024], F32)
            lnb = sbd("lnb", [128, 1024], F32)
            B_ln = Buf("ln")
            dln = S.dsem("ln")
            dma("sp", lng[:], ln_g[l], [], [], dln, appends=[B_ln])
            dma("sp", lnb[:], ln_b[l], [], [], dln, appends=[B_ln])

            score = sbd("score", [128, T], F32)
            scB = Buf("score")
            m01 = sbd("m01", [128, T], BF16)
            m01B = Buf("m01")
            junk = m01
            jkB = m01B
            mT_ring = Ring([(sbd("mT%d" % i, [128, NT, 128], BF16), Buf("mT%d" % i)) for i in range(2)])
            q_ring = slotd("qd", [128, 4, 128], BF16, 2)
            qi_ring = slotd("qid", [128, 4, 128], BF16, 2)
            wi_ring = slotd("wid", [128, 8], F32, 2)
            kv_ring = slotd("kvd", [128, 4, 512 + 520], BF16, 3)
            pt_ring = Ring([(sbd("pt%d" % i, [128, 512], BF16), Buf("pt%d" % i)) for i in range(4)])
            bis = sbd("bis", [128, 8], F32)
            bisB = Buf("bis")
            bisA = sbd("bisA", [128, 2], F32)
            bisAB = Buf("bisA")
            jkAB = Buf("junkA")
            junkA = sbd("junkA", [128, (T * 3) // 10 + 128], BF16)
            hs = sbd("hs", [128, NBIS + 1], F32)
            hsB = Buf("hs")
            ya = sbd("ya", [128, 8, 64], F32)
            yaB = Buf("ya")
            agl_ring = slotd("agl", [128, 512], F32, 1)
            ya16 = sbd("ya16", [128, 512], BF16)
            ya16B = Buf("ya16")
            mixT = sbd("mixT", [128, 4, 128], BF16)
            mixB = Buf("mixT")
            ybl_ring = slotd("ybl", [128, 4, 128], BF16, 2)
            xl2_ring = slotd("xl2", [128, 1024], F32, 1)
            z_ring = slotd("z", [128, 1024], F32, 2)
            stat = sbd("stat", [128, 2, 6], F32)
            statB = Buf("stat")
            mv = sbd("mv", [128, 4], F32)
            mvB = Buf("mv")
            rcp = sbd("rcp", [128, 8], F32)
            rcpB = Buf("rcp")
            pOa, pOaB = (pS_ring.items[0][0][:, 0:260].rearrange("p (a b) -> p a b", b=65), pS_ring.items[0][1])
            pOb, pObB = (pS_ring.items[1][0][:, 0:260].rearrange("p (a b) -> p a b", b=65), pS_ring.items[1][1])
            ST = {}

            def stage1a(qb):
                t0 = qb * 128
                n = t0 + 128
                qd, qdB, qdD = q_ring.next()
                dma("sp", qd[:], qT_d[:, :, t0:t0 + 128].rearrange("a p t -> p a t"), [DB("qT", qb)], [qdB], qdD)
                qid, qidB, qidD = qi_ring.next()
                dma("sp", qid[:], qiT_d[:, :, t0:t0 + 128].rearrange("a p t -> p a t"), [DB("qiT", qb)], [qidB], qidD)
                wid, widB, widD = wi_ring.next()
                dma("sp", wid[:], wi_d[t0:t0 + 128, :], [DB("wi", qb)], [widB], widD)
                ST[qb] = (qd, qdB)
                nch = (n + 511) // 512
                for hh in range(8):
                    for c2 in range(0, nch, 2):
                        rl, rlB = rl_ring.next()
                        wtot = 0
                        for c in range(c2, min(c2 + 2, nch)):
                            k0 = c * 512
                            kw = min(512, n - k0)
                            pA, pAB = pA_ring.next()
                            pb = (hh % 2) * 64
                            S.op("pe", lambda h: h.matmul(pA[:, 0:kw], lhsT=qid[pb:pb + 64, hh // 2, :], rhs=kiT[pb:pb + 64, k0:k0 + kw],
                                                          start=True, stop=True), reads=[qidB, kiB], writes=[pAB])
                            off = (c - c2) * 512
                            S.op("act", lambda h: h.activation(out=rl[:, off:off + kw], in_=pA[:, 0:kw], func=AF.Relu),
                                 reads=[pAB], writes=[rlB] if c == c2 else [], appends=[rlB] if c != c2 else [])
                            wtot += kw
                        k0 = c2 * 512
                        if hh == 0:
                            S.op("dve", lambda h: h.tensor_scalar(out=score[:, k0:k0 + wtot], in0=rl[:, 0:wtot], scalar1=wid[:, 0:1],
                                                                  scalar2=None, op0=ALU.mult),
                                 reads=[rlB, widB], writes=[scB] if c2 == 0 else [], appends=[scB] if c2 else [])
                        else:
                            S.op("dve", lambda h: h.scalar_tensor_tensor(out=score[:, k0:k0 + wtot], in0=rl[:, 0:wtot],
                                                                         scalar=wid[:, hh:hh + 1], in1=score[:, k0:k0 + wtot],
                                                                         op0=ALU.mult, op1=ALU.add),
                                 reads=[rlB, widB, scB], writes=[scB])
                S.op("dve", lambda h: h.tensor_reduce(out=bis[:, 0:1], in_=score[:, 0:n], axis=AX.X, op=ALU.max, apply_absolute_value=True),
                     reads=[scB], writes=[bisB])
                S.op("dve", lambda h: h.tensor_scalar(out=bis[:, 0:1], in0=bis[:, 0:1], scalar1=1.0, scalar2=None, op0=ALU.add), reads=[bisB], writes=[bisB])
                S.op("dve", lambda h: h.tensor_scalar(out=hs[:], in0=pow2[:, 0:NBIS + 1], scalar1=bis[:, 0:1], scalar2=None, op0=ALU.mult),
                     reads=[bisB] + CONSTS, writes=[hsB])
                S.op("dve", lambda h: h.memset(bis[:, 3:4], 0.0), reads=[bisB], writes=[bisB])
                S.op("dve", lambda h: h.tensor_tensor(out=score[:, n - 128:n], in0=score[:, n - 128:n], in1=cbias[:], op=ALU.add), reads=[scB, bisB] + CONSTS, writes=[scB])
                nA = ((n * 3) // 10 // 128) * 128 if n >= 1024 else 0
                nD = n - nA
                for it in range(NBIS):
                    if nA:
                        S.op("act", lambda h: h.activation(out=junkA[:, 0:nA], in_=score[:, nD:n], func=AF.Sign, scale=-1.0, bias=bis[:, 3:4],
                                                           accum_out=bisA[:, 0:1]), reads=[scB, bisB], writes=[bisAB, jkAB])
                    S.op("dve", lambda h: h.tensor_scalar(out=junk[:, 0:nD], in0=score[:, 0:nD], scalar1=bis[:, 3:4], scalar2=None, op0=ALU.is_ge, op1=ALU.add,
                                                          accum_out=bis[:, 4:5]), reads=[scB, bisB], writes=[bisB, jkB])
                    if nA:
                        S.op("dve", lambda h: h.scalar_tensor_tensor(out=bis[:, 4:5], in0=bisA[:, 0:1], scalar=-0.5, in1=bis[:, 4:5], op0=ALU.mult, op1=ALU.add),
                             reads=[bisB, bisAB], writes=[bisB])
                    S.op("dve", lambda h: h.tensor_scalar(out=bis[:, 5:6], in0=bis[:, 4:5], scalar1=float(topk) - 0.5 - nA / 2.0, scalar2=hs[:, it:it + 1], op0=ALU.is_ge, op1=ALU.mult),
                         reads=[bisB, hsB], writes=[bisB])
                    S.op("dve", lambda h: h.scalar_tensor_tensor(out=bis[:, 3:4], in0=bis[:, 5:6], scalar=hs[:, it + 1:it + 2], in1=bis[:, 3:4], op0=ALU.subtract, op1=ALU.add),
                         reads=[bisB, hsB], writes=[bisB])
                S.op("dve", lambda h: h.tensor_tensor(out=bis[:, 1:2], in0=bis[:, 3:4], in1=hs[:, NBIS:NBIS + 1], op=ALU.subtract), reads=[bisB, hsB], writes=[bisB])
                S.op("dve", lambda h: h.tensor_scalar(out=m01[:, 0:n], in0=score[:, 0:n], scalar1=bis[:, 1:2], scalar2=None, op0=ALU.is_ge),
                     reads=[scB, bisB], writes=[m01B])

            def stage1b(qb):
                nkb = qb + 1
                mT, mTB = mT_ring.next()
                ST[qb] = ST[qb] + (mT, mTB)
                for kb8 in range(0, nkb, 8):
                    cnt = min(8, nkb - kb8)
                    pT, pTB = pT_ring.next()
                    for i in range(cnt):
                        kb = kb8 + i
                        S.op("pe", lambda h: h.transpose(out=pT[:, i * 128:(i + 1) * 128], in_=m01[:, kb * 128:(kb + 1) * 128], identity=identb[:]),
                             reads=[m01B] + CONSTS, writes=[pTB] if i == 0 else [], appends=[pTB] if i else [])
                    S.op("act", lambda h: h.activation(out=mT[:, kb8:kb8 + cnt, :].rearrange("p a b -> p (a b)"), in_=pT[:, 0:cnt * 128], func=AF.Copy),
                         reads=[pTB], writes=[mTB] if kb8 == 0 else [], appends=[mTB] if kb8 else [])

            def stage2(qb):
                t0 = qb * 128
                qd, qdB, mT, mTB = ST.pop(qb)
                nkb = qb + 1
                ngr = (nkb + 3) // 4
                units = [(g, hh) for g in range(ngr) for hh in range(8)]
                ust = {}
                kvs = {}
                LOOK = 2

                def emit_qk(u):
                    g, hh = units[u]
                    kb0 = g * 4
                    nb = min(4, nkb - kb0)
                    if hh == 0:
                        kvt, kvB_, kvD = kv_ring.next()
                        pairs = [(kvt[:, :, 0:nb * 128], kT_d[:, :, kb0 * 128:(kb0 + nb) * 128].rearrange("a p t -> p a t"))]
                        for j in range(nb):
                            pairs.append((kvt[:, j, 512:512 + 520], V_d[(kb0 + j) * 128:(kb0 + j + 1) * 128, :]))
                        dma_multi("sp", pairs, [DB("kT", kb0 + j) for j in range(nb)] + [DB("V", kb0 + j) for j in range(nb)], [kvB_], kvD)
                        kvs[g] = (kvt, kvB_)
                    kvt, kvB_ = kvs[g]
                    pb = (hh % 2) * 64
                    pL_, pLB_ = pA_ring.next()
                    for j in range(nb):
                        S.op("pe", lambda h: h.matmul(pL_[:, j * 128:(j + 1) * 128], lhsT=kvt[pb:pb + 64, hh // 2, j * 128:(j + 1) * 128],
                                                      rhs=qd[pb:pb + 64, hh // 2, :], start=True, stop=True),
                             reads=[kvB_, qdB], writes=[pLB_] if j == 0 else [], appends=[pLB_] if j else [])
                    pt, ptB = pt_ring.next()
                    S.op("act", lambda h: h.activation(out=pt[:, 0:nb * 128], in_=pL_[:, 0:nb * 128], func=AF.Exp, scale=0.125),
                         reads=[pLB_], writes=[ptB])
                    S.op("pool", lambda h: h.tensor_tensor(out=pt[:, 0:nb * 128], in0=pt[:, 0:nb * 128],
                                                           in1=mT[:, kb0:kb0 + nb, :].rearrange("p a b -> p (a b)"), op=ALU.mult),
                         reads=[ptB, mTB], writes=[ptB])
                    ust[u] = (pt, ptB, kvt, kvB_, nb)

                def emit_pv(u):
                    g, hh = units[u]
                    pt, ptB, kvt, kvB_, nb = ust.pop(u)
                    po, poB = (pOa, pOaB) if hh < 4 else (pOb, pObB)
                    for j in range(nb):
                        first = (g == 0 and j == 0)
                        last = (g == ngr - 1 and j == nb - 1)
                        S.op("pe", lambda h: h.matmul(po[:, hh % 4, :], lhsT=pt[:, j * 128:(j + 1) * 128], rhs=kvt[:, j, 512 + hh * 65:512 + (hh + 1) * 65],
                                                      start=(first and hh % 4 == 0), stop=last, skip_group_check=True),
                             reads=[ptB, kvB_], writes=[poB] if (first and hh % 4 == 0) else [], appends=[] if (first and hh % 4 == 0) else [poB])

                for u in range(len(units) + LOOK):
                    if u < len(units):
                        emit_qk(u)
                    if u - LOOK >= 0:
                        emit_pv(u - LOOK)
                for hd in range(8):
                    po, poB = (pOa, pOaB) if hd < 4 else (pOb, pObB)
                    S.op("dve", lambda h: h.reciprocal(out=rcp[:, hd:hd + 1], in_=po[:, hd % 4, 64:65]), reads=[poB], writes=[rcpB] if hd == 0 else [],
                         appends=[rcpB] if hd else [])
                for hd in range(8):
                    po, poB = (pOa, pOaB) if hd < 4 else (pOb, pObB)
                    S.op("dve", lambda h: h.tensor_scalar(out=ya[:, hd, :], in0=po[:, hd % 4, 0:64], scalar1=rcp[:, hd:hd + 1], scalar2=None, op0=ALU.mult),
                         reads=[poB, rcpB], writes=[yaB] if hd == 0 else [], appends=[yaB] if hd else [])
                agl, aglB, aglD = agl_ring.next()
                dma("sp", agl[:], ag_d[t0:t0 + 128, :], [DB("ag", qb)], [aglB], aglD)
                S.op("dve", lambda h: h.tensor_tensor(out=ya16[:], in0=ya[:].rearrange("p a b -> p (a b)"), in1=agl[:], op=ALU.mult), reads=[yaB, aglB], writes=[ya16B])
                pT, pTB = pT_ring.next()
                for i in range(4):
                    S.op("pe", lambda h: h.transpose(out=pT[:, i * 128:(i + 1) * 128], in_=ya16[:, i * 128:(i + 1) * 128], identity=identb[:]),
                         reads=[ya16B] + CONSTS, writes=[pTB] if i == 0 else [], appends=[pTB] if i else [])
                S.op("dve", lambda h: h.tensor_copy(out=mixT[:].rearrange("p a b -> p (a b)"), in_=pT[:, 0:512]), reads=[pTB], writes=[mixB])
                ybl, yblB, yblD = ybl_ring.next()
                dma("sp", ybl[:], ybc_d[:, :, t0:t0 + 128].rearrange("a p t -> p a t"), [DB("ybc", qb)], [yblB], yblD)
                xl2, xl2B, xl2D = xl2_ring.next()
                dma("sp", xl2[:], x_src[t0:t0 + 128, :], [DB("x", qb)], [xl2B], xl2D)
                z, zB, zD = z_ring.next()
                for hf in range(2):
                    pY, pYB = pA_ring.next()
                    for kc in range(8):
                        S.op("pe", lambda h: h.matmul(pY[:, 0:512], lhsT=(mixT[:, kc, :] if kc < 4 else ybl[:, kc - 4, :]),
                                                      rhs=wob[:, kc, hf * 512:(hf + 1) * 512], start=(kc == 0), stop=(kc == 7)),
                             reads=[mixB, yblB, B_wo], writes=[pYB] if kc == 0 else [], appends=[pYB] if kc else [])
                    S.op("dve", lambda h: h.scalar_tensor_tensor(out=z[:, hf * 512:(hf + 1) * 512], in0=xl2[:, hf * 512:(hf + 1) * 512], scalar=alpha,
                                                                 in1=pY[:, 0:512], op0=ALU.mult, op1=ALU.add),
                         reads=[pYB, xl2B], writes=[zB] if hf == 0 else [], appends=[zB] if hf else [])
                    S.op("dve", lambda h: h.bn_stats(out=stat[:, hf, :], in_=z[:, hf * 512:(hf + 1) * 512]), reads=[zB], writes=[statB] if hf == 0 else [],
                         appends=[statB] if hf else [])
                S.op("dve", lambda h: h.bn_aggr(out=mv[:, 0:2], in_=stat[:].rearrange("p a b -> p (a b)")), reads=[statB], writes=[mvB])
                S.op("dve", lambda h: h.tensor_scalar(out=mv[:, 2:3], in0=mv[:, 1:2], scalar1=EPS, scalar2=None, op0=ALU.add), reads=[mvB], writes=[mvB])
                S.op("act", lambda h: h.activation(out=mv[:, 2:3], in_=mv[:, 2:3], func=AF.Sqrt), reads=[mvB], writes=[mvB])
                S.op("dve", lambda h: h.reciprocal(out=mv[:, 3:4], in_=mv[:, 2:3]), reads=[mvB], writes=[mvB])
                S.op("dve", lambda h: h.tensor_scalar(out=z[:], in0=z[:], scalar1=mv[:, 0:1], scalar2=mv[:, 3:4], op0=ALU.subtract, op1=ALU.mult), reads=[zB, mvB], writes=[zB])
                S.op("pool", lambda h: h.tensor_tensor(out=z[:], in0=z[:], in1=lng[:], op=ALU.mult), reads=[zB, B_ln], writes=[zB])
                S.op("dve", lambda h: h.tensor_tensor(out=z[:], in0=z[:], in1=lnb[:], op=ALU.add), reads=[zB, B_ln], writes=[zB])
                dma("sp", x_dst[t0:t0 + 128, :], z[:], [zB], [DB("x", qb)], zD)
                if l < DEPTH - 1:
                    emit_xT(qb, z[:], zB)

            stage1a(0)
            stage1b(0)
            for qb in range(NT):
                if qb + 1 < NT:
                    stage1a(qb + 1)
                stage2(qb)
                if qb + 1 < NT:
                    stage1b(qb + 1)

    S.barrier()
    print('KERNEL nops', S.nops, 'sems', len(S.dsems) + 5, flush=True)
    S.emit()
    es.close()
    return nc


def make_consts():
    ident = np.eye(128, dtype=np.float32)
    j = np.arange(128)
    tri = (j[:, None] <= j[None, :]).astype(np.float32)
    cb = np.where(j[None, :] <= j[:, None], 0.0, -1e30).astype(np.float32)
    hm = np.zeros((128, 4), np.float32)
    for h in range(4):
        hm[h * 32:(h + 1) * 32, h] = 1.0
    inv = (10000.0 ** (-np.arange(0, 64, 2, dtype=np.float32) / 64.0)).astype(np.float32)
    inv8 = np.tile((inv / np.float32(2.0 * math.pi)).astype(np.float32), 8)[None, :].repeat(128, 0).astype(np.float32)
    p2 = np.broadcast_to((2.0 ** (-np.arange(32, dtype=np.float32)))[None, :], (128, 32)).astype(np.float32)
    return {"c_pow2": np.ascontiguousarray(p2), "c_ident": ident, "c_tri": tri, "c_cbias": cb, "c_hmask": hm, "c_inv8": np.ascontiguousarray(inv8)}


_CACHE = {}
_NCORES = [8]
_DBG = [False]
_LAST = [None]


def kernel(x, positions, w_in, conv_w, conv_b, cln_g, cln_b, pw_w, pw_b, gate_w2, gate_b, gnorm_g, w_out, ln_g, ln_b):
    x = np.asarray(x)
    B, T, _ = x.shape
    DEPTH = int(np.asarray(w_in).shape[0])
    key = (T, DEPTH)
    if key not in _CACHE:
        _CACHE[key] = build_program(T, DEPTH, dbg=_DBG[0])
    nc = _CACHE[key]
    consts = make_consts()
    shared = {"w_in": w_in, "conv_w": conv_w, "conv_b": conv_b, "cln_g": cln_g, "cln_b": cln_b, "pw_w": pw_w, "pw_b": pw_b,
              "gate_w2": gate_w2, "gate_b": gate_b, "gnorm_g": gnorm_g, "w_out": w_out, "ln_g": ln_g, "ln_b": ln_b}
    shared = {k: np.asarray(v, dtype=np.float32) for k, v in shared.items()}
    shared["conv_w"] = shared["conv_w"].reshape(DEPTH, CONVW, 2, 128).transpose(0, 2, 3, 1)
    for nm in ("conv_b", "cln_g", "cln_b", "pw_b"):
        shared[nm] = shared[nm].reshape(DEPTH, 2, 128).transpose(0, 2, 1)
    for nm in ("gnorm_g", "ln_g", "ln_b"):
        shared[nm] = np.broadcast_to(shared[nm][:, None, :], (DEPTH, 128, shared[nm].shape[-1]))
    shared = {k: np.ascontiguousarray(v) for k, v in shared.items()}
    n_cores = _NCORES[0]
    in_maps = []
    for c in range(n_cores):
        b = c % B
        m = {"x": np.ascontiguousarray(x[b]), "positions": np.ascontiguousarray(np.asarray(positions)[b].astype(np.int32).reshape(T // 128, 128).T)}
        m.update(shared)
        m.update(consts)
        in_maps.append(m)
    res = run_bass_kernel_spmd(nc, in_maps, core_ids=list(range(n_cores)))
    _LAST[0] = res
    return np.stack([np.asarray(res.results[b]["out"]) for b in range(min(B, n_cores))], axis=0).astype(np.float32)
```

```python
import math
import os
from contextlib import ExitStack
import numpy as np
import concourse.bass as bass
import concourse.mybir as mybir
from concourse.bass_utils import run_bass_kernel_spmd

F32 = mybir.dt.float32
BF16 = mybir.dt.bfloat16
I32 = mybir.dt.int32
ALU = mybir.AluOpType
AF = mybir.ActivationFunctionType
AX = mybir.AxisListType

D_MODEL = 1024
D_IN = 4184
TOPK_MAX = 256
CONVW = 31
EPS = 1e-5
NEG = -30000.0
NBIS = 16


class Buf:
    __slots__ = ("writers", "readers", "old", "name", "excl")

    def __init__(self, name="", excl=False):
        self.excl = excl
        self.writers = []
        self.readers = []
        self.old = []
        self.name = name


class DSem:
    def __init__(self, sem):
        self.sem = sem
        self.count = 0


class Eng:
    def __init__(self, name, sem):
        self.name = name
        self.sem = sem
        self.count = 0
        self.known = {}
        self.prog = []


class _Rec:
    def __init__(self):
        self.calls = []

    def __getattr__(self, name):
        def f(*a, **k):
            self.calls.append((name, a, k))
            return self
        return f


class Sched:
    def __init__(self, nc, es):
        self.nc = nc
        self.es = es
        self.engs = {}
        for n in ("pe", "act", "dve", "pool", "sp"):
            self.engs[n] = Eng(n, es.enter_context(nc.semaphore("s_" + n)))
        self.dsems = {}
        self.nops = 0

    def dsem(self, key):
        if key not in self.dsems:
            self.dsems[key] = DSem(self.es.enter_context(self.nc.semaphore("d_%d" % len(self.dsems))))
        return self.dsems[key]

    def _compress(self, lst):
        if len(lst) > 6:
            best = {}
            for (s, v) in lst:
                k = id(s)
                if k not in best or best[k][1] < v:
                    best[k] = (s, v)
            lst[:] = list(best.values())

    def op(self, eng, fn, reads=(), writes=(), appends=(), dsem=None, ndma=1):
        e = self.engs[eng]
        deps = {}

        def need(t):
            k = id(t[0])
            if k not in deps or deps[k][1] < t[1]:
                deps[k] = t

        for b in reads:
            for t in b.writers:
                need(t)
            if b.excl:
                for t in b.readers:
                    if t[0] is not e.sem:
                        need(t)
        for b in writes:
            for t in b.writers:
                need(t)
            for t in b.readers:
                need(t)
        for b in appends:
            for t in b.readers:
                need(t)
            for t in b.old:
                need(t)
        waits = []
        for k, (s, v) in deps.items():
            if eng == "pe" and s is e.sem:
                continue
            if e.known.get(k, 0) < v:
                waits.append((s, v))
                e.known[k] = v
        if dsem is None:
            e.count += 1
            t = (e.sem, e.count)
            inc = (e.sem, 1)
        else:
            dsem.count += 16 * ndma
            t = (dsem.sem, dsem.count)
            inc = (dsem.sem, 16)
        rec = _Rec()
        fn(rec)
        e.prog.append((waits, rec.calls, inc))
        self.nops += 1
        for b in reads:
            b.readers.append(t)
            self._compress(b.readers)
        for b in writes:
            b.old = b.writers + b.readers
            self._compress(b.old)
            b.writers = [t]
            b.readers = []
        for b in appends:
            b.writers.append(t)
            self._compress(b.writers)
        return t

    def barrier(self):
        for e in self.engs.values():
            waits = []
            for f in self.engs.values():
                if f is e or f.count == 0:
                    continue
                if e.known.get(id(f.sem), 0) < f.count:
                    waits.append((f.sem, f.count))
                    e.known[id(f.sem)] = f.count
            for d in self.dsems.values():
                if d.count and e.known.get(id(d.sem), 0) < d.count:
                    waits.append((d.sem, d.count))
                    e.known[id(d.sem)] = d.count
            if waits:
                e.prog.append((waits, None, None))

    def emit(self):
        nc = self.nc
        hmap = {"pe": "tensor", "act": "scalar", "dve": "vector", "pool": "gpsimd", "sp": "sync"}
        with nc.Block() as block:
            for n, e in self.engs.items():
                def body(h, e=e):
                    for waits, fn, inc in e.prog:
                        for (s, v) in waits:
                            h.wait_ge(s, v)
                        if fn is None:
                            continue
                        for (name, a, k) in fn:
                            getattr(h, name)(*a, **k).then_inc(inc[0], inc[1])
                getattr(block, hmap[n])(body)


class Ring:
    def __init__(self, items):
        self.items = items
        self.i = 0

    def next(self):
        it = self.items[self.i % len(self.items)]
        self.i += 1
        return it


def build_program(T, DEPTH, dbg=False):
    NT = T // 128
    topk = min(TOPK_MAX, T // 4)
    alpha = (2 * DEPTH) ** 0.25
    nc = bass.Bass("TRN2", target_bir_lowering=False)
    es = ExitStack()
    S = Sched(nc, es)

    def din(name, shape, dt=F32):
        return nc.dram_tensor(name, list(shape), dt, kind="ExternalInput").ap()

    def dscr(name, shape, dt):
        return nc.dram_tensor(name, list(shape), dt, kind="ExternalOutput" if dbg else "Internal").ap()

    x_in = din("x", [T, D_MODEL])
    pos_in = din("positions", [128, T // 128], I32)
    w_in = din("w_in", [DEPTH, D_MODEL, D_IN])
    conv_w = din("conv_w", [DEPTH, 2, 128, CONVW])
    conv_b = din("conv_b", [DEPTH, 128, 2])
    cln_g = din("cln_g", [DEPTH, 128, 2])
    cln_b = din("cln_b", [DEPTH, 128, 2])
    pw_w = din("pw_w", [DEPTH, 256, 256])
    pw_b = din("pw_b", [DEPTH, 128, 2])
    gate_w2 = din("gate_w2", [DEPTH, 16, 128])
    gate_b = din("gate_b", [DEPTH, 128])
    gnorm_g = din("gnorm_g", [DEPTH, 128, 256])
    w_out = din("w_out", [DEPTH, 1024, 1024])
    ln_g = din("ln_g", [DEPTH, 128, 1024])
    ln_b = din("ln_b", [DEPTH, 128, 1024])
    c_ident = din("c_ident", [128, 128])
    c_tri = din("c_tri", [128, 128])
    c_cbias = din("c_cbias", [128, 128])
    c_hmask = din("c_hmask", [128, 4])
    c_inv8 = din("c_inv8", [128, 256])
    c_pow2 = din("c_pow2", [128, 32])
    out_d = nc.dram_tensor("out", [T, D_MODEL], F32, kind="ExternalOutput").ap()

    xcur = dscr("xcur", [T, D_MODEL], F32)
    xT_d = dscr("xT", [D_MODEL, T], BF16)
    cs_d = dscr("cs", [T, 512], F32)
    qT_d = dscr("qT", [4, 128, T], BF16)
    kT_d = dscr("kT", [4, 128, T], BF16)
    qiT_d = dscr("qiT", [4, 128, T], BF16)
    kiT_d = dscr("kiT", [64, T], BF16)
    wi_d = dscr("wi", [T, 8], F32)
    V_d = dscr("V", [T, 520], BF16)
    ag_d = dscr("ag", [T, 512], F32)
    ybc_d = dscr("ybcT", [4, 128, T], BF16)

    dbufs = {}

    def DB(name, i):
        k = (name, i)
        if k not in dbufs:
            dbufs[k] = Buf("%s%d" % (name, i))
        return dbufs[k]

    def sb(name, shape, dt):
        return es.enter_context(nc.sbuf_tensor(name, list(shape), dt))

    def ps(name, shape, dt):
        return es.enter_context(nc.psum_tensor(name, list(shape), dt))

    def slot(name, shape, dt, n=2):
        items = []
        for i in range(n):
            t = sb("%s_%d" % (name, i), shape, dt)
            items.append((t, Buf("%s_%d" % (name, i)), S.dsem("%s_%d" % (name, i))))
        return Ring(items)

    def dma(eng, out_ap, in_ap, reads, writes, dsem, appends=()):
        return S.op(eng, lambda h: h.dma_start(out=out_ap, in_=in_ap), reads=reads, writes=writes,
                    appends=appends, dsem=dsem)

    def dma_multi(eng, pairs, reads, writes, dsem):
        return S.op(eng, lambda h: [h.dma_start(out=o, in_=i) for (o, i) in pairs], reads=reads, writes=writes,
                    dsem=dsem, ndma=len(pairs))

    identf = sb("identf", [128, 128], F32)
    identb = sb("identb", [128, 128], BF16)
    trif = sb("trif", [128, 128], F32)
    onesf = sb("onesf", [128, 128], F32)
    cbias = sb("cbias", [128, 128], F32)
    hmask = sb("hmask", [128, 4], F32)
    cmaskb = sb("cmaskb", [128, 4, 128], F32)
    inv8 = sb("inv8", [128, 256], F32)
    pow2 = sb("pow2", [128, 32], F32)
    B_const = Buf("const")
    dsc = S.dsem("const")
    dma("sp", identf[:], c_ident, [], [], dsc, appends=[B_const])
    dma("sp", trif[:], c_tri, [], [], dsc, appends=[B_const])
    dma("sp", cbias[:], c_cbias, [], [], dsc, appends=[B_const])
    dma("sp", hmask[:], c_hmask, [], [], dsc, appends=[B_const])
    dma("sp", inv8[:], c_inv8, [], [], dsc, appends=[B_const])
    dma("sp", pow2[:], c_pow2, [], [], dsc, appends=[B_const])
    B_c2 = Buf("const2")
    S.op("dve", lambda h: h.tensor_copy(out=identb[:], in_=identf[:]), reads=[B_const], appends=[B_c2])
    S.op("dve", lambda h: h.memset(onesf[:], 1.0), appends=[B_c2])
    for hh in range(4):
        S.op("dve", lambda h, hh=hh: h.tensor_copy(out=cmaskb[:, hh, :], in_=trif[:]), reads=[B_const], appends=[B_c2])
    CONSTS = [B_const, B_c2]

    xb_ring = slot("xb", [128, 1024], BF16, 2)
    xts_ring = slot("xts", [128, 8, 128], BF16, 2)
    pT_ring = Ring([(ps("pT%d" % i, [128, 1024], BF16), Buf("pT%d" % i, True)) for i in range(2)])
    pA_ring = Ring([(ps("pA%d" % i, [128, 512], F32), Buf("pA%d" % i, True)) for i in range(4)])
    pS_ring = Ring([(ps("pS%d" % i, [128, 512], F32), Buf("pS%d" % i, True)) for i in range(2)])

    xT_v = xT_d.rearrange("(kc p) t -> p kc t", p=128)

    def emit_xT(blk, x_sb, x_buf):
        xb, xbB, _ = xb_ring.next()
        S.op("act", lambda h: h.activation(out=xb[:], in_=x_sb, func=AF.Copy), reads=[x_buf], writes=[xbB])
        pT, pTB = pT_ring.next()
        for kc in range(8):
            S.op("pe", lambda h, kc=kc: h.transpose(out=pT[:, kc * 128:(kc + 1) * 128], in_=xb[:, kc * 128:(kc + 1) * 128],
                                                    identity=identb[:]),
                 reads=[xbB] + CONSTS, writes=[pTB] if kc == 0 else [], appends=[pTB] if kc else [])
        xts, xtsB, xtsD = xts_ring.next()
        S.op("dve", lambda h: h.tensor_copy(out=xts[:].rearrange("p a b -> p (a b)"), in_=pT[:]), reads=[pTB], writes=[xtsB])
        dma("sp", xT_v[:, :, blk * 128:(blk + 1) * 128], xts[:], [xtsB], [DB("xT", blk)], xtsD)

    TWO_PI = 2.0 * math.pi
    MAGIC = 12582912.0
    posi = sb("posi", [128, NT], I32)
    posf = sb("posf", [128, NT], F32)
    B_pos = Buf("pos")
    dma("sp", posi[:], pos_in, [], [B_pos], S.dsem("pos"))
    S.op("dve", lambda h: h.tensor_copy(out=posf[:], in_=posi[:]), reads=[B_pos], writes=[B_pos])
    pp = ExitStack()

    def sbp(name, shape, dt):
        return pp.enter_context(nc.sbuf_tensor(name, list(shape), dt))

    def slotp(name, shape, dt, n=2):
        return Ring([(sbp("%s_%d" % (name, i), shape, dt), Buf("%s_%d" % (name, i)), S.dsem("%s_%d" % (name, i))) for i in range(n)])

    cs_ring = slotp("cs", [128, 512], F32, 2)
    ang_ring = Ring([(sbp("ang%d" % i, [128, 256], F32), Buf("ang%d" % i)) for i in range(2)])
    rr_ring = Ring([(sbp("rr%d" % i, [128, 256], F32), Buf("rr%d" % i)) for i in range(4)])
    xl_ring = slotp("xl", [128, 1024], F32, 2)
    for blk in range(NT):
        ang, angB = ang_ring.next()
        S.op("dve", lambda h, blk=blk, ang=ang: h.tensor_scalar(out=ang[:], in0=inv8[:], scalar1=posf[:, blk:blk + 1], scalar2=None,
                                                       op0=ALU.mult), reads=[B_pos] + CONSTS, writes=[angB])
        cst, cstB, cstD = cs_ring.next()
        for which, off in ((0, 0.25), (1, 0.0)):
            yo, yoB = rr_ring.next()
            rr, rrB = rr_ring.next()
            S.op("dve", lambda h, off=off, yo=yo: h.tensor_scalar(out=yo[:], in0=ang[:], scalar1=off, scalar2=None, op0=ALU.add), reads=[angB], writes=[yoB])
            S.op("dve", lambda h, yo=yo, rr=rr: h.tensor_scalar(out=rr[:], in0=yo[:], scalar1=MAGIC, scalar2=MAGIC,
                                                                op0=ALU.add, op1=ALU.subtract), reads=[yoB], writes=[rrB])
            S.op("dve", lambda h, yo=yo, rr=rr: h.tensor_tensor(out=rr[:], in0=yo[:], in1=rr[:], op=ALU.subtract), reads=[yoB, rrB], writes=[rrB])
            S.op("act", lambda h, which=which, rr=rr, cst=cst: h.activation(out=cst[:, which * 256:(which + 1) * 256], in_=rr[:], func=AF.Sin,
                                                            scale=TWO_PI), reads=[rrB],
                 writes=[cstB] if which == 0 else [], appends=[cstB] if which else [])
        dma("sp", cs_d[blk * 128:(blk + 1) * 128, :], cst[:], [cstB], [DB("cs", blk)], cstD)
        xl, xlB, xlD = xl_ring.next()
        dma("sp", xl[:], x_in[blk * 128:(blk + 1) * 128, :], [], [xlB], xlD)
        emit_xT(blk, xl[:], xlB)

    S.barrier()
    pp.close()
    convw_sb = sb("convw_sb", [128, DEPTH, 2, CONVW], F32)
    convb_sb = sb("convb_sb", [128, DEPTH, 2], F32)
    clng_sb = sb("clng_sb", [128, DEPTH, 2], F32)
    clnb_sb = sb("clnb_sb", [128, DEPTH, 2], F32)
    pwb_sb = sb("pwb_sb", [128, DEPTH, 2], F32)
    B_par = Buf("par")
    dpar = S.dsem("par")
    for l in range(DEPTH):
        for ct in range(2):
            dma("sp", convw_sb[:, l, ct, :], conv_w[l, ct], [], [], dpar, appends=[B_par])
        dma("sp", convb_sb[:, l, :], conv_b[l], [], [], dpar, appends=[B_par])
        dma("sp", clng_sb[:, l, :], cln_g[l], [], [], dpar, appends=[B_par])
        dma("sp", clnb_sb[:, l, :], cln_b[l], [], [], dpar, appends=[B_par])
        dma("sp", pwb_sb[:, l, :], pw_b[l], [], [], dpar, appends=[B_par])

    STOP = os.environ.get('KSTOP', '')
    SL = int(os.environ.get('KSL', '99'))
    SUB = int(os.environ.get('KSUB', '99'))
    G = int(os.environ.get('KG', '99'))
    for l in range(DEPTH if STOP != 'pro' else 0):
        x_src = x_in if l == 0 else xcur
        x_dst = out_d if l == DEPTH - 1 else xcur
        S.barrier()
        with ExitStack() as pa:
            def sba(name, shape, dt):
                return pa.enter_context(nc.sbuf_tensor("%s_a%d" % (name, l), list(shape), dt))

            def slota(name, shape, dt, n=2):
                items = []
                for i in range(n):
                    t = sba("%s_%d" % (name, i), shape, dt)
                    items.append((t, Buf("%s_%d" % (name, i)), S.dsem("a%s_%d" % (name, i))))
                return Ring(items)

            wbf = sba("wbf", [128, 8, D_IN], BF16)
            B_w = Buf("wbf")
            wst_ring = slota("wst", [128, 8, 256], F32, 2)
            w_v = w_in[l].rearrange("(kc p) n -> p kc n", p=128)
            c0 = 0
            ci = 0
            while c0 < D_IN:
                cw = min(256, D_IN - c0)
                wst, wstB, wstD = wst_ring.next()
                dma("sp", wst[:, :, 0:cw], w_v[:, :, c0:c0 + cw], [], [wstB], wstD)
                eng = "act" if ci % 2 == 0 else "dve"
                if eng == "act":
                    S.op("act", lambda h, c0=c0, cw=cw, wst=wst: h.activation(out=wbf[:, :, c0:c0 + cw], in_=wst[:, :, 0:cw], func=AF.Copy),
                         reads=[wstB], appends=[B_w])
                else:
                    S.op("dve", lambda h, c0=c0, cw=cw, wst=wst: h.tensor_copy(out=wbf[:, :, c0:c0 + cw], in_=wst[:, :, 0:cw]),
                         reads=[wstB], appends=[B_w])
                c0 += cw
                ci += 1
            gw2f = sba("gw2f", [16, 128], F32)
            gw2b = sba("gw2b", [16, 128], BF16)
            gbf = sba("gbf", [1, 128], F32)
            gbb = sba("gbb", [1, 128], BF16)
            onesb = sba("onesb", [1, 128], BF16)
            gng = sba("gng", [128, 256], F32)
            pwf = sba("pwf", [128, 2, 256], F32)
            pwb16 = sba("pwb16", [128, 2, 256], BF16)
            diag = sba("diag", [128, 2, CONVW, 128], BF16)
            onesm = sba("onesm", [128, 128], BF16)
            B_lw = Buf("lw")
            dlw = S.dsem("lw")
            dma("sp", gw2f[:], gate_w2[l], [], [], dlw, appends=[B_lw])
            dma("sp", gbf[:], gate_b[l:l + 1, :], [], [], dlw, appends=[B_lw])
            dma("sp", gng[:], gnorm_g[l], [], [], dlw, appends=[B_lw])
            dma("sp", pwf[:], pw_w[l].rearrange("(ct c) n -> c ct n", c=128), [], [], dlw, appends=[B_lw])
            B_lw2 = Buf("lw2")
            S.op("dve", lambda h: h.tensor_copy(out=gw2b[:], in_=gw2f[:]), reads=[B_lw], appends=[B_lw2])
            S.op("dve", lambda h: h.tensor_copy(out=gbb[:], in_=gbf[:]), reads=[B_lw], appends=[B_lw2])
            S.op("dve", lambda h: h.memset(onesb[:], 1.0), appends=[B_lw2])
            S.op("dve", lambda h: h.memset(onesm[:], 1.0 / 256.0), appends=[B_lw2])
            S.op("dve", lambda h: h.tensor_copy(out=pwb16[:].rearrange("p a b -> p (a b)"), in_=pwf[:].rearrange("p a b -> p (a b)")),
                 reads=[B_lw], appends=[B_lw2])
            for ct in range(2):
                for j in range(CONVW):
                    S.op("pool" if (j % 2) else "dve",
                         lambda h, ct=ct, j=j: h.tensor_scalar(out=diag[:, ct, j, :], in0=identf[:], scalar1=convw_sb[:, l, ct, j:j + 1],
                                                               scalar2=None, op0=ALU.mult),
                         reads=[B_par] + CONSTS, appends=[B_lw2])
            LW = [B_lw, B_lw2, B_par] + CONSTS

            hT = sba("hT", [128, 2, T + 32], BF16)
            B_h0 = Buf("h0")
            S.op("pool", lambda h: h.memset(hT[:, :, 0:32], 0.0), writes=[B_h0])
            hB = [Buf("hT%d" % i) for i in range(NT)]
            Sst = sba("Sst", [128, 64], F32)
            Sbf = sba("Sbf", [128, 64], BF16)
            B_S = Buf("S")
            B_Sb = Buf("Sb")
            S.op("dve", lambda h: h.memset(Sst[:], 0.0), writes=[B_S])
            S.op("dve", lambda h: h.memset(Sbf[:], 0.0), writes=[B_Sb])
            Kbd_ring = Ring([(sba("Kbd%d" % i, [128, 4, 128], BF16), Buf("Kbd%d" % i)) for i in range(2)])
            for (kb_t, kb_B) in Kbd_ring.items:
                S.op("pool", lambda h, kb_t=kb_t: h.memset(kb_t[:].rearrange("p a b -> p (a b)"), 0.0), writes=[kb_B])

            xt_ring = slota("xt", [128, 8, 128], BF16, 2)
            csl_ring = slota("csl", [128, 512], F32, 2)
            ev_ring = Ring([(sba("ev%d" % i, [128, 512], F32), Buf("ev%d" % i)) for i in range(3)])
            t_ring = Ring([(sba("tt%d" % i, [128, 256], F32), Buf("tt%d" % i)) for i in range(4)])
            rb_ring = Ring([(sba("rb%d" % i, [128, 512], BF16), Buf("rb%d" % i)) for i in range(2)])
            qts_ring = slota("qts", [128, 4, 128], BF16, 3)
            kis_ring = slota("kis", [64, 128], BF16, 2)
            wis_ring = slota("wis", [128, 8], F32, 2)
            vs_ring = slota("vs", [128, 8, 65], BF16, 2)
            ags_ring = slota("ags", [128, 512], F32, 2)
            ybc_ring = slota("ybc", [128, 4, 128], BF16, 2)
            fm_ring = Ring([(sba("fm%d" % i, [128, 128], F32), Buf("fm%d" % i)) for i in range(6)])
            sm_ring = Ring([(sba("sm%d" % i, [128, 256], F32), Buf("sm%d" % i)) for i in range(6)])
            smb_ring = Ring([(sba("smb%d" % i, [128, 512], BF16), Buf("smb%d" % i)) for i in range(6)])
            col_ring = Ring([(sba("col%d" % i, [128, 8], F32), Buf("col%d" % i)) for i in range(6)])
            clr_ring = Ring([(sba("clr%d" % i, [16, 128], BF16), Buf("clr%d" % i)) for i in range(2)])

            def proj_tm(xt, xtB, c0, cw):
                pA, pAB = pA_ring.next()
                for kc in range(8):
                    S.op("pe", lambda h, kc=kc, pA=pA: h.matmul(pA[:, 0:cw], lhsT=xt[:, kc, :], rhs=wbf[:, kc, c0:c0 + cw],
                                                                start=(kc == 0), stop=(kc == 7)),
                         reads=[xtB, B_w], writes=[pAB] if kc == 0 else [], appends=[pAB] if kc else [])
                return pA, pAB

            def proj_fm(xt, xtB, c0, cw):
                pA, pAB = pA_ring.next()
                for kc in range(8):
                    S.op("pe", lambda h, kc=kc, pA=pA: h.matmul(pA[0:cw, 0:128], lhsT=wbf[:, kc, c0:c0 + cw], rhs=xt[:, kc, :],
                                                                start=(kc == 0), stop=(kc == 7)),
                         reads=[xtB, B_w], writes=[pAB] if kc == 0 else [], appends=[pAB] if kc else [])
                return pA, pAB

            def rope_tm(src, srcB, csl, cslB, nh, dst, dstB, first=True):
                sv = src.rearrange("p (h d) -> p h d", d=64)
                dv = dst.rearrange("p (h d) -> p h d", d=64)
                C = csl[:, 0:nh * 32].rearrange("p (h d) -> p h d", d=32)
                Sn = csl[:, 256:256 + nh * 32].rearrange("p (h d) -> p h d", d=32)
                t1, t1B = t_ring.next()
                t2, t2B = t_ring.next()
                a1 = t1[:, 0:nh * 32].rearrange("p (h d) -> p h d", d=32)
                a2 = t2[:, 0:nh * 32].rearrange("p (h d) -> p h d", d=32)
                S.op("dve", lambda h: h.tensor_tensor(out=a1, in0=sv[:, :, 0:32], in1=C, op=ALU.mult), reads=[srcB, cslB], writes=[t1B])
                S.op("pool", lambda h: h.tensor_tensor(out=a2, in0=sv[:, :, 32:64], in1=Sn, op=ALU.mult), reads=[srcB, cslB], writes=[t2B])
                S.op("dve", lambda h: h.tensor_tensor(out=dv[:, :, 0:32], in0=a1, in1=a2, op=ALU.subtract), reads=[t1B, t2B],
                     writes=[dstB] if first else [], appends=[] if first else [dstB])
                t3, t3B = t_ring.next()
                t4, t4B = t_ring.next()
                a3 = t3[:, 0:nh * 32].rearrange("p (h d) -> p h d", d=32)
                a4 = t4[:, 0:nh * 32].rearrange("p (h d) -> p h d", d=32)
                S.op("dve", lambda h: h.tensor_tensor(out=a3, in0=sv[:, :, 32:64], in1=C, op=ALU.mult), reads=[srcB, cslB], writes=[t3B])
                S.op("pool", lambda h: h.tensor_tensor(out=a4, in0=sv[:, :, 0:32], in1=Sn, op=ALU.mult), reads=[srcB, cslB], writes=[t4B])
                S.op("dve", lambda h: h.tensor_tensor(out=dv[:, :, 32:64], in0=a3, in1=a4, op=ALU.add), reads=[t3B, t4B], appends=[dstB])

            def evac(pA, pAB, cw, rows=128):
                ev, evB = ev_ring.next()
                S.op("act", lambda h: h.activation(out=ev[0:rows, 0:cw], in_=pA[0:rows, 0:cw], func=AF.Copy), reads=[pAB], writes=[evB])
                return ev, evB

            def tr_to(dst3, dstB, srcb, srcB, ntile, first=True):
                pT, pTB = pT_ring.next()
                for i in range(ntile):
                    S.op("pe", lambda h, i=i: h.transpose(out=pT[:, i * 128:(i + 1) * 128], in_=srcb[:, i * 128:(i + 1) * 128], identity=identb[:]),
                         reads=[srcB] + CONSTS, writes=[pTB] if i == 0 else [], appends=[pTB] if i else [])
                S.op("dve", lambda h: h.tensor_copy(out=dst3, in_=pT[:, 0:ntile * 128]), reads=[pTB],
                     writes=[dstB] if first else [], appends=[] if first else [dstB])

            for blk in range(NT if SL >= 1 else 0):
                t0 = blk * 128
                xt, xtB, xtD = xt_ring.next()
                dma("sp", xt[:], xT_v[:, :, t0:t0 + 128], [DB("xT", blk)], [xtB], xtD)
                csl, cslB, cslD = csl_ring.next()
                dma("sp", csl[:], cs_d[t0:t0 + 128, :], [DB("cs", blk)], [cslB], cslD)

                for (c0, dst_d, nm) in ((0, qT_d, "qT"), (512, kT_d, "kT"), (2048, qiT_d, "qiT")):
                    pA, pAB = proj_tm(xt, xtB, c0, 512)
                    ev, evB = evac(pA, pAB, 512)
                    rb, rbB = rb_ring.next()
                    rope_tm(ev[:, 0:512], evB, csl, cslB, 8, rb[:, 0:512], rbB)
                    qts, qtsB, qtsD = qts_ring.next()
                    tr_to(qts[:].rearrange("p a b -> p (a b)"), qtsB, rb, rbB, 4)
                    dma("sp", dst_d[:, :, t0:t0 + 128].rearrange("a p t -> p a t"), qts[:], [qtsB], [DB(nm, blk)], qtsD)
                if SL < 2:
                    continue
                pA, pAB = proj_tm(xt, xtB, 2560, 72)
                ev, evB = evac(pA, pAB, 72)
                rb, rbB = rb_ring.next()
                rope_tm(ev[:, 0:64], evB, csl, cslB, 1, rb[:, 0:64], rbB)
                pT, pTB = pT_ring.next()
                S.op("pe", lambda h: h.transpose(out=pT[0:64, 0:128], in_=rb[:, 0:64], identity=identb[:]), reads=[rbB] + CONSTS, writes=[pTB])
                kis, kisB, kisD = kis_ring.next()
                S.op("dve", lambda h: h.tensor_copy(out=kis[:], in_=pT[0:64, 0:128]), reads=[pTB], writes=[kisB])
                dma("sp", kiT_d[:, t0:t0 + 128], kis[:], [kisB], [DB("kiT", blk)], kisD)
                wis, wisB, wisD = wis_ring.next()
                S.op("dve", lambda h: h.tensor_scalar(out=wis[:], in0=ev[:, 64:72], scalar1=(8 ** -0.5) * (64 ** -0.5), scalar2=None, op0=ALU.mult),
                     reads=[evB], writes=[wisB])
                dma("sp", wi_d[t0:t0 + 128, :], wis[:], [wisB], [DB("wi", blk)], wisD)
                if SL < 3:
                    continue
                pA, pAB = proj_tm(xt, xtB, 1024, 512)
                vs, vsB, vsD = vs_ring.next()
                S.op("act", lambda h: h.activation(out=vs[:, :, 0:64], in_=pA[:, 0:512].rearrange("p (h d) -> p h d", d=64), func=AF.Copy),
                     reads=[pAB], writes=[vsB])
                S.op("pool", lambda h: h.memset(vs[:, :, 64:65], 1.0), appends=[vsB])
                dma("sp", V_d[t0:t0 + 128, :], vs[:].rearrange("p a b -> p (a b)"), [vsB], [DB("V", blk)], vsD)
                pA, pAB = proj_tm(xt, xtB, 1536, 512)
                ags, agsB, agsD = ags_ring.next()
                S.op("act", lambda h: h.activation(out=ags[:], in_=pA[:, 0:512], func=AF.Silu), reads=[pAB], writes=[agsB])
                dma("sp", ag_d[t0:t0 + 128, :], ags[:], [agsB], [DB("ag", blk)], agsD)

                if SL < 4:
                    continue
                ybc, ybcB, ybcD = ybc_ring.next()
                for ct in range(2):
                    pv_, pvB = proj_fm(xt, xtB, 2632 + ct * 128, 128)
                    pg_, pgB = proj_fm(xt, xtB, 2888 + ct * 128, 128)
                    sg, sgB = fm_ring.next()
                    S.op("act", lambda h, sg=sg, pg_=pg_: h.activation(out=sg[:], in_=pg_[:, 0:128], func=AF.Sigmoid), reads=[pgB], writes=[sgB])
                    S.op("dve", lambda h, sg=sg, pv_=pv_, ct=ct: h.tensor_tensor(out=hT[:, ct, 32 + t0:32 + t0 + 128], in0=pv_[:, 0:128], in1=sg[:], op=ALU.mult),
                         reads=[pvB, sgB], writes=[hB[blk]] if ct == 0 else [], appends=[hB[blk]] if ct else [])
                if SUB < 1:
                    continue
                hdeps = [hB[blk]] + ([hB[blk - 1]] if blk else [B_h0])
                cv = []
                for ct in range(2):
                    pC, pCB = pA_ring.next()
                    for j in range(CONVW):
                        S.op("pe", lambda h, ct=ct, j=j, pC=pC: h.matmul(pC[:, 0:128], lhsT=diag[:, ct, j, :],
                                                                         rhs=hT[:, ct, 32 + t0 - 30 + j:32 + t0 - 30 + j + 128],
                                                                         start=(j == 0), stop=(j == CONVW - 1)),
                             reads=hdeps + LW, writes=[pCB] if j == 0 else [], appends=[pCB] if j else [])
                    cvt, cvB = fm_ring.next()
                    S.op("act", lambda h, cvt=cvt, pC=pC, ct=ct: h.activation(out=cvt[:], in_=pC[:, 0:128], func=AF.Identity,
                                                                             bias=convb_sb[:, l, ct:ct + 1]), reads=[pCB] + LW, writes=[cvB])
                    cv.append((cvt, cvB))
                if SUB < 2:
                    continue
                cb16, cb16B = smb_ring.next()
                sq16, sq16B = smb_ring.next()
                for ct in range(2):
                    cvt, cvB = cv[ct]
                    S.op("dve", lambda h, cvt=cvt, ct=ct: h.tensor_copy(out=cb16[:, ct * 256:ct * 256 + 128], in_=cvt[:]), reads=[cvB],
                         writes=[cb16B] if ct == 0 else [], appends=[cb16B] if ct else [])
                    S.op("dve", lambda h, cvt=cvt, ct=ct: h.tensor_tensor(out=cb16[:, ct * 256 + 128:ct * 256 + 256], in0=cvt[:],
                                                                         in1=cb16[:, ct * 256:ct * 256 + 128], op=ALU.subtract),
                         reads=[cvB, cb16B], appends=[cb16B])
                    sqf, sqfB = fm_ring.next()
                    S.op("pool", lambda h, cvt=cvt, sqf=sqf: h.tensor_tensor(out=sqf[:], in0=cvt[:], in1=cvt[:], op=ALU.mult), reads=[cvB], writes=[sqfB])
                    S.op("dve", lambda h, sqf=sqf, ct=ct: h.tensor_copy(out=sq16[:, ct * 256:ct * 256 + 128], in_=sqf[:]), reads=[sqfB],
                         writes=[sq16B] if ct == 0 else [], appends=[sq16B] if ct else [])
                    S.op("dve", lambda h, sqf=sqf, ct=ct: h.tensor_tensor(out=sq16[:, ct * 256 + 128:ct * 256 + 256], in0=sqf[:],
                                                                         in1=sq16[:, ct * 256:ct * 256 + 128], op=ALU.subtract),
                         reads=[sqfB, sq16B], appends=[sq16B])
                pM, pMB = pS_ring.next()
                n = 0
                for ct in range(2):
                    for part in range(2):
                        S.op("pe", lambda h, ct=ct, part=part, n=n: h.matmul(pM[:, 0:128], lhsT=onesm[:], rhs=cb16[:, ct * 256 + part * 128:ct * 256 + part * 128 + 128],
                                                                             start=(n == 0), stop=(n == 3)),
                             reads=[cb16B, B_lw2], writes=[pMB] if n == 0 else [], appends=[pMB] if n else [])
                        n += 1
                n = 0
                for ct in range(2):
                    for part in range(2):
                        S.op("pe", lambda h, ct=ct, part=part, n=n: h.matmul(pM[:, 128:256], lhsT=onesm[:], rhs=sq16[:, ct * 256 + part * 128:ct * 256 + part * 128 + 128],
                                                                             start=(n == 0), stop=(n == 3)),
                             reads=[sq16B, B_lw2], appends=[pMB])
                        n += 1
                if SUB < 3:
                    continue
                st, stB = sm_ring.next()
                S.op("act", lambda h: h.activation(out=st[:, 0:256], in_=pM[:, 0:256], func=AF.Copy), reads=[pMB], writes=[stB])
                m2, m2B = fm_ring.next()
                S.op("dve", lambda h: h.tensor_tensor(out=m2[:], in0=st[:, 0:128], in1=st[:, 0:128], op=ALU.mult), reads=[stB], writes=[m2B])
                S.op("dve", lambda h: h.tensor_tensor(out=m2[:], in0=st[:, 128:256], in1=m2[:], op=ALU.subtract), reads=[stB, m2B], writes=[m2B])
                S.op("dve", lambda h: h.tensor_scalar(out=m2[:], in0=m2[:], scalar1=EPS, scalar2=None, op0=ALU.add), reads=[m2B], writes=[m2B])
                S.op("act", lambda h: h.activation(out=m2[:], in_=m2[:], func=AF.Sqrt), reads=[m2B], writes=[m2B])
                S.op("dve", lambda h: h.reciprocal(out=st[:, 128:256], in_=m2[:]), reads=[m2B, stB], writes=[stB])
                hs16, hs16B = smb_ring.next()
                for ct in range(2):
                    cvt, cvB = cv[ct]
                    S.op("dve", lambda h, cvt=cvt: h.tensor_tensor(out=cvt[:], in0=cvt[:], in1=st[:, 0:128], op=ALU.subtract), reads=[cvB, stB], writes=[cvB])
                    S.op("dve", lambda h, cvt=cvt: h.tensor_tensor(out=cvt[:], in0=cvt[:], in1=st[:, 128:256], op=ALU.mult), reads=[cvB, stB], writes=[cvB])
                    S.op("act", lambda h, cvt=cvt, ct=ct: h.activation(out=hs16[:, ct * 128:(ct + 1) * 128], in_=cvt[:], func=AF.Silu,
                                                                      scale=clng_sb[:, l, ct:ct + 1], bias=clnb_sb[:, l, ct:ct + 1]),
                         reads=[cvB] + LW, writes=[hs16B] if ct == 0 else [], appends=[hs16B] if ct else [])
                if SUB < 4:
                    continue
                for co in range(2):
                    pP, pPB = pA_ring.next()
                    for ct in range(2):
                        S.op("pe", lambda h, co=co, ct=ct, pP=pP: h.matmul(pP[:, 0:128], lhsT=pwb16[:, ct, co * 128:(co + 1) * 128],
                                                                           rhs=hs16[:, ct * 128:(ct + 1) * 128], start=(ct == 0), stop=(ct == 1)),
                             reads=[hs16B] + LW, writes=[pPB] if ct == 0 else [], appends=[pPB] if ct else [])
                    pG, pGB = proj_fm(xt, xtB, 3144 + co * 128, 128)
                    sgl, sglB = fm_ring.next()
                    S.op("act", lambda h, sgl=sgl, pG=pG: h.activation(out=sgl[:], in_=pG[:, 0:128], func=AF.Silu), reads=[pGB], writes=[sglB])
                    yb, ybB_ = fm_ring.next()
                    S.op("act", lambda h, yb=yb, pP=pP, co=co: h.activation(out=yb[:], in_=pP[:, 0:128], func=AF.Identity, bias=pwb_sb[:, l, co:co + 1]),
                         reads=[pPB] + LW, writes=[ybB_])
                    S.op("dve", lambda h, yb=yb, sgl=sgl, co=co: h.tensor_tensor(out=ybc[:, co, :], in0=yb[:], in1=sgl[:], op=ALU.mult),
                         reads=[ybB_, sglB], writes=[ybcB] if co == 0 else [], appends=[ybcB] if co else [])

                if SL < 5:
                    continue
                pK, pKB = proj_tm(xt, xtB, 3528, 384)
                kv, kvB = evac(pK, pKB, 384)
                vb, vbB = smb_ring.next()
                S.op("dve", lambda h: h.tensor_copy(out=vb[:, 0:256], in_=kv[:, 128:384]), reads=[kvB], writes=[vbB])
                pQ, pQB = proj_tm(xt, xtB, 3400, 128)
                qf, qfB = fm_ring.next()
                S.op("act", lambda h: h.activation(out=qf[:], in_=pQ[:, 0:128], func=AF.Copy, scale=32 ** -0.5), reads=[pQB], writes=[qfB])
                pL, pLB = proj_fm(xt, xtB, 4168, 16)
                clr, clrB = clr_ring.next()
                S.op("act", lambda h: h.activation(out=clr[:], in_=pL[0:16, 0:128], func=AF.Copy), reads=[pLB], writes=[clrB])
                if G < 1:
                    continue
                pZ, pZB = pS_ring.next()
                S.op("pe", lambda h: h.matmul(pZ[:, 0:128], lhsT=clr[:], rhs=gw2b[:], start=True, stop=False), reads=[clrB, B_lw2], writes=[pZB])
                S.op("pe", lambda h: h.matmul(pZ[:, 0:128], lhsT=onesb[:], rhs=gbb[:], start=False, stop=True), reads=[B_lw2], appends=[pZB])
                la, laB = fm_ring.next()
                S.op("act", lambda h: h.activation(out=la[:], in_=pZ[:, 0:128], func=AF.Exp, scale=-1.0), reads=[pZB], writes=[laB])
                S.op("act", lambda h: h.activation(out=la[:], in_=la[:], func=AF.Ln, bias=1.0), reads=[laB], writes=[laB])
                S.op("dve", lambda h: h.tensor_scalar(out=la[:], in0=la[:], scalar1=-1.0 / 16.0, scalar2=None, op0=ALU.mult), reads=[laB], writes=[laB])
                if G < 2:
                    continue
                pB_, pBB = pS_ring.next()
                S.op("pe", lambda h: h.matmul(pB_[:, 0:128], lhsT=trif[:], rhs=la[:], start=True, stop=True), reads=[laB] + CONSTS, writes=[pBB])
                S.op("pe", lambda h: h.matmul(pB_[:, 128:256], lhsT=onesf[:], rhs=la[:], start=True, stop=True), reads=[laB] + CONSTS, appends=[pBB])
                S.op("pe", lambda h: h.matmul(pB_[:, 256:257], lhsT=la[:], rhs=onesf[:, 0:1], start=True, stop=True), reads=[laB] + CONSTS, appends=[pBB])
                bb, bbB = sm_ring.next()
                S.op("act", lambda h: h.activation(out=bb[:, 0:256], in_=pB_[:, 0:256], func=AF.Copy), reads=[pBB], writes=[bbB])
                dec, decB = col_ring.next()
                S.op("act", lambda h: h.activation(out=dec[:, 0:1], in_=pB_[:, 256:257], func=AF.Exp), reads=[pBB], writes=[decB])
                eb, ebB = sm_ring.next()
                S.op("act", lambda h: h.activation(out=eb[:, 0:128], in_=bb[:, 0:128], func=AF.Exp), reads=[bbB], writes=[ebB])
                S.op("act", lambda h: h.activation(out=eb[:, 128:256], in_=bb[:, 0:128], func=AF.Exp, scale=-1.0), reads=[bbB], appends=[ebB])
                el, elB = fm_ring.next()
                S.op("dve", lambda h: h.tensor_tensor(out=el[:], in0=bb[:, 128:256], in1=bb[:, 0:128], op=ALU.subtract), reads=[bbB], writes=[elB])
                S.op("act", lambda h: h.activation(out=el[:], in_=el[:], func=AF.Exp), reads=[elB], writes=[elB])
                if G < 3:
                    continue
                qk16, qk16B = smb_ring.next()
                S.op("dve", lambda h: h.tensor_tensor(out=qk16[:, 0:128], in0=qf[:], in1=eb[:, 0:128], op=ALU.mult), reads=[qfB, ebB], writes=[qk16B])
                S.op("dve", lambda h: h.tensor_tensor(out=qk16[:, 128:256], in0=kv[:, 0:128], in1=eb[:, 128:256], op=ALU.mult), reads=[kvB, ebB], appends=[qk16B])
                Kbd, KbdB = Kbd_ring.next()
                for hh in range(4):
                    S.op("dve", lambda h, hh=hh: h.tensor_tensor(out=Kbd[:, hh, hh * 32:(hh + 1) * 32], in0=kv[:, hh * 32:(hh + 1) * 32],
                                                                 in1=el[:, hh * 32:(hh + 1) * 32], op=ALU.mult),
                         reads=[kvB, elB], writes=[KbdB] if hh == 0 else [], appends=[KbdB] if hh else [])
                pT, pTB = pT_ring.next()
                S.op("pe", lambda h: h.transpose(out=pT[:, 0:128], in_=qk16[:, 0:128], identity=identb[:]), reads=[qk16B] + CONSTS, writes=[pTB])
                S.op("pe", lambda h: h.transpose(out=pT[:, 128:256], in_=qk16[:, 128:256], identity=identb[:]), reads=[qk16B] + CONSTS, appends=[pTB])
                ktT, ktTB = smb_ring.next()
                S.op("act", lambda h: h.activation(out=ktT[:, 0:128], in_=pT[:, 128:256], func=AF.Copy), reads=[pTB], writes=[ktTB])
                Qbd, QbdB = smb_ring.next()
                for hh in range(4):
                    S.op("dve", lambda h, hh=hh: h.tensor_scalar(out=Qbd[:, hh * 128:(hh + 1) * 128], in0=pT[:, 0:128], scalar1=hmask[:, hh:hh + 1],
                                                                 scalar2=None, op0=ALU.mult),
                         reads=[pTB] + CONSTS, writes=[QbdB] if hh == 0 else [], appends=[QbdB] if hh else [])
                if G < 4:
                    continue
                pAt, pAtB = pA_ring.next()
                S.op("pe", lambda h: h.matmul(pAt[:, 0:512], lhsT=ktT[:, 0:128], rhs=Qbd[:, 0:512], start=True, stop=True), reads=[ktTB, QbdB], writes=[pAtB])
                AT, ATB = smb_ring.next()
                S.op("dve", lambda h: h.tensor_tensor(out=AT[:, 0:512], in0=pAt[:, 0:512], in1=cmaskb[:].rearrange("p a b -> p (a b)"), op=ALU.mult),
                     reads=[pAtB] + CONSTS, writes=[ATB])
                pO, pOB = pS_ring.next()
                for hh in range(4):
                    S.op("pe", lambda h, hh=hh: h.matmul(pO[:, hh * 64:(hh + 1) * 64], lhsT=AT[:, hh * 128:(hh + 1) * 128], rhs=vb[:, hh * 64:(hh + 1) * 64],
                                                         start=True, stop=False), reads=[ATB, vbB], writes=[pOB] if hh == 0 else [], appends=[pOB] if hh else [])
                    S.op("pe", lambda h, hh=hh: h.matmul(pO[:, hh * 64:(hh + 1) * 64], lhsT=Qbd[:, hh * 128:(hh + 1) * 128], rhs=Sbf[:],
                                                         start=False, stop=True), reads=[QbdB, B_Sb], appends=[pOB])
                pN, pNB = pS_ring.next()
                for hh in range(4):
                    S.op("pe", lambda h, hh=hh: h.matmul(pN[:, 0:64], lhsT=Kbd[:, hh, :], rhs=vb[:, hh * 64:(hh + 1) * 64], start=(hh == 0), stop=(hh == 3)),
                         reads=[KbdB, vbB], writes=[pNB] if hh == 0 else [], appends=[pNB] if hh else [])
                S.op("dve", lambda h: h.scalar_tensor_tensor(out=Sst[:], in0=Sst[:], scalar=dec[:, 0:1], in1=pN[:, 0:64], op0=ALU.mult, op1=ALU.add),
                     reads=[decB, pNB, B_S], writes=[B_S])
                S.op("dve", lambda h: h.tensor_copy(out=Sbf[:], in_=Sst[:]), reads=[B_S], writes=[B_Sb])
                if G < 5:
                    continue
                of, ofB = sm_ring.next()
                S.op("act", lambda h: h.activation(out=of[:, 0:256], in_=pO[:, 0:256], func=AF.Copy), reads=[pOB], writes=[ofB])
                osq, osqB = sm_ring.next()
                S.op("pool", lambda h: h.tensor_tensor(out=osq[:, 0:256], in0=of[:, 0:256], in1=of[:, 0:256], op=ALU.mult), reads=[ofB], writes=[osqB])
                ss, ssB = col_ring.next()
                S.op("dve", lambda h: h.tensor_reduce(out=ss[:, 0:4], in_=osq[:, 0:256].rearrange("p (h e) -> p h e", e=64), axis=AX.X, op=ALU.add),
                     reads=[osqB], writes=[ssB])
                S.op("dve", lambda h: h.tensor_scalar(out=ss[:, 0:4], in0=ss[:, 0:4], scalar1=1.0 / 64.0, scalar2=EPS, op0=ALU.mult, op1=ALU.add),
                     reads=[ssB], writes=[ssB])
                S.op("act", lambda h: h.activation(out=ss[:, 0:4], in_=ss[:, 0:4], func=AF.Sqrt), reads=[ssB], writes=[ssB])
                S.op("dve", lambda h: h.reciprocal(out=ss[:, 0:4], in_=ss[:, 0:4]), reads=[ssB], writes=[ssB])
                for hh in range(4):
                    S.op("dve", lambda h, hh=hh: h.tensor_scalar(out=of[:, hh * 64:(hh + 1) * 64], in0=of[:, hh * 64:(hh + 1) * 64], scalar1=ss[:, hh:hh + 1],
                                                                 scalar2=None, op0=ALU.mult), reads=[ofB, ssB], writes=[ofB])
                S.op("dve", lambda h: h.tensor_tensor(out=of[:, 0:256], in0=of[:, 0:256], in1=gng[:], op=ALU.mult), reads=[ofB] + LW, writes=[ofB])
                pCg, pCgB = proj_tm(xt, xtB, 3912, 256)
                cgs, cgsB = sm_ring.next()
                S.op("act", lambda h: h.activation(out=cgs[:, 0:256], in_=pCg[:, 0:256], func=AF.Silu), reads=[pCgB], writes=[cgsB])
                yc16, yc16B = smb_ring.next()
                S.op("dve", lambda h: h.tensor_tensor(out=yc16[:, 0:256], in0=of[:, 0:256], in1=cgs[:, 0:256], op=ALU.mult), reads=[ofB, cgsB], writes=[yc16B])
                tr_to(ybc[:, 2:4, :].rearrange("p a b -> p (a b)"), ybcB, yc16, yc16B, 2, first=False)
                dma("sp", ybc_d[:, :, t0:t0 + 128].rearrange("a p t -> p a t"), ybc[:], [ybcB], [DB("ybc", blk)], ybcD)

        S.barrier()
        if SL < 6:
            continue
        with ExitStack() as pd:
            def sbd(name, shape, dt):
                return pd.enter_context(nc.sbuf_tensor("%s_d%d" % (name, l), list(shape), dt))

            def slotd(name, shape, dt, n=2):
                items = []
                for i in range(n):
                    t = sbd("%s_%d" % (name, i), shape, dt)
                    items.append((t, Buf("%s_%d" % (name, i)), S.dsem("d%s_%d" % (name, i))))
                return Ring(items)

            kiT = sbd("kiT_sb", [128, T], BF16)
            kiB = Buf("kiT_sb")
            dma_multi("sp", [(kiT[0:64, :], kiT_d[:, :]), (kiT[64:128, :], kiT_d[:, :])],
                      [DB("kiT", b) for b in range(NT)], [kiB], S.dsem("kiT_sb"))
            wob = sbd("wob", [128, 8, 1024], BF16)
            B_wo = Buf("wob")
            rl_ring = Ring([(sbd("rl%d" % i, [128, 1024], F32), Buf("rl%d" % i), S.dsem("drl%d" % i)) for i in range(3)])
            wo_v = w_out[l].rearrange("(kc p) n -> p kc n", p=128)
            for hf in range(8):
                wos, wosB, wosD = rl_ring.next()
                wv = wos[:].rearrange("p (a b) -> p a b", b=128)
                dma("sp", wv, wo_v[:, :, hf * 128:(hf + 1) * 128], [], [wosB], wosD)
                if hf % 2:
                    S.op("act", lambda h: h.activation(out=wob[:, :, hf * 128:(hf + 1) * 128], in_=wv, func=AF.Copy), reads=[wosB], appends=[B_wo])
                else:
                    S.op("dve", lambda h: h.tensor_copy(out=wob[:, :, hf * 128:(hf + 1) * 128], in_=wv), reads=[wosB], appends=[B_wo])
            rl_ring = Ring([(a, b) for (a, b, c) in rl_ring.items])
            lng = sbd("lng", [128, 1024], F32)
            lnb = sbd("lnb", [128, 1024], F32)
            B_ln = Buf("ln")
            dln = S.dsem("ln")
            dma("sp", lng[:], ln_g[l], [], [], dln, appends=[B_ln])
            dma("sp", lnb[:], ln_b[l], [], [], dln, appends=[B_ln])

            score = sbd("score", [128, T], F32)
            scB = Buf("score")
            m01 = sbd("m01", [128, T], BF16)
            m01B = Buf("m01")
            junk = m01
            jkB = m01B
            mT_ring = Ring([(sbd("mT%d" % i, [128, NT, 128], BF16), Buf("mT%d" % i)) for i in range(2)])
            q_ring = slotd("qd", [128, 4, 128], BF16, 2)
            qi_ring = slotd("qid", [128, 4, 128], BF16, 2)
            wi_ring = slotd("wid", [128, 8], F32, 2)
            kv_ring = slotd("kvd", [128, 4, 512 + 520], BF16, 3)
            pt_ring = Ring([(sbd("pt%d" % i, [128, 512], BF16), Buf("pt%d" % i)) for i in range(4)])
            bis = sbd("bis", [128, 8], F32)
            bisB = Buf("bis")
            bisA = sbd("bisA", [128, 2], F32)
            bisAB = Buf("bisA")
            jkAB = Buf("junkA")
            junkA = sbd("junkA", [128, (T * 3) // 10 + 128], BF16)
            hs = sbd("hs", [128, NBIS + 1], F32)
            hsB = Buf("hs")
            ya = sbd("ya", [128, 8, 64], F32)
            yaB = Buf("ya")
            agl_ring = slotd("agl", [128, 512], F32, 1)
            ya16 = sbd("ya16", [128, 512], BF16)
            ya16B = Buf("ya16")
            mixT = sbd("mixT", [128, 4, 128], BF16)
            mixB = Buf("mixT")
            ybl_ring = slotd("ybl", [128, 4, 128], BF16, 2)
            xl2_ring = slotd("xl2", [128, 1024], F32, 1)
            z_ring = slotd("z", [128, 1024], F32, 2)
            stat = sbd("stat", [128, 2, 6], F32)
            statB = Buf("stat")
            mv = sbd("mv", [128, 4], F32)
            mvB = Buf("mv")
            rcp = sbd("rcp", [128, 8], F32)
            rcpB = Buf("rcp")
            pOa, pOaB = (pS_ring.items[0][0][:, 0:260].rearrange("p (a b) -> p a b", b=65), pS_ring.items[0][1])
            pOb, pObB = (pS_ring.items[1][0][:, 0:260].rearrange("p (a b) -> p a b", b=65), pS_ring.items[1][1])
            ST = {}

            def stage1a(qb):
                t0 = qb * 128
                n = t0 + 128
                qd, qdB, qdD = q_ring.next()
                dma("sp", qd[:], qT_d[:, :, t0:t0 + 128].rearrange("a p t -> p a t"), [DB("qT", qb)], [qdB], qdD)
                qid, qidB, qidD = qi_ring.next()
                dma("sp", qid[:], qiT_d[:, :, t0:t0 + 128].rearrange("a p t -> p a t"), [DB("qiT", qb)], [qidB], qidD)
                wid, widB, widD = wi_ring.next()
                dma("sp", wid[:], wi_d[t0:t0 + 128, :], [DB("wi", qb)], [widB], widD)
                ST[qb] = (qd, qdB)
                nch = (n + 511) // 512
                steps = []

                def idx_step(hh, c2):
                    def run():
                        rl, rlB = rl_ring.next()
                        wtot = 0
                        for c in range(c2, min(c2 + 2, nch)):
                            k0 = c * 512
                            kw = min(512, n - k0)
                            pA, pAB = pA_ring.next()
                            pb = (hh % 2) * 64
                            S.op("pe", lambda h: h.matmul(pA[:, 0:kw], lhsT=qid[pb:pb + 64, hh // 2, :], rhs=kiT[pb:pb + 64, k0:k0 + kw],
                                                          start=True, stop=True), reads=[qidB, kiB], writes=[pAB])
                            off = (c - c2) * 512
                            S.op("act", lambda h: h.activation(out=rl[:, off:off + kw], in_=pA[:, 0:kw], func=AF.Relu),
                                 reads=[pAB], writes=[rlB] if c == c2 else [], appends=[rlB] if c != c2 else [])
                            wtot += kw
                        k0 = c2 * 512
                        if hh == 0:
                            S.op("dve", lambda h: h.tensor_scalar(out=score[:, k0:k0 + wtot], in0=rl[:, 0:wtot], scalar1=wid[:, 0:1],
                                                                  scalar2=None, op0=ALU.mult),
                                 reads=[rlB, widB], writes=[scB] if c2 == 0 else [], appends=[scB] if c2 else [])
                        else:
                            S.op("dve", lambda h: h.scalar_tensor_tensor(out=score[:, k0:k0 + wtot], in0=rl[:, 0:wtot],
                                                                         scalar=wid[:, hh:hh + 1], in1=score[:, k0:k0 + wtot],
                                                                         op0=ALU.mult, op1=ALU.add),
                                 reads=[rlB, widB, scB], writes=[scB])
                    return run

                for hh in range(8):
                    for c2 in range(0, nch, 2):
                        steps.append(idx_step(hh, c2))
                steps.append(lambda: bis_step(qb, n))
                return steps

            def bis_step(qb, n):
                S.op("dve", lambda h: h.tensor_reduce(out=bis[:, 0:1], in_=score[:, 0:n], axis=AX.X, op=ALU.max, apply_absolute_value=True),
                     reads=[scB], writes=[bisB])
                S.op("dve", lambda h: h.tensor_scalar(out=bis[:, 0:1], in0=bis[:, 0:1], scalar1=1.0, scalar2=None, op0=ALU.add), reads=[bisB], writes=[bisB])
                S.op("dve", lambda h: h.tensor_scalar(out=hs[:], in0=pow2[:, 0:NBIS + 1], scalar1=bis[:, 0:1], scalar2=None, op0=ALU.mult),
                     reads=[bisB] + CONSTS, writes=[hsB])
                S.op("dve", lambda h: h.memset(bis[:, 3:4], 0.0), reads=[bisB], writes=[bisB])
                S.op("dve", lambda h: h.tensor_tensor(out=score[:, n - 128:n], in0=score[:, n - 128:n], in1=cbias[:], op=ALU.add), reads=[scB, bisB] + CONSTS, writes=[scB])
                nA = 0
                nD = n - nA
                for it in range(NBIS):
                    if nA:
                        S.op("act", lambda h: h.activation(out=junkA[:, 0:nA], in_=score[:, nD:n], func=AF.Sign, scale=-1.0, bias=bis[:, 3:4],
                                                           accum_out=bisA[:, 0:1]), reads=[scB, bisB], writes=[bisAB, jkAB])
                    S.op("dve", lambda h: h.tensor_scalar(out=junk[:, 0:nD], in0=score[:, 0:nD], scalar1=bis[:, 3:4], scalar2=None, op0=ALU.is_ge, op1=ALU.add,
                                                          accum_out=bis[:, 4:5]), reads=[scB, bisB], writes=[bisB, jkB])
                    if nA:
                        S.op("dve", lambda h: h.scalar_tensor_tensor(out=bis[:, 4:5], in0=bisA[:, 0:1], scalar=-0.5, in1=bis[:, 4:5], op0=ALU.mult, op1=ALU.add),
                             reads=[bisB, bisAB], writes=[bisB])
                    S.op("dve", lambda h: h.tensor_scalar(out=bis[:, 5:6], in0=bis[:, 4:5], scalar1=float(topk) - 0.5 - nA / 2.0, scalar2=hs[:, it:it + 1], op0=ALU.is_ge, op1=ALU.mult),
                         reads=[bisB, hsB], writes=[bisB])
                    S.op("dve", lambda h: h.scalar_tensor_tensor(out=bis[:, 3:4], in0=bis[:, 5:6], scalar=hs[:, it + 1:it + 2], in1=bis[:, 3:4], op0=ALU.subtract, op1=ALU.add),
                         reads=[bisB, hsB], writes=[bisB])
                S.op("dve", lambda h: h.tensor_tensor(out=bis[:, 1:2], in0=bis[:, 3:4], in1=hs[:, NBIS:NBIS + 1], op=ALU.subtract), reads=[bisB, hsB], writes=[bisB])
                S.op("dve", lambda h: h.tensor_scalar(out=m01[:, 0:n], in0=score[:, 0:n], scalar1=bis[:, 1:2], scalar2=None, op0=ALU.is_ge),
                     reads=[scB, bisB], writes=[m01B])

            def stage1b(qb):
                nkb = qb + 1
                mT, mTB = mT_ring.next()
                ST[qb] = ST[qb] + (mT, mTB)
                for kb8 in range(0, nkb, 8):
                    cnt = min(8, nkb - kb8)
                    pT, pTB = pT_ring.next()
                    for i in range(cnt):
                        kb = kb8 + i
                        S.op("pe", lambda h: h.transpose(out=pT[:, i * 128:(i + 1) * 128], in_=m01[:, kb * 128:(kb + 1) * 128], identity=identb[:]),
                             reads=[m01B] + CONSTS, writes=[pTB] if i == 0 else [], appends=[pTB] if i else [])
                    S.op("act", lambda h: h.activation(out=mT[:, kb8:kb8 + cnt, :].rearrange("p a b -> p (a b)"), in_=pT[:, 0:cnt * 128], func=AF.Copy),
                         reads=[pTB], writes=[mTB] if kb8 == 0 else [], appends=[mTB] if kb8 else [])

            def stage2(qb, steps):
                t0 = qb * 128
                qd, qdB, mT, mTB = ST.pop(qb)
                nkb = qb + 1
                ngr = (nkb + 3) // 4
                units = [(g, hh) for g in range(ngr) for hh in range(8)]
                ust = {}
                kvs = {}
                LOOK = 2

                def emit_qk(u):
                    g, hh = units[u]
                    kb0 = g * 4
                    nb = min(4, nkb - kb0)
                    if hh == 0:
                        kvt, kvB_, kvD = kv_ring.next()
                        pairs = [(kvt[:, :, 0:nb * 128], kT_d[:, :, kb0 * 128:(kb0 + nb) * 128].rearrange("a p t -> p a t"))]
                        for j in range(nb):
                            pairs.append((kvt[:, j, 512:512 + 520], V_d[(kb0 + j) * 128:(kb0 + j + 1) * 128, :]))
                        dma_multi("sp", pairs, [DB("kT", kb0 + j) for j in range(nb)] + [DB("V", kb0 + j) for j in range(nb)], [kvB_], kvD)
                        kvs[g] = (kvt, kvB_)
                    kvt, kvB_ = kvs[g]
                    pb = (hh % 2) * 64
                    pL_, pLB_ = pA_ring.next()
                    for j in range(nb):
                        S.op("pe", lambda h: h.matmul(pL_[:, j * 128:(j + 1) * 128], lhsT=kvt[pb:pb + 64, hh // 2, j * 128:(j + 1) * 128],
                                                      rhs=qd[pb:pb + 64, hh // 2, :], start=True, stop=True),
                             reads=[kvB_, qdB], writes=[pLB_] if j == 0 else [], appends=[pLB_] if j else [])
                    pt, ptB = pt_ring.next()
                    S.op("act", lambda h: h.activation(out=pt[:, 0:nb * 128], in_=pL_[:, 0:nb * 128], func=AF.Exp, scale=0.125),
                         reads=[pLB_], writes=[ptB])
                    S.op("pool", lambda h: h.tensor_tensor(out=pt[:, 0:nb * 128], in0=pt[:, 0:nb * 128],
                                                           in1=mT[:, kb0:kb0 + nb, :].rearrange("p a b -> p (a b)"), op=ALU.mult),
                         reads=[ptB, mTB], writes=[ptB])
                    ust[u] = (pt, ptB, kvt, kvB_, nb)

                def emit_pv(u):
                    g, hh = units[u]
                    pt, ptB, kvt, kvB_, nb = ust.pop(u)
                    po, poB = (pOa, pOaB) if hh < 4 else (pOb, pObB)
                    for j in range(nb):
                        first = (g == 0 and j == 0)
                        last = (g == ngr - 1 and j == nb - 1)
                        S.op("pe", lambda h: h.matmul(po[:, hh % 4, :], lhsT=pt[:, j * 128:(j + 1) * 128], rhs=kvt[:, j, 512 + hh * 65:512 + (hh + 1) * 65],
                                                      start=(first and hh % 4 == 0), stop=last, skip_group_check=True),
                             reads=[ptB, kvB_], writes=[poB] if (first and hh % 4 == 0) else [], appends=[] if (first and hh % 4 == 0) else [poB])

                nst = len(steps)
                done = 0
                for u in range(len(units) + LOOK):
                    want = min(nst, ((u + 1) * nst + len(units) - 1) // len(units))
                    while done < want:
                        steps[done]()
                        done += 1
                    if u < len(units):
                        emit_qk(u)
                    if u - LOOK >= 0:
                        emit_pv(u - LOOK)
                while done < nst:
                    steps[done]()
                    done += 1
                for hd in range(8):
                    po, poB = (pOa, pOaB) if hd < 4 else (pOb, pObB)
                    S.op("dve", lambda h: h.reciprocal(out=rcp[:, hd:hd + 1], in_=po[:, hd % 4, 64:65]), reads=[poB], writes=[rcpB] if hd == 0 else [],
                         appends=[rcpB] if hd else [])
                for hd in range(8):
                    po, poB = (pOa, pOaB) if hd < 4 else (pOb, pObB)
                    S.op("dve", lambda h: h.tensor_scalar(out=ya[:, hd, :], in0=po[:, hd % 4, 0:64], scalar1=rcp[:, hd:hd + 1], scalar2=None, op0=ALU.mult),
                         reads=[poB, rcpB], writes=[yaB] if hd == 0 else [], appends=[yaB] if hd else [])
                agl, aglB, aglD = agl_ring.next()
                dma("sp", agl[:], ag_d[t0:t0 + 128, :], [DB("ag", qb)], [aglB], aglD)
                S.op("dve", lambda h: h.tensor_tensor(out=ya16[:], in0=ya[:].rearrange("p a b -> p (a b)"), in1=agl[:], op=ALU.mult), reads=[yaB, aglB], writes=[ya16B])
                pT, pTB = pT_ring.next()
                for i in range(4):
                    S.op("pe", lambda h: h.transpose(out=pT[:, i * 128:(i + 1) * 128], in_=ya16[:, i * 128:(i + 1) * 128], identity=identb[:]),
                         reads=[ya16B] + CONSTS, writes=[pTB] if i == 0 else [], appends=[pTB] if i else [])
                S.op("dve", lambda h: h.tensor_copy(out=mixT[:].rearrange("p a b -> p (a b)"), in_=pT[:, 0:512]), reads=[pTB], writes=[mixB])
                ybl, yblB, yblD = ybl_ring.next()
                dma("sp", ybl[:], ybc_d[:, :, t0:t0 + 128].rearrange("a p t -> p a t"), [DB("ybc", qb)], [yblB], yblD)
                xl2, xl2B, xl2D = xl2_ring.next()
                dma("sp", xl2[:], x_src[t0:t0 + 128, :], [DB("x", qb)], [xl2B], xl2D)
                z, zB, zD = z_ring.next()
                for hf in range(2):
                    pY, pYB = pA_ring.next()
                    for kc in range(8):
                        S.op("pe", lambda h: h.matmul(pY[:, 0:512], lhsT=(mixT[:, kc, :] if kc < 4 else ybl[:, kc - 4, :]),
                                                      rhs=wob[:, kc, hf * 512:(hf + 1) * 512], start=(kc == 0), stop=(kc == 7)),
                             reads=[mixB, yblB, B_wo], writes=[pYB] if kc == 0 else [], appends=[pYB] if kc else [])
                    S.op("dve", lambda h: h.scalar_tensor_tensor(out=z[:, hf * 512:(hf + 1) * 512], in0=xl2[:, hf * 512:(hf + 1) * 512], scalar=alpha,
                                                                 in1=pY[:, 0:512], op0=ALU.mult, op1=ALU.add),
                         reads=[pYB, xl2B], writes=[zB] if hf == 0 else [], appends=[zB] if hf else [])
                    S.op("dve", lambda h: h.bn_stats(out=stat[:, hf, :], in_=z[:, hf * 512:(hf + 1) * 512]), reads=[zB], writes=[statB] if hf == 0 else [],
                         appends=[statB] if hf else [])
                S.op("dve", lambda h: h.bn_aggr(out=mv[:, 0:2], in_=stat[:].rearrange("p a b -> p (a b)")), reads=[statB], writes=[mvB])
                S.op("dve", lambda h: h.tensor_scalar(out=mv[:, 2:3], in0=mv[:, 1:2], scalar1=EPS, scalar2=None, op0=ALU.add), reads=[mvB], writes=[mvB])
                S.op("act", lambda h: h.activation(out=mv[:, 2:3], in_=mv[:, 2:3], func=AF.Sqrt), reads=[mvB], writes=[mvB])
                S.op("dve", lambda h: h.reciprocal(out=mv[:, 3:4], in_=mv[:, 2:3]), reads=[mvB], writes=[mvB])
                S.op("dve", lambda h: h.tensor_scalar(out=z[:], in0=z[:], scalar1=mv[:, 0:1], scalar2=mv[:, 3:4], op0=ALU.subtract, op1=ALU.mult), reads=[zB, mvB], writes=[zB])
                S.op("pool", lambda h: h.tensor_tensor(out=z[:], in0=z[:], in1=lng[:], op=ALU.mult), reads=[zB, B_ln], writes=[zB])
                S.op("dve", lambda h: h.tensor_tensor(out=z[:], in0=z[:], in1=lnb[:], op=ALU.add), reads=[zB, B_ln], writes=[zB])
                dma("sp", x_dst[t0:t0 + 128, :], z[:], [zB], [DB("x", qb)], zD)
                if l < DEPTH - 1:
                    emit_xT(qb, z[:], zB)

            for st_ in stage1a(0):
                st_()
            stage1b(0)
            for qb in range(NT):
                stage2(qb, stage1a(qb + 1) if qb + 1 < NT else [])
                if qb + 1 < NT:
                    stage1b(qb + 1)

    S.barrier()
    print('KERNEL nops', S.nops, 'sems', len(S.dsems) + 5, flush=True)
    S.emit()
    es.close()
    return nc


def make_consts():
    ident = np.eye(128, dtype=np.float32)
    j = np.arange(128)
    tri = (j[:, None] <= j[None, :]).astype(np.float32)
    cb = np.where(j[None, :] <= j[:, None], 0.0, -1e30).astype(np.float32)
    hm = np.zeros((128, 4), np.float32)
    for h in range(4):
        hm[h * 32:(h + 1) * 32, h] = 1.0
    inv = (10000.0 ** (-np.arange(0, 64, 2, dtype=np.float32) / 64.0)).astype(np.float32)
    inv8 = np.tile((inv / np.float32(2.0 * math.pi)).astype(np.float32), 8)[None, :].repeat(128, 0).astype(np.float32)
    p2 = np.broadcast_to((2.0 ** (-np.arange(32, dtype=np.float32)))[None, :], (128, 32)).astype(np.float32)
    return {"c_pow2": np.ascontiguousarray(p2), "c_ident": ident, "c_tri": tri, "c_cbias": cb, "c_hmask": hm, "c_inv8": np.ascontiguousarray(inv8)}


_CACHE = {}
_NCORES = [8]
_DBG = [False]
_LAST = [None]


def kernel(x, positions, w_in, conv_w, conv_b, cln_g, cln_b, pw_w, pw_b, gate_w2, gate_b, gnorm_g, w_out, ln_g, ln_b):
    x = np.asarray(x)
    B, T, _ = x.shape
    DEPTH = int(np.asarray(w_in).shape[0])
    key = (T, DEPTH)
    if key not in _CACHE:
        _CACHE[key] = build_program(T, DEPTH, dbg=_DBG[0])
    nc = _CACHE[key]
    consts = make_consts()
    shared = {"w_in": w_in, "conv_w": conv_w, "conv_b": conv_b, "cln_g": cln_g, "cln_b": cln_b, "pw_w": pw_w, "pw_b": pw_b,
              "gate_w2": gate_w2, "gate_b": gate_b, "gnorm_g": gnorm_g, "w_out": w_out, "ln_g": ln_g, "ln_b": ln_b}
    shared = {k: np.asarray(v, dtype=np.float32) for k, v in shared.items()}
    shared["conv_w"] = shared["conv_w"].reshape(DEPTH, CONVW, 2, 128).transpose(0, 2, 3, 1)
    for nm in ("conv_b", "cln_g", "cln_b", "pw_b"):
        shared[nm] = shared[nm].reshape(DEPTH, 2, 128).transpose(0, 2, 1)
    for nm in ("gnorm_g", "ln_g", "ln_b"):
        shared[nm] = np.broadcast_to(shared[nm][:, None, :], (DEPTH, 128, shared[nm].shape[-1]))
    shared = {k: np.ascontiguousarray(v) for k, v in shared.items()}
    n_cores = _NCORES[0]
    in_maps = []
    for c in range(n_cores):
        b = c % B
        m = {"x": np.ascontiguousarray(x[b]), "positions": np.ascontiguousarray(np.asarray(positions)[b].astype(np.int32).reshape(T // 128, 128).T)}
        m.update(shared)
        m.update(consts)
        in_maps.append(m)
    res = run_bass_kernel_spmd(nc, in_maps, core_ids=list(range(n_cores)))
    _LAST[0] = res
    return np.stack([np.asarray(res.results[b]["out"]) for b in range(min(B, n_cores))], axis=0).astype(np.float32)
```

```python
import math
import os
from contextlib import ExitStack
import numpy as np
import concourse.bass as bass
import concourse.mybir as mybir
from concourse.bass_utils import run_bass_kernel_spmd

F32 = mybir.dt.float32
BF16 = mybir.dt.bfloat16
I32 = mybir.dt.int32
ALU = mybir.AluOpType
AF = mybir.ActivationFunctionType
AX = mybir.AxisListType

D_MODEL = 1024
D_IN = 4184
TOPK_MAX = 256
CONVW = 31
EPS = 1e-5
NEG = -30000.0
NBIS = 16


class Buf:
    __slots__ = ("writers", "readers", "old", "name", "excl")

    def __init__(self, name="", excl=False):
        self.excl = excl
        self.writers = []
        self.readers = []
        self.old = []
        self.name = name


class DSem:
    def __init__(self, sem):
        self.sem = sem
        self.count = 0


class Eng:
    def __init__(self, name, sem):
        self.name = name
        self.sem = sem
        self.count = 0
        self.known = {}
        self.prog = []


class _Rec:
    def __init__(self):
        self.calls = []

    def __getattr__(self, name):
        def f(*a, **k):
            self.calls.append((name, a, k))
            return self
        return f


class Sched:
    def __init__(self, nc, es):
        self.nc = nc
        self.es = es
        self.engs = {}
        for n in ("pe", "act", "dve", "pool", "sp"):
            self.engs[n] = Eng(n, es.enter_context(nc.semaphore("s_" + n)))
        self.dsems = {}
        self.nops = 0

    def dsem(self, key):
        if key not in self.dsems:
            self.dsems[key] = DSem(self.es.enter_context(self.nc.semaphore("d_%d" % len(self.dsems))))
        return self.dsems[key]

    def _compress(self, lst):
        if len(lst) > 6:
            best = {}
            for (s, v) in lst:
                k = id(s)
                if k not in best or best[k][1] < v:
                    best[k] = (s, v)
            lst[:] = list(best.values())

    def op(self, eng, fn, reads=(), writes=(), appends=(), dsem=None, ndma=1):
        e = self.engs[eng]
        deps = {}

        def need(t):
            k = id(t[0])
            if k not in deps or deps[k][1] < t[1]:
                deps[k] = t

        for b in reads:
            for t in b.writers:
                need(t)
            if b.excl:
                for t in b.readers:
                    if t[0] is not e.sem:
                        need(t)
        for b in writes:
            for t in b.writers:
                need(t)
            for t in b.readers:
                need(t)
        for b in appends:
            for t in b.readers:
                need(t)
            for t in b.old:
                need(t)
        waits = []
        for k, (s, v) in deps.items():
            if eng == "pe" and s is e.sem:
                continue
            if e.known.get(k, 0) < v:
                waits.append((s, v))
                e.known[k] = v
        if dsem is None:
            e.count += 1
            t = (e.sem, e.count)
            inc = (e.sem, 1)
        else:
            dsem.count += 16 * ndma
            t = (dsem.sem, dsem.count)
            inc = (dsem.sem, 16)
        rec = _Rec()
        fn(rec)
        e.prog.append((waits, rec.calls, inc))
        self.nops += 1
        for b in reads:
            b.readers.append(t)
            self._compress(b.readers)
        for b in writes:
            b.old = b.writers + b.readers
            self._compress(b.old)
            b.writers = [t]
            b.readers = []
        for b in appends:
            b.writers.append(t)
            self._compress(b.writers)
        return t

    def barrier(self):
        for e in self.engs.values():
            waits = []
            for f in self.engs.values():
                if f is e or f.count == 0:
                    continue
                if e.known.get(id(f.sem), 0) < f.count:
                    waits.append((f.sem, f.count))
                    e.known[id(f.sem)] = f.count
            for d in self.dsems.values():
                if d.count and e.known.get(id(d.sem), 0) < d.count:
                    waits.append((d.sem, d.count))
                    e.known[id(d.sem)] = d.count
            if waits:
                e.prog.append((waits, None, None))

    def emit(self):
        nc = self.nc
        hmap = {"pe": "tensor", "act": "scalar", "dve": "vector", "pool": "gpsimd", "sp": "sync"}
        with nc.Block() as block:
            for n, e in self.engs.items():
                def body(h, e=e):
                    for waits, fn, inc in e.prog:
                        for (s, v) in waits:
                            h.wait_ge(s, v)
                        if fn is None:
                            continue
                        for (name, a, k) in fn:
                            getattr(h, name)(*a, **k).then_inc(inc[0], inc[1])
                getattr(block, hmap[n])(body)


class Ring:
    def __init__(self, items):
        self.items = items
        self.i = 0

    def next(self):
        it = self.items[self.i % len(self.items)]
        self.i += 1
        return it


def build_program(T, DEPTH, dbg=False):
    NT = T // 128
    topk = min(TOPK_MAX, T // 4)
    alpha = (2 * DEPTH) ** 0.25
    nc = bass.Bass("TRN2", target_bir_lowering=False)
    es = ExitStack()
    S = Sched(nc, es)

    def din(name, shape, dt=F32):
        return nc.dram_tensor(name, list(shape), dt, kind="ExternalInput").ap()

    def dscr(name, shape, dt):
        return nc.dram_tensor(name, list(shape), dt, kind="ExternalOutput" if dbg else "Internal").ap()

    x_in = din("x", [T, D_MODEL])
    pos_in = din("positions", [128, T // 128], I32)
    w_in = din("w_in", [DEPTH, D_MODEL, D_IN])
    conv_w = din("conv_w", [DEPTH, 2, 128, CONVW])
    conv_b = din("conv_b", [DEPTH, 128, 2])
    cln_g = din("cln_g", [DEPTH, 128, 2])
    cln_b = din("cln_b", [DEPTH, 128, 2])
    pw_w = din("pw_w", [DEPTH, 256, 256])
    pw_b = din("pw_b", [DEPTH, 128, 2])
    gate_w2 = din("gate_w2", [DEPTH, 16, 128])
    gate_b = din("gate_b", [DEPTH, 128])
    gnorm_g = din("gnorm_g", [DEPTH, 128, 256])
    w_out = din("w_out", [DEPTH, 1024, 1024])
    ln_g = din("ln_g", [DEPTH, 128, 1024])
    ln_b = din("ln_b", [DEPTH, 128, 1024])
    c_ident = din("c_ident", [128, 128])
    c_tri = din("c_tri", [128, 128])
    c_cbias = din("c_cbias", [128, 128])
    c_hmask = din("c_hmask", [128, 4])
    c_inv8 = din("c_inv8", [128, 256])
    c_pow2 = din("c_pow2", [128, 32])
    out_d = nc.dram_tensor("out", [T, D_MODEL], F32, kind="ExternalOutput").ap()

    xcur = dscr("xcur", [T, D_MODEL], F32)
    xT_d = dscr("xT", [D_MODEL, T], BF16)
    cs_d = dscr("cs", [T, 512], F32)
    qT_d = dscr("qT", [4, 128, T], BF16)
    kT_d = dscr("kT", [4, 128, T], BF16)
    qiT_d = dscr("qiT", [4, 128, T], BF16)
    kiT_d = dscr("kiT", [64, T], BF16)
    wi_d = dscr("wi", [T, 8], F32)
    V_d = dscr("V", [T, 520], BF16)
    ag_d = dscr("ag", [T, 512], F32)
    ybc_d = dscr("ybcT", [4, 128, T], BF16)

    dbufs = {}

    def DB(name, i):
        k = (name, i)
        if k not in dbufs:
            dbufs[k] = Buf("%s%d" % (name, i))
        return dbufs[k]

    def sb(name, shape, dt):
        return es.enter_context(nc.sbuf_tensor(name, list(shape), dt))

    def ps(name, shape, dt):
        return es.enter_context(nc.psum_tensor(name, list(shape), dt))

    def slot(name, shape, dt, n=2):
        items = []
        for i in range(n):
            t = sb("%s_%d" % (name, i), shape, dt)
            items.append((t, Buf("%s_%d" % (name, i)), S.dsem("%s_%d" % (name, i))))
        return Ring(items)

    def dma(eng, out_ap, in_ap, reads, writes, dsem, appends=()):
        return S.op(eng, lambda h: h.dma_start(out=out_ap, in_=in_ap), reads=reads, writes=writes,
                    appends=appends, dsem=dsem)

    def dma_multi(eng, pairs, reads, writes, dsem):
        return S.op(eng, lambda h: [h.dma_start(out=o, in_=i) for (o, i) in pairs], reads=reads, writes=writes,
                    dsem=dsem, ndma=len(pairs))

    identf = sb("identf", [128, 128], F32)
    identb = sb("identb", [128, 128], BF16)
    trif = sb("trif", [128, 128], F32)
    onesf = sb("onesf", [128, 128], F32)
    cbias = sb("cbias", [128, 128], F32)
    hmask = sb("hmask", [128, 4], F32)
    cmaskb = sb("cmaskb", [128, 4, 128], F32)
    inv8 = sb("inv8", [128, 256], F32)
    pow2 = sb("pow2", [128, 32], F32)
    B_const = Buf("const")
    dsc = S.dsem("const")
    dma("sp", identf[:], c_ident, [], [], dsc, appends=[B_const])
    dma("sp", trif[:], c_tri, [], [], dsc, appends=[B_const])
    dma("sp", cbias[:], c_cbias, [], [], dsc, appends=[B_const])
    dma("sp", hmask[:], c_hmask, [], [], dsc, appends=[B_const])
    dma("sp", inv8[:], c_inv8, [], [], dsc, appends=[B_const])
    dma("sp", pow2[:], c_pow2, [], [], dsc, appends=[B_const])
    B_c2 = Buf("const2")
    S.op("dve", lambda h: h.tensor_copy(out=identb[:], in_=identf[:]), reads=[B_const], appends=[B_c2])
    S.op("dve", lambda h: h.memset(onesf[:], 1.0), appends=[B_c2])
    for hh in range(4):
        S.op("dve", lambda h, hh=hh: h.tensor_copy(out=cmaskb[:, hh, :], in_=trif[:]), reads=[B_const], appends=[B_c2])
    CONSTS = [B_const, B_c2]

    xb_ring = slot("xb", [128, 1024], BF16, 2)
    xts_ring = slot("xts", [128, 8, 128], BF16, 2)
    pT_ring = Ring([(ps("pT%d" % i, [128, 1024], BF16), Buf("pT%d" % i, True)) for i in range(2)])
    pA_ring = Ring([(ps("pA%d" % i, [128, 512], F32), Buf("pA%d" % i, True)) for i in range(4)])
    pS_ring = Ring([(ps("pS%d" % i, [128, 512], F32), Buf("pS%d" % i, True)) for i in range(2)])

    xT_v = xT_d.rearrange("(kc p) t -> p kc t", p=128)

    def emit_xT(blk, x_sb, x_buf):
        xb, xbB, _ = xb_ring.next()
        S.op("act", lambda h: h.activation(out=xb[:], in_=x_sb, func=AF.Copy), reads=[x_buf], writes=[xbB])
        pT, pTB = pT_ring.next()
        for kc in range(8):
            S.op("pe", lambda h, kc=kc: h.transpose(out=pT[:, kc * 128:(kc + 1) * 128], in_=xb[:, kc * 128:(kc + 1) * 128],
                                                    identity=identb[:]),
                 reads=[xbB] + CONSTS, writes=[pTB] if kc == 0 else [], appends=[pTB] if kc else [])
        xts, xtsB, xtsD = xts_ring.next()
        S.op("dve", lambda h: h.tensor_copy(out=xts[:].rearrange("p a b -> p (a b)"), in_=pT[:]), reads=[pTB], writes=[xtsB])
        dma("sp", xT_v[:, :, blk * 128:(blk + 1) * 128], xts[:], [xtsB], [DB("xT", blk)], xtsD)

    TWO_PI = 2.0 * math.pi
    MAGIC = 12582912.0
    posi = sb("posi", [128, NT], I32)
    posf = sb("posf", [128, NT], F32)
    B_pos = Buf("pos")
    dma("sp", posi[:], pos_in, [], [B_pos], S.dsem("pos"))
    S.op("dve", lambda h: h.tensor_copy(out=posf[:], in_=posi[:]), reads=[B_pos], writes=[B_pos])
    pp = ExitStack()

    def sbp(name, shape, dt):
        return pp.enter_context(nc.sbuf_tensor(name, list(shape), dt))

    def slotp(name, shape, dt, n=2):
        return Ring([(sbp("%s_%d" % (name, i), shape, dt), Buf("%s_%d" % (name, i)), S.dsem("%s_%d" % (name, i))) for i in range(n)])

    cs_ring = slotp("cs", [128, 512], F32, 2)
    ang_ring = Ring([(sbp("ang%d" % i, [128, 256], F32), Buf("ang%d" % i)) for i in range(2)])
    rr_ring = Ring([(sbp("rr%d" % i, [128, 256], F32), Buf("rr%d" % i)) for i in range(4)])
    xl_ring = slotp("xl", [128, 1024], F32, 2)
    for blk in range(NT):
        ang, angB = ang_ring.next()
        S.op("dve", lambda h, blk=blk, ang=ang: h.tensor_scalar(out=ang[:], in0=inv8[:], scalar1=posf[:, blk:blk + 1], scalar2=None,
                                                       op0=ALU.mult), reads=[B_pos] + CONSTS, writes=[angB])
        cst, cstB, cstD = cs_ring.next()
        for which, off in ((0, 0.25), (1, 0.0)):
            yo, yoB = rr_ring.next()
            rr, rrB = rr_ring.next()
            S.op("dve", lambda h, off=off, yo=yo: h.tensor_scalar(out=yo[:], in0=ang[:], scalar1=off, scalar2=None, op0=ALU.add), reads=[angB], writes=[yoB])
            S.op("dve", lambda h, yo=yo, rr=rr: h.tensor_scalar(out=rr[:], in0=yo[:], scalar1=MAGIC, scalar2=MAGIC,
                                                                op0=ALU.add, op1=ALU.subtract), reads=[yoB], writes=[rrB])
            S.op("dve", lambda h, yo=yo, rr=rr: h.tensor_tensor(out=rr[:], in0=yo[:], in1=rr[:], op=ALU.subtract), reads=[yoB, rrB], writes=[rrB])
            S.op("act", lambda h, which=which, rr=rr, cst=cst: h.activation(out=cst[:, which * 256:(which + 1) * 256], in_=rr[:], func=AF.Sin,
                                                            scale=TWO_PI), reads=[rrB],
                 writes=[cstB] if which == 0 else [], appends=[cstB] if which else [])
        dma("sp", cs_d[blk * 128:(blk + 1) * 128, :], cst[:], [cstB], [DB("cs", blk)], cstD)
        xl, xlB, xlD = xl_ring.next()
        dma("sp", xl[:], x_in[blk * 128:(blk + 1) * 128, :], [], [xlB], xlD)
        emit_xT(blk, xl[:], xlB)

    S.barrier()
    pp.close()
    convw_sb = sb("convw_sb", [128, DEPTH, 2, CONVW], F32)
    convb_sb = sb("convb_sb", [128, DEPTH, 2], F32)
    clng_sb = sb("clng_sb", [128, DEPTH, 2], F32)
    clnb_sb = sb("clnb_sb", [128, DEPTH, 2], F32)
    pwb_sb = sb("pwb_sb", [128, DEPTH, 2], F32)
    B_par = Buf("par")
    dpar = S.dsem("par")
    for l in range(DEPTH):
        for ct in range(2):
            dma("sp", convw_sb[:, l, ct, :], conv_w[l, ct], [], [], dpar, appends=[B_par])
        dma("sp", convb_sb[:, l, :], conv_b[l], [], [], dpar, appends=[B_par])
        dma("sp", clng_sb[:, l, :], cln_g[l], [], [], dpar, appends=[B_par])
        dma("sp", clnb_sb[:, l, :], cln_b[l], [], [], dpar, appends=[B_par])
        dma("sp", pwb_sb[:, l, :], pw_b[l], [], [], dpar, appends=[B_par])

    STOP = os.environ.get('KSTOP', '')
    SL = int(os.environ.get('KSL', '99'))
    SUB = int(os.environ.get('KSUB', '99'))
    G = int(os.environ.get('KG', '99'))
    for l in range(DEPTH if STOP != 'pro' else 0):
        x_src = x_in if l == 0 else xcur
        x_dst = out_d if l == DEPTH - 1 else xcur
        S.barrier()
        with ExitStack() as pa:
            def sba(name, shape, dt):
                return pa.enter_context(nc.sbuf_tensor("%s_a%d" % (name, l), list(shape), dt))

            def slota(name, shape, dt, n=2):
                items = []
                for i in range(n):
                    t = sba("%s_%d" % (name, i), shape, dt)
                    items.append((t, Buf("%s_%d" % (name, i)), S.dsem("a%s_%d" % (name, i))))
                return Ring(items)

            wbf = sba("wbf", [128, 8, D_IN], BF16)
            B_w = Buf("wbf")
            wst_ring = slota("wst", [128, 8, 256], F32, 2)
            w_v = w_in[l].rearrange("(kc p) n -> p kc n", p=128)
            c0 = 0
            ci = 0
            while c0 < D_IN:
                cw = min(256, D_IN - c0)
                wst, wstB, wstD = wst_ring.next()
                dma("sp", wst[:, :, 0:cw], w_v[:, :, c0:c0 + cw], [], [wstB], wstD)
                eng = "act" if ci % 2 == 0 else "dve"
                if eng == "act":
                    S.op("act", lambda h, c0=c0, cw=cw, wst=wst: h.activation(out=wbf[:, :, c0:c0 + cw], in_=wst[:, :, 0:cw], func=AF.Copy),
                         reads=[wstB], appends=[B_w])
                else:
                    S.op("dve", lambda h, c0=c0, cw=cw, wst=wst: h.tensor_copy(out=wbf[:, :, c0:c0 + cw], in_=wst[:, :, 0:cw]),
                         reads=[wstB], appends=[B_w])
                c0 += cw
                ci += 1
            gw2f = sba("gw2f", [16, 128], F32)
            gw2b = sba("gw2b", [16, 128], BF16)
            gbf = sba("gbf", [1, 128], F32)
            gbb = sba("gbb", [1, 128], BF16)
            onesb = sba("onesb", [1, 128], BF16)
            gng = sba("gng", [128, 256], F32)
            pwf = sba("pwf", [128, 2, 256], F32)
            pwb16 = sba("pwb16", [128, 2, 256], BF16)
            diag = sba("diag", [128, 2, CONVW, 128], BF16)
            onesm = sba("onesm", [128, 128], BF16)
            B_lw = Buf("lw")
            dlw = S.dsem("lw")
            dma("sp", gw2f[:], gate_w2[l], [], [], dlw, appends=[B_lw])
            dma("sp", gbf[:], gate_b[l:l + 1, :], [], [], dlw, appends=[B_lw])
            dma("sp", gng[:], gnorm_g[l], [], [], dlw, appends=[B_lw])
            dma("sp", pwf[:], pw_w[l].rearrange("(ct c) n -> c ct n", c=128), [], [], dlw, appends=[B_lw])
            B_lw2 = Buf("lw2")
            S.op("dve", lambda h: h.tensor_copy(out=gw2b[:], in_=gw2f[:]), reads=[B_lw], appends=[B_lw2])
            S.op("dve", lambda h: h.tensor_copy(out=gbb[:], in_=gbf[:]), reads=[B_lw], appends=[B_lw2])
            S.op("dve", lambda h: h.memset(onesb[:], 1.0), appends=[B_lw2])
            S.op("dve", lambda h: h.memset(onesm[:], 1.0 / 256.0), appends=[B_lw2])
            S.op("dve", lambda h: h.tensor_copy(out=pwb16[:].rearrange("p a b -> p (a b)"), in_=pwf[:].rearrange("p a b -> p (a b)")),
                 reads=[B_lw], appends=[B_lw2])
            for ct in range(2):
                for j in range(CONVW):
                    S.op("pool" if (j % 2) else "dve",
                         lambda h, ct=ct, j=j: h.tensor_scalar(out=diag[:, ct, j, :], in0=identf[:], scalar1=convw_sb[:, l, ct, j:j + 1],
                                                               scalar2=None, op0=ALU.mult),
                         reads=[B_par] + CONSTS, appends=[B_lw2])
            LW = [B_lw, B_lw2, B_par] + CONSTS

            hT = sba("hT", [128, 2, T + 32], BF16)
            B_h0 = Buf("h0")
            S.op("pool", lambda h: h.memset(hT[:, :, 0:32], 0.0), writes=[B_h0])
            hB = [Buf("hT%d" % i) for i in range(NT)]
            Sst = sba("Sst", [128, 64], F32)
            Sbf = sba("Sbf", [128, 64], BF16)
            B_S = Buf("S")
            B_Sb = Buf("Sb")
            S.op("dve", lambda h: h.memset(Sst[:], 0.0), writes=[B_S])
            S.op("dve", lambda h: h.memset(Sbf[:], 0.0), writes=[B_Sb])
            Kbd_ring = Ring([(sba("Kbd%d" % i, [128, 4, 128], BF16), Buf("Kbd%d" % i)) for i in range(2)])
            for (kb_t, kb_B) in Kbd_ring.items:
                S.op("pool", lambda h, kb_t=kb_t: h.memset(kb_t[:].rearrange("p a b -> p (a b)"), 0.0), writes=[kb_B])

            xt_ring = slota("xt", [128, 8, 128], BF16, 2)
            csl_ring = slota("csl", [128, 512], F32, 2)
            ev_ring = Ring([(sba("ev%d" % i, [128, 512], F32), Buf("ev%d" % i)) for i in range(3)])
            t_ring = Ring([(sba("tt%d" % i, [128, 256], F32), Buf("tt%d" % i)) for i in range(4)])
            rb_ring = Ring([(sba("rb%d" % i, [128, 512], BF16), Buf("rb%d" % i)) for i in range(2)])
            qts_ring = slota("qts", [128, 4, 128], BF16, 3)
            kis_ring = slota("kis", [64, 128], BF16, 2)
            wis_ring = slota("wis", [128, 8], F32, 2)
            vs_ring = slota("vs", [128, 8, 65], BF16, 2)
            ags_ring = slota("ags", [128, 512], F32, 2)
            ybc_ring = slota("ybc", [128, 4, 128], BF16, 2)
            fm_ring = Ring([(sba("fm%d" % i, [128, 128], F32), Buf("fm%d" % i)) for i in range(6)])
            sm_ring = Ring([(sba("sm%d" % i, [128, 256], F32), Buf("sm%d" % i)) for i in range(6)])
            smb_ring = Ring([(sba("smb%d" % i, [128, 512], BF16), Buf("smb%d" % i)) for i in range(6)])
            col_ring = Ring([(sba("col%d" % i, [128, 8], F32), Buf("col%d" % i)) for i in range(6)])
            clr_ring = Ring([(sba("clr%d" % i, [16, 128], BF16), Buf("clr%d" % i)) for i in range(2)])

            def proj_tm(xt, xtB, c0, cw):
                pA, pAB = pA_ring.next()
                for kc in range(8):
                    S.op("pe", lambda h, kc=kc, pA=pA: h.matmul(pA[:, 0:cw], lhsT=xt[:, kc, :], rhs=wbf[:, kc, c0:c0 + cw],
                                                                start=(kc == 0), stop=(kc == 7)),
                         reads=[xtB, B_w], writes=[pAB] if kc == 0 else [], appends=[pAB] if kc else [])
                return pA, pAB

            def proj_fm(xt, xtB, c0, cw):
                pA, pAB = pA_ring.next()
                for kc in range(8):
                    S.op("pe", lambda h, kc=kc, pA=pA: h.matmul(pA[0:cw, 0:128], lhsT=wbf[:, kc, c0:c0 + cw], rhs=xt[:, kc, :],
                                                                start=(kc == 0), stop=(kc == 7)),
                         reads=[xtB, B_w], writes=[pAB] if kc == 0 else [], appends=[pAB] if kc else [])
                return pA, pAB

            def rope_tm(src, srcB, csl, cslB, nh, dst, dstB, first=True):
                sv = src.rearrange("p (h d) -> p h d", d=64)
                dv = dst.rearrange("p (h d) -> p h d", d=64)
                C = csl[:, 0:nh * 32].rearrange("p (h d) -> p h d", d=32)
                Sn = csl[:, 256:256 + nh * 32].rearrange("p (h d) -> p h d", d=32)
                t1, t1B = t_ring.next()
                t2, t2B = t_ring.next()
                a1 = t1[:, 0:nh * 32].rearrange("p (h d) -> p h d", d=32)
                a2 = t2[:, 0:nh * 32].rearrange("p (h d) -> p h d", d=32)
                S.op("dve", lambda h: h.tensor_tensor(out=a1, in0=sv[:, :, 0:32], in1=C, op=ALU.mult), reads=[srcB, cslB], writes=[t1B])
                S.op("pool", lambda h: h.tensor_tensor(out=a2, in0=sv[:, :, 32:64], in1=Sn, op=ALU.mult), reads=[srcB, cslB], writes=[t2B])
                S.op("dve", lambda h: h.tensor_tensor(out=dv[:, :, 0:32], in0=a1, in1=a2, op=ALU.subtract), reads=[t1B, t2B],
                     writes=[dstB] if first else [], appends=[] if first else [dstB])
                t3, t3B = t_ring.next()
                t4, t4B = t_ring.next()
                a3 = t3[:, 0:nh * 32].rearrange("p (h d) -> p h d", d=32)
                a4 = t4[:, 0:nh * 32].rearrange("p (h d) -> p h d", d=32)
                S.op("dve", lambda h: h.tensor_tensor(out=a3, in0=sv[:, :, 32:64], in1=C, op=ALU.mult), reads=[srcB, cslB], writes=[t3B])
                S.op("pool", lambda h: h.tensor_tensor(out=a4, in0=sv[:, :, 0:32], in1=Sn, op=ALU.mult), reads=[srcB, cslB], writes=[t4B])
                S.op("dve", lambda h: h.tensor_tensor(out=dv[:, :, 32:64], in0=a3, in1=a4, op=ALU.add), reads=[t3B, t4B], appends=[dstB])

            def evac(pA, pAB, cw, rows=128):
                ev, evB = ev_ring.next()
                S.op("act", lambda h: h.activation(out=ev[0:rows, 0:cw], in_=pA[0:rows, 0:cw], func=AF.Copy), reads=[pAB], writes=[evB])
                return ev, evB

            def tr_to(dst3, dstB, srcb, srcB, ntile, first=True):
                pT, pTB = pT_ring.next()
                for i in range(ntile):
                    S.op("pe", lambda h, i=i: h.transpose(out=pT[:, i * 128:(i + 1) * 128], in_=srcb[:, i * 128:(i + 1) * 128], identity=identb[:]),
                         reads=[srcB] + CONSTS, writes=[pTB] if i == 0 else [], appends=[pTB] if i else [])
                S.op("dve", lambda h: h.tensor_copy(out=dst3, in_=pT[:, 0:ntile * 128]), reads=[pTB],
                     writes=[dstB] if first else [], appends=[] if first else [dstB])

            for blk in range(NT if SL >= 1 else 0):
                t0 = blk * 128
                xt, xtB, xtD = xt_ring.next()
                dma("sp", xt[:], xT_v[:, :, t0:t0 + 128], [DB("xT", blk)], [xtB], xtD)
                csl, cslB, cslD = csl_ring.next()
                dma("sp", csl[:], cs_d[t0:t0 + 128, :], [DB("cs", blk)], [cslB], cslD)

                for (c0, dst_d, nm) in ((0, qT_d, "qT"), (512, kT_d, "kT"), (2048, qiT_d, "qiT")):
                    pA, pAB = proj_tm(xt, xtB, c0, 512)
                    ev, evB = evac(pA, pAB, 512)
                    rb, rbB = rb_ring.next()
                    rope_tm(ev[:, 0:512], evB, csl, cslB, 8, rb[:, 0:512], rbB)
                    qts, qtsB, qtsD = qts_ring.next()
                    tr_to(qts[:].rearrange("p a b -> p (a b)"), qtsB, rb, rbB, 4)
                    dma("sp", dst_d[:, :, t0:t0 + 128].rearrange("a p t -> p a t"), qts[:], [qtsB], [DB(nm, blk)], qtsD)
                if SL < 2:
                    continue
                pA, pAB = proj_tm(xt, xtB, 2560, 72)
                ev, evB = evac(pA, pAB, 72)
                rb, rbB = rb_ring.next()
                rope_tm(ev[:, 0:64], evB, csl, cslB, 1, rb[:, 0:64], rbB)
                pT, pTB = pT_ring.next()
                S.op("pe", lambda h: h.transpose(out=pT[0:64, 0:128], in_=rb[:, 0:64], identity=identb[:]), reads=[rbB] + CONSTS, writes=[pTB])
                kis, kisB, kisD = kis_ring.next()
                S.op("dve", lambda h: h.tensor_copy(out=kis[:], in_=pT[0:64, 0:128]), reads=[pTB], writes=[kisB])
                dma("sp", kiT_d[:, t0:t0 + 128], kis[:], [kisB], [DB("kiT", blk)], kisD)
                wis, wisB, wisD = wis_ring.next()
                S.op("dve", lambda h: h.tensor_scalar(out=wis[:], in0=ev[:, 64:72], scalar1=(8 ** -0.5) * (64 ** -0.5), scalar2=None, op0=ALU.mult),
                     reads=[evB], writes=[wisB])
                dma("sp", wi_d[t0:t0 + 128, :], wis[:], [wisB], [DB("wi", blk)], wisD)
                if SL < 3:
                    continue
                pA, pAB = proj_tm(xt, xtB, 1024, 512)
                vs, vsB, vsD = vs_ring.next()
                S.op("act", lambda h: h.activation(out=vs[:, :, 0:64], in_=pA[:, 0:512].rearrange("p (h d) -> p h d", d=64), func=AF.Copy),
                     reads=[pAB], writes=[vsB])
                S.op("pool", lambda h: h.memset(vs[:, :, 64:65], 1.0), appends=[vsB])
                dma("sp", V_d[t0:t0 + 128, :], vs[:].rearrange("p a b -> p (a b)"), [vsB], [DB("V", blk)], vsD)
                pA, pAB = proj_tm(xt, xtB, 1536, 512)
                ags, agsB, agsD = ags_ring.next()
                S.op("act", lambda h: h.activation(out=ags[:], in_=pA[:, 0:512], func=AF.Silu), reads=[pAB], writes=[agsB])
                dma("sp", ag_d[t0:t0 + 128, :], ags[:], [agsB], [DB("ag", blk)], agsD)

                if SL < 4:
                    continue
                ybc, ybcB, ybcD = ybc_ring.next()
                for ct in range(2):
                    pv_, pvB = proj_fm(xt, xtB, 2632 + ct * 128, 128)
                    pg_, pgB = proj_fm(xt, xtB, 2888 + ct * 128, 128)
                    sg, sgB = fm_ring.next()
                    S.op("act", lambda h, sg=sg, pg_=pg_: h.activation(out=sg[:], in_=pg_[:, 0:128], func=AF.Sigmoid), reads=[pgB], writes=[sgB])
                    S.op("dve", lambda h, sg=sg, pv_=pv_, ct=ct: h.tensor_tensor(out=hT[:, ct, 32 + t0:32 + t0 + 128], in0=pv_[:, 0:128], in1=sg[:], op=ALU.mult),
                         reads=[pvB, sgB], writes=[hB[blk]] if ct == 0 else [], appends=[hB[blk]] if ct else [])
                if SUB < 1:
                    continue
                hdeps = [hB[blk]] + ([hB[blk - 1]] if blk else [B_h0])
                cv = []
                for ct in range(2):
                    pC, pCB = pA_ring.next()
                    for j in range(CONVW):
                        S.op("pe", lambda h, ct=ct, j=j, pC=pC: h.matmul(pC[:, 0:128], lhsT=diag[:, ct, j, :],
                                                                         rhs=hT[:, ct, 32 + t0 - 30 + j:32 + t0 - 30 + j + 128],
                                                                         start=(j == 0), stop=(j == CONVW - 1)),
                             reads=hdeps + LW, writes=[pCB] if j == 0 else [], appends=[pCB] if j else [])
                    cvt, cvB = fm_ring.next()
                    S.op("act", lambda h, cvt=cvt, pC=pC, ct=ct: h.activation(out=cvt[:], in_=pC[:, 0:128], func=AF.Identity,
                                                                             bias=convb_sb[:, l, ct:ct + 1]), reads=[pCB] + LW, writes=[cvB])
                    cv.append((cvt, cvB))
                if SUB < 2:
                    continue
                cb16, cb16B = smb_ring.next()
                sq16, sq16B = smb_ring.next()
                for ct in range(2):
                    cvt, cvB = cv[ct]
                    S.op("dve", lambda h, cvt=cvt, ct=ct: h.tensor_copy(out=cb16[:, ct * 256:ct * 256 + 128], in_=cvt[:]), reads=[cvB],
                         writes=[cb16B] if ct == 0 else [], appends=[cb16B] if ct else [])
                    S.op("dve", lambda h, cvt=cvt, ct=ct: h.tensor_tensor(out=cb16[:, ct * 256 + 128:ct * 256 + 256], in0=cvt[:],
                                                                         in1=cb16[:, ct * 256:ct * 256 + 128], op=ALU.subtract),
                         reads=[cvB, cb16B], appends=[cb16B])
                    sqf, sqfB = fm_ring.next()
                    S.op("pool", lambda h, cvt=cvt, sqf=sqf: h.tensor_tensor(out=sqf[:], in0=cvt[:], in1=cvt[:], op=ALU.mult), reads=[cvB], writes=[sqfB])
                    S.op("dve", lambda h, sqf=sqf, ct=ct: h.tensor_copy(out=sq16[:, ct * 256:ct * 256 + 128], in_=sqf[:]), reads=[sqfB],
                         writes=[sq16B] if ct == 0 else [], appends=[sq16B] if ct else [])
                    S.op("dve", lambda h, sqf=sqf, ct=ct: h.tensor_tensor(out=sq16[:, ct * 256 + 128:ct * 256 + 256], in0=sqf[:],
                                                                         in1=sq16[:, ct * 256:ct * 256 + 128], op=ALU.subtract),
                         reads=[sqfB, sq16B], appends=[sq16B])
                pM, pMB = pS_ring.next()
                n = 0
                for ct in range(2):
                    for part in range(2):
                        S.op("pe", lambda h, ct=ct, part=part, n=n: h.matmul(pM[:, 0:128], lhsT=onesm[:], rhs=cb16[:, ct * 256 + part * 128:ct * 256 + part * 128 + 128],
                                                                             start=(n == 0), stop=(n == 3)),
                             reads=[cb16B, B_lw2], writes=[pMB] if n == 0 else [], appends=[pMB] if n else [])
                        n += 1
                n = 0
                for ct in range(2):
                    for part in range(2):
                        S.op("pe", lambda h, ct=ct, part=part, n=n: h.matmul(pM[:, 128:256], lhsT=onesm[:], rhs=sq16[:, ct * 256 + part * 128:ct * 256 + part * 128 + 128],
                                                                             start=(n == 0), stop=(n == 3)),
                             reads=[sq16B, B_lw2], appends=[pMB])
                        n += 1
                if SUB < 3:
                    continue
                st, stB = sm_ring.next()
                S.op("act", lambda h: h.activation(out=st[:, 0:256], in_=pM[:, 0:256], func=AF.Copy), reads=[pMB], writes=[stB])
                m2, m2B = fm_ring.next()
                S.op("dve", lambda h: h.tensor_tensor(out=m2[:], in0=st[:, 0:128], in1=st[:, 0:128], op=ALU.mult), reads=[stB], writes=[m2B])
                S.op("dve", lambda h: h.tensor_tensor(out=m2[:], in0=st[:, 128:256], in1=m2[:], op=ALU.subtract), reads=[stB, m2B], writes=[m2B])
                S.op("dve", lambda h: h.tensor_scalar(out=m2[:], in0=m2[:], scalar1=EPS, scalar2=None, op0=ALU.add), reads=[m2B], writes=[m2B])
                S.op("act", lambda h: h.activation(out=m2[:], in_=m2[:], func=AF.Sqrt), reads=[m2B], writes=[m2B])
                S.op("dve", lambda h: h.reciprocal(out=st[:, 128:256], in_=m2[:]), reads=[m2B, stB], writes=[stB])
                hs16, hs16B = smb_ring.next()
                for ct in range(2):
                    cvt, cvB = cv[ct]
                    S.op("dve", lambda h, cvt=cvt: h.tensor_tensor(out=cvt[:], in0=cvt[:], in1=st[:, 0:128], op=ALU.subtract), reads=[cvB, stB], writes=[cvB])
                    S.op("dve", lambda h, cvt=cvt: h.tensor_tensor(out=cvt[:], in0=cvt[:], in1=st[:, 128:256], op=ALU.mult), reads=[cvB, stB], writes=[cvB])
                    S.op("act", lambda h, cvt=cvt, ct=ct: h.activation(out=hs16[:, ct * 128:(ct + 1) * 128], in_=cvt[:], func=AF.Silu,
                                                                      scale=clng_sb[:, l, ct:ct + 1], bias=clnb_sb[:, l, ct:ct + 1]),
                         reads=[cvB] + LW, writes=[hs16B] if ct == 0 else [], appends=[hs16B] if ct else [])
                if SUB < 4:
                    continue
                for co in range(2):
                    pP, pPB = pA_ring.next()
                    for ct in range(2):
                        S.op("pe", lambda h, co=co, ct=ct, pP=pP: h.matmul(pP[:, 0:128], lhsT=pwb16[:, ct, co * 128:(co + 1) * 128],
                                                                           rhs=hs16[:, ct * 128:(ct + 1) * 128], start=(ct == 0), stop=(ct == 1)),
                             reads=[hs16B] + LW, writes=[pPB] if ct == 0 else [], appends=[pPB] if ct else [])
                    pG, pGB = proj_fm(xt, xtB, 3144 + co * 128, 128)
                    sgl, sglB = fm_ring.next()
                    S.op("act", lambda h, sgl=sgl, pG=pG: h.activation(out=sgl[:], in_=pG[:, 0:128], func=AF.Silu), reads=[pGB], writes=[sglB])
                    yb, ybB_ = fm_ring.next()
                    S.op("act", lambda h, yb=yb, pP=pP, co=co: h.activation(out=yb[:], in_=pP[:, 0:128], func=AF.Identity, bias=pwb_sb[:, l, co:co + 1]),
                         reads=[pPB] + LW, writes=[ybB_])
                    S.op("dve", lambda h, yb=yb, sgl=sgl, co=co: h.tensor_tensor(out=ybc[:, co, :], in0=yb[:], in1=sgl[:], op=ALU.mult),
                         reads=[ybB_, sglB], writes=[ybcB] if co == 0 else [], appends=[ybcB] if co else [])

                if SL < 5:
                    continue
                pK, pKB = proj_tm(xt, xtB, 3528, 384)
                kv, kvB = evac(pK, pKB, 384)
                vb, vbB = smb_ring.next()
                S.op("dve", lambda h: h.tensor_copy(out=vb[:, 0:256], in_=kv[:, 128:384]), reads=[kvB], writes=[vbB])
                pQ, pQB = proj_tm(xt, xtB, 3400, 128)
                qf, qfB = fm_ring.next()
                S.op("act", lambda h: h.activation(out=qf[:], in_=pQ[:, 0:128], func=AF.Copy, scale=32 ** -0.5), reads=[pQB], writes=[qfB])
                pL, pLB = proj_fm(xt, xtB, 4168, 16)
                clr, clrB = clr_ring.next()
                S.op("act", lambda h: h.activation(out=clr[:], in_=pL[0:16, 0:128], func=AF.Copy), reads=[pLB], writes=[clrB])
                if G < 1:
                    continue
                pZ, pZB = pS_ring.next()
                S.op("pe", lambda h: h.matmul(pZ[:, 0:128], lhsT=clr[:], rhs=gw2b[:], start=True, stop=False), reads=[clrB, B_lw2], writes=[pZB])
                S.op("pe", lambda h: h.matmul(pZ[:, 0:128], lhsT=onesb[:], rhs=gbb[:], start=False, stop=True), reads=[B_lw2], appends=[pZB])
                la, laB = fm_ring.next()
                S.op("act", lambda h: h.activation(out=la[:], in_=pZ[:, 0:128], func=AF.Exp, scale=-1.0), reads=[pZB], writes=[laB])
                S.op("act", lambda h: h.activation(out=la[:], in_=la[:], func=AF.Ln, bias=1.0), reads=[laB], writes=[laB])
                S.op("dve", lambda h: h.tensor_scalar(out=la[:], in0=la[:], scalar1=-1.0 / 16.0, scalar2=None, op0=ALU.mult), reads=[laB], writes=[laB])
                if G < 2:
                    continue
                pB_, pBB = pS_ring.next()
                S.op("pe", lambda h: h.matmul(pB_[:, 0:128], lhsT=trif[:], rhs=la[:], start=True, stop=True), reads=[laB] + CONSTS, writes=[pBB])
                S.op("pe", lambda h: h.matmul(pB_[:, 128:256], lhsT=onesf[:], rhs=la[:], start=True, stop=True), reads=[laB] + CONSTS, appends=[pBB])
                S.op("pe", lambda h: h.matmul(pB_[:, 256:257], lhsT=la[:], rhs=onesf[:, 0:1], start=True, stop=True), reads=[laB] + CONSTS, appends=[pBB])
                bb, bbB = sm_ring.next()
                S.op("act", lambda h: h.activation(out=bb[:, 0:256], in_=pB_[:, 0:256], func=AF.Copy), reads=[pBB], writes=[bbB])
                dec, decB = col_ring.next()
                S.op("act", lambda h: h.activation(out=dec[:, 0:1], in_=pB_[:, 256:257], func=AF.Exp), reads=[pBB], writes=[decB])
                eb, ebB = sm_ring.next()
                S.op("act", lambda h: h.activation(out=eb[:, 0:128], in_=bb[:, 0:128], func=AF.Exp), reads=[bbB], writes=[ebB])
                S.op("act", lambda h: h.activation(out=eb[:, 128:256], in_=bb[:, 0:128], func=AF.Exp, scale=-1.0), reads=[bbB], appends=[ebB])
                el, elB = fm_ring.next()
                S.op("dve", lambda h: h.tensor_tensor(out=el[:], in0=bb[:, 128:256], in1=bb[:, 0:128], op=ALU.subtract), reads=[bbB], writes=[elB])
                S.op("act", lambda h: h.activation(out=el[:], in_=el[:], func=AF.Exp), reads=[elB], writes=[elB])
                if G < 3:
                    continue
                qk16, qk16B = smb_ring.next()
                S.op("dve", lambda h: h.tensor_tensor(out=qk16[:, 0:128], in0=qf[:], in1=eb[:, 0:128], op=ALU.mult), reads=[qfB, ebB], writes=[qk16B])
                S.op("dve", lambda h: h.tensor_tensor(out=qk16[:, 128:256], in0=kv[:, 0:128], in1=eb[:, 128:256], op=ALU.mult), reads=[kvB, ebB], appends=[qk16B])
                Kbd, KbdB = Kbd_ring.next()
                for hh in range(4):
                    S.op("dve", lambda h, hh=hh: h.tensor_tensor(out=Kbd[:, hh, hh * 32:(hh + 1) * 32], in0=kv[:, hh * 32:(hh + 1) * 32],
                                                                 in1=el[:, hh * 32:(hh + 1) * 32], op=ALU.mult),
                         reads=[kvB, elB], writes=[KbdB] if hh == 0 else [], appends=[KbdB] if hh else [])
                pT, pTB = pT_ring.next()
                S.op("pe", lambda h: h.transpose(out=pT[:, 0:128], in_=qk16[:, 0:128], identity=identb[:]), reads=[qk16B] + CONSTS, writes=[pTB])
                S.op("pe", lambda h: h.transpose(out=pT[:, 128:256], in_=qk16[:, 128:256], identity=identb[:]), reads=[qk16B] + CONSTS, appends=[pTB])
                ktT, ktTB = smb_ring.next()
                S.op("act", lambda h: h.activation(out=ktT[:, 0:128], in_=pT[:, 128:256], func=AF.Copy), reads=[pTB], writes=[ktTB])
                Qbd, QbdB = smb_ring.next()
                for hh in range(4):
                    S.op("dve", lambda h, hh=hh: h.tensor_scalar(out=Qbd[:, hh * 128:(hh + 1) * 128], in0=pT[:, 0:128], scalar1=hmask[:, hh:hh + 1],
                                                                 scalar2=None, op0=ALU.mult),
                         reads=[pTB] + CONSTS, writes=[QbdB] if hh == 0 else [], appends=[QbdB] if hh else [])
                if G < 4:
                    continue
                pAt, pAtB = pA_ring.next()
                S.op("pe", lambda h: h.matmul(pAt[:, 0:512], lhsT=ktT[:, 0:128], rhs=Qbd[:, 0:512], start=True, stop=True), reads=[ktTB, QbdB], writes=[pAtB])
                AT, ATB = smb_ring.next()
                S.op("dve", lambda h: h.tensor_tensor(out=AT[:, 0:512], in0=pAt[:, 0:512], in1=cmaskb[:].rearrange("p a b -> p (a b)"), op=ALU.mult),
                     reads=[pAtB] + CONSTS, writes=[ATB])
                pO, pOB = pS_ring.next()
                for hh in range(4):
                    S.op("pe", lambda h, hh=hh: h.matmul(pO[:, hh * 64:(hh + 1) * 64], lhsT=AT[:, hh * 128:(hh + 1) * 128], rhs=vb[:, hh * 64:(hh + 1) * 64],
                                                         start=True, stop=False), reads=[ATB, vbB], writes=[pOB] if hh == 0 else [], appends=[pOB] if hh else [])
                    S.op("pe", lambda h, hh=hh: h.matmul(pO[:, hh * 64:(hh + 1) * 64], lhsT=Qbd[:, hh * 128:(hh + 1) * 128], rhs=Sbf[:],
                                                         start=False, stop=True), reads=[QbdB, B_Sb], appends=[pOB])
                pN, pNB = pS_ring.next()
                for hh in range(4):
                    S.op("pe", lambda h, hh=hh: h.matmul(pN[:, 0:64], lhsT=Kbd[:, hh, :], rhs=vb[:, hh * 64:(hh + 1) * 64], start=(hh == 0), stop=(hh == 3)),
                         reads=[KbdB, vbB], writes=[pNB] if hh == 0 else [], appends=[pNB] if hh else [])
                S.op("dve", lambda h: h.scalar_tensor_tensor(out=Sst[:], in0=Sst[:], scalar=dec[:, 0:1], in1=pN[:, 0:64], op0=ALU.mult, op1=ALU.add),
                     reads=[decB, pNB, B_S], writes=[B_S])
                S.op("dve", lambda h: h.tensor_copy(out=Sbf[:], in_=Sst[:]), reads=[B_S], writes=[B_Sb])
                if G < 5:
                    continue
                of, ofB = sm_ring.next()
                S.op("act", lambda h: h.activation(out=of[:, 0:256], in_=pO[:, 0:256], func=AF.Copy), reads=[pOB], writes=[ofB])
                osq, osqB = sm_ring.next()
                S.op("pool", lambda h: h.tensor_tensor(out=osq[:, 0:256], in0=of[:, 0:256], in1=of[:, 0:256], op=ALU.mult), reads=[ofB], writes=[osqB])
                ss, ssB = col_ring.next()
                S.op("dve", lambda h: h.tensor_reduce(out=ss[:, 0:4], in_=osq[:, 0:256].rearrange("p (h e) -> p h e", e=64), axis=AX.X, op=ALU.add),
                     reads=[osqB], writes=[ssB])
                S.op("dve", lambda h: h.tensor_scalar(out=ss[:, 0:4], in0=ss[:, 0:4], scalar1=1.0 / 64.0, scalar2=EPS, op0=ALU.mult, op1=ALU.add),
                     reads=[ssB], writes=[ssB])
                S.op("act", lambda h: h.activation(out=ss[:, 0:4], in_=ss[:, 0:4], func=AF.Sqrt), reads=[ssB], writes=[ssB])
                S.op("dve", lambda h: h.reciprocal(out=ss[:, 0:4], in_=ss[:, 0:4]), reads=[ssB], writes=[ssB])
                for hh in range(4):
                    S.op("dve", lambda h, hh=hh: h.tensor_scalar(out=of[:, hh * 64:(hh + 1) * 64], in0=of[:, hh * 64:(hh + 1) * 64], scalar1=ss[:, hh:hh + 1],
                                                                 scalar2=None, op0=ALU.mult), reads=[ofB, ssB], writes=[ofB])
                S.op("dve", lambda h: h.tensor_tensor(out=of[:, 0:256], in0=of[:, 0:256], in1=gng[:], op=ALU.mult), reads=[ofB] + LW, writes=[ofB])
                pCg, pCgB = proj_tm(xt, xtB, 3912, 256)
                cgs, cgsB = sm_ring.next()
                S.op("act", lambda h: h.activation(out=cgs[:, 0:256], in_=pCg[:, 0:256], func=AF.Silu), reads=[pCgB], writes=[cgsB])
                yc16, yc16B = smb_ring.next()
                S.op("dve", lambda h: h.tensor_tensor(out=yc16[:, 0:256], in0=of[:, 0:256], in1=cgs[:, 0:256], op=ALU.mult), reads=[ofB, cgsB], writes=[yc16B])
                tr_to(ybc[:, 2:4, :].rearrange("p a b -> p (a b)"), ybcB, yc16, yc16B, 2, first=False)
                dma("sp", ybc_d[:, :, t0:t0 + 128].rearrange("a p t -> p a t"), ybc[:], [ybcB], [DB("ybc", blk)], ybcD)

        S.barrier()
        if SL < 6:
            continue
        with ExitStack() as pd:
            def sbd(name, shape, dt):
                return pd.enter_context(nc.sbuf_tensor("%s_d%d" % (name, l), list(shape), dt))

            def slotd(name, shape, dt, n=2):
                items = []
                for i in range(n):
                    t = sbd("%s_%d" % (name, i), shape, dt)
                    items.append((t, Buf("%s_%d" % (name, i)), S.dsem("d%s_%d" % (name, i))))
                return Ring(items)

            kiT = sbd("kiT_sb", [128, T], BF16)
            kiB = Buf("kiT_sb")
            dma_multi("sp", [(kiT[0:64, :], kiT_d[:, :]), (kiT[64:128, :], kiT_d[:, :])],
                      [DB("kiT", b) for b in range(NT)], [kiB], S.dsem("kiT_sb"))
            wob = sbd("wob", [128, 8, 1024], BF16)
            B_wo = Buf("wob")
            rl_ring = Ring([(sbd("rl%d" % i, [128, 1024], F32), Buf("rl%d" % i), S.dsem("drl%d" % i)) for i in range(3)])
            wo_v = w_out[l].rearrange("(kc p) n -> p kc n", p=128)
            for hf in range(8):
                wos, wosB, wosD = rl_ring.next()
                wv = wos[:].rearrange("p (a b) -> p a b", b=128)
                dma("sp", wv, wo_v[:, :, hf * 128:(hf + 1) * 128], [], [wosB], wosD)
                if hf % 2:
                    S.op("act", lambda h: h.activation(out=wob[:, :, hf * 128:(hf + 1) * 128], in_=wv, func=AF.Copy), reads=[wosB], appends=[B_wo])
                else:
                    S.op("dve", lambda h: h.tensor_copy(out=wob[:, :, hf * 128:(hf + 1) * 128], in_=wv), reads=[wosB], appends=[B_wo])
            rl_ring = Ring([(a, b) for (a, b, c) in rl_ring.items])
            lng = sbd("lng", [128, 1024], F32)
            lnb = sbd("lnb", [128, 1024], F32)
            B_ln = Buf("ln")
            dln = S.dsem("ln")
            dma("sp", lng[:], ln_g[l], [], [], dln, appends=[B_ln])
            dma("sp", lnb[:], ln_b[l], [], [], dln, appends=[B_ln])

            score = sbd("score", [128, T], F32)
            scB = Buf("score")
            m01 = sbd("m01", [128, T], BF16)
            m01B = Buf("m01")
            junk = m01
            jkB = m01B
            mT_ring = Ring([(sbd("mT%d" % i, [128, NT, 128], BF16), Buf("mT%d" % i)) for i in range(2)])
            q_ring = slotd("qd", [128, 4, 128], BF16, 2)
            qi_ring = slotd("qid", [128, 4, 128], BF16, 2)
            wi_ring = slotd("wid", [128, 8], F32, 2)
            kv_ring = slotd("kvd", [128, 4, 512 + 520], BF16, 3)
            pt_ring = Ring([(sbd("pt%d" % i, [128, 512], BF16), Buf("pt%d" % i)) for i in range(4)])
            bis = sbd("bis", [128, 8], F32)
            bisB = Buf("bis")
            bisA = sbd("bisA", [128, 2], F32)
            bisAB = Buf("bisA")
            jkAB = Buf("junkA")
            junkA = sbd("junkA", [128, (T * 3) // 10 + 128], BF16)
            hs = sbd("hs", [128, NBIS + 1], F32)
            hsB = Buf("hs")
            ya = sbd("ya", [128, 8, 64], F32)
            yaB = Buf("ya")
            agl_ring = slotd("agl", [128, 512], F32, 1)
            ya16 = sbd("ya16", [128, 512], BF16)
            ya16B = Buf("ya16")
            mixT = sbd("mixT", [128, 4, 128], BF16)
            mixB = Buf("mixT")
            ybl_ring = slotd("ybl", [128, 4, 128], BF16, 2)
            xl2_ring = slotd("xl2", [128, 1024], F32, 1)
            z_ring = slotd("z", [128, 1024], F32, 2)
            stat = sbd("stat", [128, 2, 6], F32)
            statB = Buf("stat")
            mv = sbd("mv", [128, 4], F32)
            mvB = Buf("mv")
            rcp = sbd("rcp", [128, 8], F32)
            rcpB = Buf("rcp")
            pOa, pOaB = (pS_ring.items[0][0][:, 0:260].rearrange("p (a b) -> p a b", b=65), pS_ring.items[0][1])
            pOb, pObB = (pS_ring.items[1][0][:, 0:260].rearrange("p (a b) -> p a b", b=65), pS_ring.items[1][1])
            ST = {}

            def stage1a(qb):
                t0 = qb * 128
                n = t0 + 128
                qd, qdB, qdD = q_ring.next()
                dma("sp", qd[:], qT_d[:, :, t0:t0 + 128].rearrange("a p t -> p a t"), [DB("qT", qb)], [qdB], qdD)
                qid, qidB, qidD = qi_ring.next()
                dma("sp", qid[:], qiT_d[:, :, t0:t0 + 128].rearrange("a p t -> p a t"), [DB("qiT", qb)], [qidB], qidD)
                wid, widB, widD = wi_ring.next()
                dma("sp", wid[:], wi_d[t0:t0 + 128, :], [DB("wi", qb)], [widB], widD)
                ST[qb] = (qd, qdB)
                nch = (n + 511) // 512
                steps = []

                def idx_step(hh, c2):
                    def run():
                        rl, rlB = rl_ring.next()
                        wtot = 0
                        for c in range(c2, min(c2 + 2, nch)):
                            k0 = c * 512
                            kw = min(512, n - k0)
                            pA, pAB = pA_ring.next()
                            pb = (hh % 2) * 64
                            S.op("pe", lambda h: h.matmul(pA[:, 0:kw], lhsT=qid[pb:pb + 64, hh // 2, :], rhs=kiT[pb:pb + 64, k0:k0 + kw],
                                                          start=True, stop=True), reads=[qidB, kiB], writes=[pAB])
                            off = (c - c2) * 512
                            S.op("act", lambda h: h.activation(out=rl[:, off:off + kw], in_=pA[:, 0:kw], func=AF.Relu),
                                 reads=[pAB], writes=[rlB] if c == c2 else [], appends=[rlB] if c != c2 else [])
                            wtot += kw
                        k0 = c2 * 512
                        if hh == 0:
                            S.op("dve", lambda h: h.tensor_scalar(out=score[:, k0:k0 + wtot], in0=rl[:, 0:wtot], scalar1=wid[:, 0:1],
                                                                  scalar2=None, op0=ALU.mult),
                                 reads=[rlB, widB], writes=[scB] if c2 == 0 else [], appends=[scB] if c2 else [])
                        else:
                            S.op("dve", lambda h: h.scalar_tensor_tensor(out=score[:, k0:k0 + wtot], in0=rl[:, 0:wtot],
                                                                         scalar=wid[:, hh:hh + 1], in1=score[:, k0:k0 + wtot],
                                                                         op0=ALU.mult, op1=ALU.add),
                                 reads=[rlB, widB, scB], writes=[scB])
                    return run

                for hh in range(8):
                    for c2 in range(0, nch, 2):
                        steps.append(idx_step(hh, c2))
                steps.append(lambda: bis_step(qb, n))
                return steps

            def bis_step(qb, n):
                S.op("dve", lambda h: h.tensor_reduce(out=bis[:, 0:1], in_=score[:, 0:n], axis=AX.X, op=ALU.max, apply_absolute_value=True),
                     reads=[scB], writes=[bisB])
                S.op("dve", lambda h: h.tensor_scalar(out=bis[:, 0:1], in0=bis[:, 0:1], scalar1=1.0, scalar2=None, op0=ALU.add), reads=[bisB], writes=[bisB])
                S.op("dve", lambda h: h.tensor_scalar(out=hs[:], in0=pow2[:, 0:NBIS + 1], scalar1=bis[:, 0:1], scalar2=None, op0=ALU.mult),
                     reads=[bisB] + CONSTS, writes=[hsB])
                S.op("dve", lambda h: h.memset(bis[:, 3:4], 0.0), reads=[bisB], writes=[bisB])
                S.op("dve", lambda h: h.tensor_tensor(out=score[:, n - 128:n], in0=score[:, n - 128:n], in1=cbias[:], op=ALU.add), reads=[scB, bisB] + CONSTS, writes=[scB])
                nA = 0
                nD = n - nA
                for it in range(NBIS):
                    if nA:
                        S.op("act", lambda h: h.activation(out=junkA[:, 0:nA], in_=score[:, nD:n], func=AF.Sign, scale=-1.0, bias=bis[:, 3:4],
                                                           accum_out=bisA[:, 0:1]), reads=[scB, bisB], writes=[bisAB, jkAB])
                    S.op("dve", lambda h: h.tensor_scalar(out=junk[:, 0:nD], in0=score[:, 0:nD], scalar1=bis[:, 3:4], scalar2=None, op0=ALU.is_ge, op1=ALU.add,
                                                          accum_out=bis[:, 4:5]), reads=[scB, bisB], writes=[bisB, jkB])
                    if nA:
                        S.op("dve", lambda h: h.scalar_tensor_tensor(out=bis[:, 4:5], in0=bisA[:, 0:1], scalar=-0.5, in1=bis[:, 4:5], op0=ALU.mult, op1=ALU.add),
                             reads=[bisB, bisAB], writes=[bisB])
                    S.op("dve", lambda h: h.tensor_scalar(out=bis[:, 5:6], in0=bis[:, 4:5], scalar1=float(topk) - 0.5 - nA / 2.0, scalar2=hs[:, it:it + 1], op0=ALU.is_ge, op1=ALU.mult),
                         reads=[bisB, hsB], writes=[bisB])
                    S.op("dve", lambda h: h.scalar_tensor_tensor(out=bis[:, 3:4], in0=bis[:, 5:6], scalar=hs[:, it + 1:it + 2], in1=bis[:, 3:4], op0=ALU.subtract, op1=ALU.add),
                         reads=[bisB, hsB], writes=[bisB])
                S.op("dve", lambda h: h.tensor_tensor(out=bis[:, 1:2], in0=bis[:, 3:4], in1=hs[:, NBIS:NBIS + 1], op=ALU.subtract), reads=[bisB, hsB], writes=[bisB])
                S.op("dve", lambda h: h.tensor_scalar(out=m01[:, 0:n], in0=score[:, 0:n], scalar1=bis[:, 1:2], scalar2=None, op0=ALU.is_ge),
                     reads=[scB, bisB], writes=[m01B])

            def stage1b(qb):
                nkb = qb + 1
                mT, mTB = mT_ring.next()
                ST[qb] = ST[qb] + (mT, mTB)
                for kb8 in range(0, nkb, 8):
                    cnt = min(8, nkb - kb8)
                    pT, pTB = pT_ring.next()
                    for i in range(cnt):
                        kb = kb8 + i
                        S.op("pe", lambda h: h.transpose(out=pT[:, i * 128:(i + 1) * 128], in_=m01[:, kb * 128:(kb + 1) * 128], identity=identb[:]),
                             reads=[m01B] + CONSTS, writes=[pTB] if i == 0 else [], appends=[pTB] if i else [])
                    S.op("act", lambda h: h.activation(out=mT[:, kb8:kb8 + cnt, :].rearrange("p a b -> p (a b)"), in_=pT[:, 0:cnt * 128], func=AF.Copy),
                         reads=[pTB], writes=[mTB] if kb8 == 0 else [], appends=[mTB] if kb8 else [])

            def stage2(qb, steps):
                t0 = qb * 128
                qd, qdB, mT, mTB = ST.pop(qb)
                nkb = qb + 1
                ngr = (nkb + 3) // 4
                units = [(g, hh) for g in range(ngr) for hh in range(8)]
                ust = {}
                kvs = {}
                LOOK = 2

                def emit_qk(u):
                    g, hh = units[u]
                    kb0 = g * 4
                    nb = min(4, nkb - kb0)
                    if hh == 0:
                        kvt, kvB_, kvD = kv_ring.next()
                        pairs = [(kvt[:, :, 0:nb * 128], kT_d[:, :, kb0 * 128:(kb0 + nb) * 128].rearrange("a p t -> p a t"))]
                        for j in range(nb):
                            pairs.append((kvt[:, j, 512:512 + 520], V_d[(kb0 + j) * 128:(kb0 + j + 1) * 128, :]))
                        dma_multi("sp", pairs, [DB("kT", kb0 + j) for j in range(nb)] + [DB("V", kb0 + j) for j in range(nb)], [kvB_], kvD)
                        kvs[g] = (kvt, kvB_)
                    kvt, kvB_ = kvs[g]
                    pb = (hh % 2) * 64
                    pL_, pLB_ = pA_ring.next()
                    for j in range(nb):
                        S.op("pe", lambda h: h.matmul(pL_[:, j * 128:(j + 1) * 128], lhsT=kvt[pb:pb + 64, hh // 2, j * 128:(j + 1) * 128],
                                                      rhs=qd[pb:pb + 64, hh // 2, :], start=True, stop=True),
                             reads=[kvB_, qdB], writes=[pLB_] if j == 0 else [], appends=[pLB_] if j else [])
                    pt, ptB = pt_ring.next()
                    S.op("act", lambda h: h.activation(out=pt[:, 0:nb * 128], in_=pL_[:, 0:nb * 128], func=AF.Exp, scale=0.125),
                         reads=[pLB_], writes=[ptB])
                    S.op("pool", lambda h: h.tensor_tensor(out=pt[:, 0:nb * 128], in0=pt[:, 0:nb * 128],
                                                           in1=mT[:, kb0:kb0 + nb, :].rearrange("p a b -> p (a b)"), op=ALU.mult),
                         reads=[ptB, mTB], writes=[ptB])
                    ust[u] = (pt, ptB, kvt, kvB_, nb)

                def emit_pv(u):
                    g, hh = units[u]
                    pt, ptB, kvt, kvB_, nb = ust.pop(u)
                    po, poB = (pOa, pOaB) if hh < 4 else (pOb, pObB)
                    for j in range(nb):
                        first = (g == 0 and j == 0)
                        last = (g == ngr - 1 and j == nb - 1)
                        S.op("pe", lambda h: h.matmul(po[:, hh % 4, :], lhsT=pt[:, j * 128:(j + 1) * 128], rhs=kvt[:, j, 512 + hh * 65:512 + (hh + 1) * 65],
                                                      start=(first and hh % 4 == 0), stop=last, skip_group_check=True),
                             reads=[ptB, kvB_], writes=[poB] if (first and hh % 4 == 0) else [], appends=[] if (first and hh % 4 == 0) else [poB])

                nst = len(steps)
                done = 0
                for u in range(len(units) + LOOK):
                    want = min(nst, u + 1)
                    while done < want:
                        steps[done]()
                        done += 1
                    if u < len(units):
                        emit_qk(u)
                    if u - LOOK >= 0:
                        emit_pv(u - LOOK)
                while done < nst:
                    steps[done]()
                    done += 1
                for hd in range(8):
                    po, poB = (pOa, pOaB) if hd < 4 else (pOb, pObB)
                    S.op("dve", lambda h: h.reciprocal(out=rcp[:, hd:hd + 1], in_=po[:, hd % 4, 64:65]), reads=[poB], writes=[rcpB] if hd == 0 else [],
                         appends=[rcpB] if hd else [])
                for hd in range(8):
                    po, poB = (pOa, pOaB) if hd < 4 else (pOb, pObB)
                    S.op("dve", lambda h: h.tensor_scalar(out=ya[:, hd, :], in0=po[:, hd % 4, 0:64], scalar1=rcp[:, hd:hd + 1], scalar2=None, op0=ALU.mult),
                         reads=[poB, rcpB], writes=[yaB] if hd == 0 else [], appends=[yaB] if hd else [])
                agl, aglB, aglD = agl_ring.next()
                dma("sp", agl[:], ag_d[t0:t0 + 128, :], [DB("ag", qb)], [aglB], aglD)
                S.op("dve", lambda h: h.tensor_tensor(out=ya16[:], in0=ya[:].rearrange("p a b -> p (a b)"), in1=agl[:], op=ALU.mult), reads=[yaB, aglB], writes=[ya16B])
                pT, pTB = pT_ring.next()
                for i in range(4):
                    S.op("pe", lambda h: h.transpose(out=pT[:, i * 128:(i + 1) * 128], in_=ya16[:, i * 128:(i + 1) * 128], identity=identb[:]),
                         reads=[ya16B] + CONSTS, writes=[pTB] if i == 0 else [], appends=[pTB] if i else [])
                S.op("dve", lambda h: h.tensor_copy(out=mixT[:].rearrange("p a b -> p (a b)"), in_=pT[:, 0:512]), reads=[pTB], writes=[mixB])
                ybl, yblB, yblD = ybl_ring.next()
                dma("sp", ybl[:], ybc_d[:, :, t0:t0 + 128].rearrange("a p t -> p a t"), [DB("ybc", qb)], [yblB], yblD)
                xl2, xl2B, xl2D = xl2_ring.next()
                dma("sp", xl2[:], x_src[t0:t0 + 128, :], [DB("x", qb)], [xl2B], xl2D)
                z, zB, zD = z_ring.next()
                for hf in range(2):
                    pY, pYB = pA_ring.next()
                    for kc in range(8):
                        S.op("pe", lambda h: h.matmul(pY[:, 0:512], lhsT=(mixT[:, kc, :] if kc < 4 else ybl[:, kc - 4, :]),
                                                      rhs=wob[:, kc, hf * 512:(hf + 1) * 512], start=(kc == 0), stop=(kc == 7)),
                             reads=[mixB, yblB, B_wo], writes=[pYB] if kc == 0 else [], appends=[pYB] if kc else [])
                    S.op("dve", lambda h: h.scalar_tensor_tensor(out=z[:, hf * 512:(hf + 1) * 512], in0=xl2[:, hf * 512:(hf + 1) * 512], scalar=alpha,
                                                                 in1=pY[:, 0:512], op0=ALU.mult, op1=ALU.add),
                         reads=[pYB, xl2B], writes=[zB] if hf == 0 else [], appends=[zB] if hf else [])
                    S.op("dve", lambda h: h.bn_stats(out=stat[:, hf, :], in_=z[:, hf * 512:(hf + 1) * 512]), reads=[zB], writes=[statB] if hf == 0 else [],
                         appends=[statB] if hf else [])
                S.op("dve", lambda h: h.bn_aggr(out=mv[:, 0:2], in_=stat[:].rearrange("p a b -> p (a b)")), reads=[statB], writes=[mvB])
                S.op("dve", lambda h: h.tensor_scalar(out=mv[:, 2:3], in0=mv[:, 1:2], scalar1=EPS, scalar2=None, op0=ALU.add), reads=[mvB], writes=[mvB])
                S.op("act", lambda h: h.activation(out=mv[:, 2:3], in_=mv[:, 2:3], func=AF.Sqrt), reads=[mvB], writes=[mvB])
                S.op("dve", lambda h: h.reciprocal(out=mv[:, 3:4], in_=mv[:, 2:3]), reads=[mvB], writes=[mvB])
                S.op("dve", lambda h: h.tensor_scalar(out=z[:], in0=z[:], scalar1=mv[:, 0:1], scalar2=mv[:, 3:4], op0=ALU.subtract, op1=ALU.mult), reads=[zB, mvB], writes=[zB])
                S.op("pool", lambda h: h.tensor_tensor(out=z[:], in0=z[:], in1=lng[:], op=ALU.mult), reads=[zB, B_ln], writes=[zB])
                S.op("dve", lambda h: h.tensor_tensor(out=z[:], in0=z[:], in1=lnb[:], op=ALU.add), reads=[zB, B_ln], writes=[zB])
                dma("sp", x_dst[t0:t0 + 128, :], z[:], [zB], [DB("x", qb)], zD)
                if l < DEPTH - 1:
                    emit_xT(qb, z[:], zB)

            for st_ in stage1a(0):
                st_()
            stage1b(0)
            for qb in range(NT):
                stage2(qb, stage1a(qb + 1) if qb + 1 < NT else [])
                if qb + 1 < NT:
                    stage1b(qb + 1)

    S.barrier()
    print('KERNEL nops', S.nops, 'sems', len(S.dsems) + 5, flush=True)
    S.emit()
    es.close()
    return nc


def make_consts():
    ident = np.eye(128, dtype=np.float32)
    j = np.arange(128)
    tri = (j[:, None] <= j[None, :]).astype(np.float32)
    cb = np.where(j[None, :] <= j[:, None], 0.0, -1e30).astype(np.float32)
    hm = np.zeros((128, 4), np.float32)
    for h in range(4):
        hm[h * 32:(h + 1) * 32, h] = 1.0
    inv = (10000.0 ** (-np.arange(0, 64, 2, dtype=np.float32) / 64.0)).astype(np.float32)
    inv8 = np.tile((inv / np.float32(2.0 * math.pi)).astype(np.float32), 8)[None, :].repeat(128, 0).astype(np.float32)
    p2 = np.broadcast_to((2.0 ** (-np.arange(32, dtype=np.float32)))[None, :], (128, 32)).astype(np.float32)
    return {"c_pow2": np.ascontiguousarray(p2), "c_ident": ident, "c_tri": tri, "c_cbias": cb, "c_hmask": hm, "c_inv8": np.ascontiguousarray(inv8)}


_CACHE = {}
_NCORES = [8]
_DBG = [False]
_LAST = [None]


def kernel(x, positions, w_in, conv_w, conv_b, cln_g, cln_b, pw_w, pw_b, gate_w2, gate_b, gnorm_g, w_out, ln_g, ln_b):
    x = np.asarray(x)
    B, T, _ = x.shape
    DEPTH = int(np.asarray(w_in).shape[0])
    key = (T, DEPTH)
    if key not in _CACHE:
        _CACHE[key] = build_program(T, DEPTH, dbg=_DBG[0])
    nc = _CACHE[key]
    consts = make_consts()
    shared = {"w_in": w_in, "conv_w": conv_w, "conv_b": conv_b, "cln_g": cln_g, "cln_b": cln_b, "pw_w": pw_w, "pw_b": pw_b,
              "gate_w2": gate_w2, "gate_b": gate_b, "gnorm_g": gnorm_g, "w_out": w_out, "ln_g": ln_g, "ln_b": ln_b}
    shared = {k: np.asarray(v, dtype=np.float32) for k, v in shared.items()}
    shared["conv_w"] = shared["conv_w"].reshape(DEPTH, CONVW, 2, 128).transpose(0, 2, 3, 1)
    for nm in ("conv_b", "cln_g", "cln_b", "pw_b"):
        shared[nm] = shared[nm].reshape(DEPTH, 2, 128).transpose(0, 2, 1)
    for nm in ("gnorm_g", "ln_g", "ln_b"):
        shared[nm] = np.broadcast_to(shared[nm][:, None, :], (DEPTH, 128, shared[nm].shape[-1]))
    shared = {k: np.ascontiguousarray(v) for k, v in shared.items()}
    n_cores = _NCORES[0]
    in_maps = []
    for c in range(n_cores):
        b = c % B
        m = {"x": np.ascontiguousarray(x[b]), "positions": np.ascontiguousarray(np.asarray(positions)[b].astype(np.int32).reshape(T // 128, 128).T)}
        m.update(shared)
        m.update(consts)
        in_maps.append(m)
    res = run_bass_kernel_spmd(nc, in_maps, core_ids=list(range(n_cores)))
    _LAST[0] = res
    return np.stack([np.asarray(res.results[b]["out"]) for b in range(min(B, n_cores))], axis=0).astype(np.float32)
```

```python
import math
import os
from contextlib import ExitStack
import numpy as np
import concourse.bass as bass
import concourse.mybir as mybir
from concourse.bass_utils import run_bass_kernel_spmd

F32 = mybir.dt.float32
BF16 = mybir.dt.bfloat16
I32 = mybir.dt.int32
ALU = mybir.AluOpType
AF = mybir.ActivationFunctionType
AX = mybir.AxisListType

D_MODEL = 1024
D_IN = 4184
TOPK_MAX = 256
CONVW = 31
EPS = 1e-5
NEG = -30000.0
NBIS = 16


class Buf:
    __slots__ = ("writers", "readers", "old", "name", "excl")

    def __init__(self, name="", excl=False):
        self.excl = excl
        self.writers = []
        self.readers = []
        self.old = []
        self.name = name


class DSem:
    def __init__(self, sem):
        self.sem = sem
        self.count = 0


class Eng:
    def __init__(self, name, sem):
        self.name = name
        self.sem = sem
        self.count = 0
        self.known = {}
        self.prog = []


class _Rec:
    def __init__(self):
        self.calls = []

    def __getattr__(self, name):
        def f(*a, **k):
            self.calls.append((name, a, k))
            return self
        return f


class Sched:
    def __init__(self, nc, es):
        self.nc = nc
        self.es = es
        self.engs = {}
        for n in ("pe", "act", "dve", "pool", "sp"):
            self.engs[n] = Eng(n, es.enter_context(nc.semaphore("s_" + n)))
        self.dsems = {}
        self.nops = 0

    def dsem(self, key):
        if key not in self.dsems:
            self.dsems[key] = DSem(self.es.enter_context(self.nc.semaphore("d_%d" % len(self.dsems))))
        return self.dsems[key]

    def _compress(self, lst):
        if len(lst) > 6:
            best = {}
            for (s, v) in lst:
                k = id(s)
                if k not in best or best[k][1] < v:
                    best[k] = (s, v)
            lst[:] = list(best.values())

    def op(self, eng, fn, reads=(), writes=(), appends=(), dsem=None, ndma=1):
        e = self.engs[eng]
        deps = {}

        def need(t):
            k = id(t[0])
            if k not in deps or deps[k][1] < t[1]:
                deps[k] = t

        for b in reads:
            for t in b.writers:
                need(t)
            if b.excl:
                for t in b.readers:
                    if t[0] is not e.sem:
                        need(t)
        for b in writes:
            for t in b.writers:
                need(t)
            for t in b.readers:
                need(t)
        for b in appends:
            for t in b.readers:
                need(t)
            for t in b.old:
                need(t)
        waits = []
        for k, (s, v) in deps.items():
            if eng == "pe" and s is e.sem:
                continue
            if e.known.get(k, 0) < v:
                waits.append((s, v))
                e.known[k] = v
        if dsem is None:
            e.count += 1
            t = (e.sem, e.count)
            inc = (e.sem, 1)
        else:
            dsem.count += 16 * ndma
            t = (dsem.sem, dsem.count)
            inc = (dsem.sem, 16)
        rec = _Rec()
        fn(rec)
        e.prog.append((waits, rec.calls, inc))
        self.nops += 1
        for b in reads:
            b.readers.append(t)
            self._compress(b.readers)
        for b in writes:
            b.old = b.writers + b.readers
            self._compress(b.old)
            b.writers = [t]
            b.readers = []
        for b in appends:
            b.writers.append(t)
            self._compress(b.writers)
        return t

    def barrier(self):
        for e in self.engs.values():
            waits = []
            for f in self.engs.values():
                if f is e or f.count == 0:
                    continue
                if e.known.get(id(f.sem), 0) < f.count:
                    waits.append((f.sem, f.count))
                    e.known[id(f.sem)] = f.count
            for d in self.dsems.values():
                if d.count and e.known.get(id(d.sem), 0) < d.count:
                    waits.append((d.sem, d.count))
                    e.known[id(d.sem)] = d.count
            if waits:
                e.prog.append((waits, None, None))

    def emit(self):
        nc = self.nc
        hmap = {"pe": "tensor", "act": "scalar", "dve": "vector", "pool": "gpsimd", "sp": "sync"}
        with nc.Block() as block:
            for n, e in self.engs.items():
                def body(h, e=e):
                    for waits, fn, inc in e.prog:
                        for (s, v) in waits:
                            h.wait_ge(s, v)
                        if fn is None:
                            continue
                        for (name, a, k) in fn:
                            getattr(h, name)(*a, **k).then_inc(inc[0], inc[1])
                getattr(block, hmap[n])(body)


class Ring:
    def __init__(self, items):
        self.items = items
        self.i = 0

    def next(self):
        it = self.items[self.i % len(self.items)]
        self.i += 1
        return it


def build_program(T, DEPTH, dbg=False):
    NT = T // 128
    topk = min(TOPK_MAX, T // 4)
    alpha = (2 * DEPTH) ** 0.25
    nc = bass.Bass("TRN2", target_bir_lowering=False)
    es = ExitStack()
    S = Sched(nc, es)

    def din(name, shape, dt=F32):
        return nc.dram_tensor(name, list(shape), dt, kind="ExternalInput").ap()

    def dscr(name, shape, dt):
        return nc.dram_tensor(name, list(shape), dt, kind="ExternalOutput" if dbg else "Internal").ap()

    x_in = din("x", [T, D_MODEL])
    pos_in = din("positions", [128, T // 128], I32)
    w_in = din("w_in", [DEPTH, D_MODEL, D_IN])
    conv_w = din("conv_w", [DEPTH, 2, 128, CONVW])
    conv_b = din("conv_b", [DEPTH, 128, 2])
    cln_g = din("cln_g", [DEPTH, 128, 2])
    cln_b = din("cln_b", [DEPTH, 128, 2])
    pw_w = din("pw_w", [DEPTH, 256, 256])
    pw_b = din("pw_b", [DEPTH, 128, 2])
    gate_w2 = din("gate_w2", [DEPTH, 16, 128])
    gate_b = din("gate_b", [DEPTH, 128])
    gnorm_g = din("gnorm_g", [DEPTH, 128, 256])
    w_out = din("w_out", [DEPTH, 1024, 1024])
    ln_g = din("ln_g", [DEPTH, 128, 1024])
    ln_b = din("ln_b", [DEPTH, 128, 1024])
    c_ident = din("c_ident", [128, 128])
    c_tri = din("c_tri", [128, 128])
    c_cbias = din("c_cbias", [128, 128])
    c_hmask = din("c_hmask", [128, 4])
    c_inv8 = din("c_inv8", [128, 256])
    c_pow2 = din("c_pow2", [128, 32])
    out_d = nc.dram_tensor("out", [T, D_MODEL], F32, kind="ExternalOutput").ap()

    xcur = dscr("xcur", [T, D_MODEL], F32)
    xT_d = dscr("xT", [D_MODEL, T], BF16)
    cs_d = dscr("cs", [T, 512], F32)
    qT_d = dscr("qT", [4, 128, T], BF16)
    kT_d = dscr("kT", [4, 128, T], BF16)
    qiT_d = dscr("qiT", [4, 128, T], BF16)
    kiT_d = dscr("kiT", [64, T], BF16)
    wi_d = dscr("wi", [T, 8], F32)
    V_d = dscr("V", [T, 520], BF16)
    ag_d = dscr("ag", [T, 512], F32)
    ybc_d = dscr("ybcT", [4, 128, T], BF16)

    dbufs = {}

    def DB(name, i):
        k = (name, i)
        if k not in dbufs:
            dbufs[k] = Buf("%s%d" % (name, i))
        return dbufs[k]

    def sb(name, shape, dt):
        return es.enter_context(nc.sbuf_tensor(name, list(shape), dt))

    def ps(name, shape, dt):
        return es.enter_context(nc.psum_tensor(name, list(shape), dt))

    def slot(name, shape, dt, n=2):
        items = []
        for i in range(n):
            t = sb("%s_%d" % (name, i), shape, dt)
            items.append((t, Buf("%s_%d" % (name, i)), S.dsem("%s_%d" % (name, i))))
        return Ring(items)

    def dma(eng, out_ap, in_ap, reads, writes, dsem, appends=()):
        return S.op(eng, lambda h: h.dma_start(out=out_ap, in_=in_ap), reads=reads, writes=writes,
                    appends=appends, dsem=dsem)

    def dma_multi(eng, pairs, reads, writes, dsem):
        return S.op(eng, lambda h: [h.dma_start(out=o, in_=i) for (o, i) in pairs], reads=reads, writes=writes,
                    dsem=dsem, ndma=len(pairs))

    identf = sb("identf", [128, 128], F32)
    identb = sb("identb", [128, 128], BF16)
    trif = sb("trif", [128, 128], F32)
    onesf = sb("onesf", [128, 128], F32)
    cbias = sb("cbias", [128, 128], F32)
    hmask = sb("hmask", [128, 4], F32)
    cmaskb = sb("cmaskb", [128, 4, 128], F32)
    inv8 = sb("inv8", [128, 256], F32)
    pow2 = sb("pow2", [128, 32], F32)
    B_const = Buf("const")
    dsc = S.dsem("const")
    dma("sp", identf[:], c_ident, [], [], dsc, appends=[B_const])
    dma("sp", trif[:], c_tri, [], [], dsc, appends=[B_const])
    dma("sp", cbias[:], c_cbias, [], [], dsc, appends=[B_const])
    dma("sp", hmask[:], c_hmask, [], [], dsc, appends=[B_const])
    dma("sp", inv8[:], c_inv8, [], [], dsc, appends=[B_const])
    dma("sp", pow2[:], c_pow2, [], [], dsc, appends=[B_const])
    B_c2 = Buf("const2")
    S.op("dve", lambda h: h.tensor_copy(out=identb[:], in_=identf[:]), reads=[B_const], appends=[B_c2])
    S.op("dve", lambda h: h.memset(onesf[:], 1.0), appends=[B_c2])
    for hh in range(4):
        S.op("dve", lambda h, hh=hh: h.tensor_copy(out=cmaskb[:, hh, :], in_=trif[:]), reads=[B_const], appends=[B_c2])
    CONSTS = [B_const, B_c2]

    xb_ring = slot("xb", [128, 1024], BF16, 2)
    xts_ring = slot("xts", [128, 8, 128], BF16, 2)
    pT_ring = Ring([(ps("pT%d" % i, [128, 1024], BF16), Buf("pT%d" % i, True)) for i in range(2)])
    pA_ring = Ring([(ps("pA%d" % i, [128, 512], F32), Buf("pA%d" % i, True)) for i in range(4)])
    pS_ring = Ring([(ps("pS%d" % i, [128, 512], F32), Buf("pS%d" % i, True)) for i in range(2)])

    xT_v = xT_d.rearrange("(kc p) t -> p kc t", p=128)

    def emit_xT(blk, x_sb, x_buf):
        xb, xbB, _ = xb_ring.next()
        S.op("act", lambda h: h.activation(out=xb[:], in_=x_sb, func=AF.Copy), reads=[x_buf], writes=[xbB])
        pT, pTB = pT_ring.next()
        for kc in range(8):
            S.op("pe", lambda h, kc=kc: h.transpose(out=pT[:, kc * 128:(kc + 1) * 128], in_=xb[:, kc * 128:(kc + 1) * 128],
                                                    identity=identb[:]),
                 reads=[xbB] + CONSTS, writes=[pTB] if kc == 0 else [], appends=[pTB] if kc else [])
        xts, xtsB, xtsD = xts_ring.next()
        S.op("dve", lambda h: h.tensor_copy(out=xts[:].rearrange("p a b -> p (a b)"), in_=pT[:]), reads=[pTB], writes=[xtsB])
        dma("sp", xT_v[:, :, blk * 128:(blk + 1) * 128], xts[:], [xtsB], [DB("xT", blk)], xtsD)

    TWO_PI = 2.0 * math.pi
    MAGIC = 12582912.0
    posi = sb("posi", [128, NT], I32)
    posf = sb("posf", [128, NT], F32)
    B_pos = Buf("pos")
    dma("sp", posi[:], pos_in, [], [B_pos], S.dsem("pos"))
    S.op("dve", lambda h: h.tensor_copy(out=posf[:], in_=posi[:]), reads=[B_pos], writes=[B_pos])
    pp = ExitStack()

    def sbp(name, shape, dt):
        return pp.enter_context(nc.sbuf_tensor(name, list(shape), dt))

    def slotp(name, shape, dt, n=2):
        return Ring([(sbp("%s_%d" % (name, i), shape, dt), Buf("%s_%d" % (name, i)), S.dsem("%s_%d" % (name, i))) for i in range(n)])

    cs_ring = slotp("cs", [128, 512], F32, 2)
    ang_ring = Ring([(sbp("ang%d" % i, [128, 256], F32), Buf("ang%d" % i)) for i in range(2)])
    rr_ring = Ring([(sbp("rr%d" % i, [128, 256], F32), Buf("rr%d" % i)) for i in range(4)])
    xl_ring = slotp("xl", [128, 1024], F32, 2)
    for blk in range(NT):
        ang, angB = ang_ring.next()
        S.op("dve", lambda h, blk=blk, ang=ang: h.tensor_scalar(out=ang[:], in0=inv8[:], scalar1=posf[:, blk:blk + 1], scalar2=None,
                                                       op0=ALU.mult), reads=[B_pos] + CONSTS, writes=[angB])
        cst, cstB, cstD = cs_ring.next()
        for which, off in ((0, 0.25), (1, 0.0)):
            yo, yoB = rr_ring.next()
            rr, rrB = rr_ring.next()
            S.op("dve", lambda h, off=off, yo=yo: h.tensor_scalar(out=yo[:], in0=ang[:], scalar1=off, scalar2=None, op0=ALU.add), reads=[angB], writes=[yoB])
            S.op("dve", lambda h, yo=yo, rr=rr: h.tensor_scalar(out=rr[:], in0=yo[:], scalar1=MAGIC, scalar2=MAGIC,
                                                                op0=ALU.add, op1=ALU.subtract), reads=[yoB], writes=[rrB])
            S.op("dve", lambda h, yo=yo, rr=rr: h.tensor_tensor(out=rr[:], in0=yo[:], in1=rr[:], op=ALU.subtract), reads=[yoB, rrB], writes=[rrB])
            S.op("act", lambda h, which=which, rr=rr, cst=cst: h.activation(out=cst[:, which * 256:(which + 1) * 256], in_=rr[:], func=AF.Sin,
                                                            scale=TWO_PI), reads=[rrB],
                 writes=[cstB] if which == 0 else [], appends=[cstB] if which else [])
        dma("sp", cs_d[blk * 128:(blk + 1) * 128, :], cst[:], [cstB], [DB("cs", blk)], cstD)
        xl, xlB, xlD = xl_ring.next()
        dma("sp", xl[:], x_in[blk * 128:(blk + 1) * 128, :], [], [xlB], xlD)
        emit_xT(blk, xl[:], xlB)

    S.barrier()
    pp.close()
    convw_sb = sb("convw_sb", [128, DEPTH, 2, CONVW], F32)
    convb_sb = sb("convb_sb", [128, DEPTH, 2], F32)
    clng_sb = sb("clng_sb", [128, DEPTH, 2], F32)
    clnb_sb = sb("clnb_sb", [128, DEPTH, 2], F32)
    pwb_sb = sb("pwb_sb", [128, DEPTH, 2], F32)
    B_par = Buf("par")
    dpar = S.dsem("par")
    for l in range(DEPTH):
        for ct in range(2):
            dma("sp", convw_sb[:, l, ct, :], conv_w[l, ct], [], [], dpar, appends=[B_par])
        dma("sp", convb_sb[:, l, :], conv_b[l], [], [], dpar, appends=[B_par])
        dma("sp", clng_sb[:, l, :], cln_g[l], [], [], dpar, appends=[B_par])
        dma("sp", clnb_sb[:, l, :], cln_b[l], [], [], dpar, appends=[B_par])
        dma("sp", pwb_sb[:, l, :], pw_b[l], [], [], dpar, appends=[B_par])

    STOP = os.environ.get('KSTOP', '')
    SL = int(os.environ.get('KSL', '99'))
    SUB = int(os.environ.get('KSUB', '99'))
    G = int(os.environ.get('KG', '99'))
    for l in range(DEPTH if STOP != 'pro' else 0):
        x_src = x_in if l == 0 else xcur
        x_dst = out_d if l == DEPTH - 1 else xcur
        S.barrier()
        with ExitStack() as pa:
            def sba(name, shape, dt):
                return pa.enter_context(nc.sbuf_tensor("%s_a%d" % (name, l), list(shape), dt))

            def slota(name, shape, dt, n=2):
                items = []
                for i in range(n):
                    t = sba("%s_%d" % (name, i), shape, dt)
                    items.append((t, Buf("%s_%d" % (name, i)), S.dsem("a%s_%d" % (name, i))))
                return Ring(items)

            wbf = sba("wbf", [128, 8, D_IN], BF16)
            B_w = Buf("wbf")
            wst_ring = slota("wst", [128, 8, 256], F32, 2)
            w_v = w_in[l].rearrange("(kc p) n -> p kc n", p=128)
            c0 = 0
            ci = 0
            while c0 < D_IN:
                cw = min(256, D_IN - c0)
                wst, wstB, wstD = wst_ring.next()
                dma("sp", wst[:, :, 0:cw], w_v[:, :, c0:c0 + cw], [], [wstB], wstD)
                eng = "act" if ci % 2 == 0 else "dve"
                if eng == "act":
                    S.op("act", lambda h, c0=c0, cw=cw, wst=wst: h.activation(out=wbf[:, :, c0:c0 + cw], in_=wst[:, :, 0:cw], func=AF.Copy),
                         reads=[wstB], appends=[B_w])
                else:
                    S.op("dve", lambda h, c0=c0, cw=cw, wst=wst: h.tensor_copy(out=wbf[:, :, c0:c0 + cw], in_=wst[:, :, 0:cw]),
                         reads=[wstB], appends=[B_w])
                c0 += cw
                ci += 1
            gw2f = sba("gw2f", [16, 128], F32)
            gw2b = sba("gw2b", [16, 128], BF16)
            gbf = sba("gbf", [1, 128], F32)
            gbb = sba("gbb", [1, 128], BF16)
            onesb = sba("onesb", [1, 128], BF16)
            gng = sba("gng", [128, 256], F32)
            pwf = sba("pwf", [128, 2, 256], F32)
            pwb16 = sba("pwb16", [128, 2, 256], BF16)
            diag = sba("diag", [128, 2, CONVW, 128], BF16)
            onesm = sba("onesm", [128, 128], BF16)
            B_lw = Buf("lw")
            dlw = S.dsem("lw")
            dma("sp", gw2f[:], gate_w2[l], [], [], dlw, appends=[B_lw])
            dma("sp", gbf[:], gate_b[l:l + 1, :], [], [], dlw, appends=[B_lw])
            dma("sp", gng[:], gnorm_g[l], [], [], dlw, appends=[B_lw])
            dma("sp", pwf[:], pw_w[l].rearrange("(ct c) n -> c ct n", c=128), [], [], dlw, appends=[B_lw])
            B_lw2 = Buf("lw2")
            S.op("dve", lambda h: h.tensor_copy(out=gw2b[:], in_=gw2f[:]), reads=[B_lw], appends=[B_lw2])
            S.op("dve", lambda h: h.tensor_copy(out=gbb[:], in_=gbf[:]), reads=[B_lw], appends=[B_lw2])
            S.op("dve", lambda h: h.memset(onesb[:], 1.0), appends=[B_lw2])
            S.op("dve", lambda h: h.memset(onesm[:], 1.0 / 256.0), appends=[B_lw2])
            S.op("dve", lambda h: h.tensor_copy(out=pwb16[:].rearrange("p a b -> p (a b)"), in_=pwf[:].rearrange("p a b -> p (a b)")),
                 reads=[B_lw], appends=[B_lw2])
            for ct in range(2):
                for j in range(CONVW):
                    S.op("pool" if (j % 2) else "dve",
                         lambda h, ct=ct, j=j: h.tensor_scalar(out=diag[:, ct, j, :], in0=identf[:], scalar1=convw_sb[:, l, ct, j:j + 1],
                                                               scalar2=None, op0=ALU.mult),
                         reads=[B_par] + CONSTS, appends=[B_lw2])
            LW = [B_lw, B_lw2, B_par] + CONSTS

            hT = sba("hT", [128, 2, T + 32], BF16)
            B_h0 = Buf("h0")
            S.op("pool", lambda h: h.memset(hT[:, :, 0:32], 0.0), writes=[B_h0])
            hB = [Buf("hT%d" % i) for i in range(NT)]
            Sst = sba("Sst", [128, 64], F32)
            Sbf = sba("Sbf", [128, 64], BF16)
            B_S = Buf("S")
            B_Sb = Buf("Sb")
            S.op("dve", lambda h: h.memset(Sst[:], 0.0), writes=[B_S])
            S.op("dve", lambda h: h.memset(Sbf[:], 0.0), writes=[B_Sb])
            Kbd_ring = Ring([(sba("Kbd%d" % i, [128, 4, 128], BF16), Buf("Kbd%d" % i)) for i in range(2)])
            for (kb_t, kb_B) in Kbd_ring.items:
                S.op("pool", lambda h, kb_t=kb_t: h.memset(kb_t[:].rearrange("p a b -> p (a b)"), 0.0), writes=[kb_B])

            xt_ring = slota("xt", [128, 8, 128], BF16, 2)
            csl_ring = slota("csl", [128, 512], F32, 2)
            ev_ring = Ring([(sba("ev%d" % i, [128, 512], F32), Buf("ev%d" % i)) for i in range(3)])
            t_ring = Ring([(sba("tt%d" % i, [128, 256], F32), Buf("tt%d" % i)) for i in range(4)])
            rb_ring = Ring([(sba("rb%d" % i, [128, 512], BF16), Buf("rb%d" % i)) for i in range(2)])
            qts_ring = slota("qts", [128, 4, 128], BF16, 3)
            kis_ring = slota("kis", [64, 128], BF16, 2)
            wis_ring = slota("wis", [128, 8], F32, 2)
            vs_ring = slota("vs", [128, 8, 65], BF16, 2)
            ags_ring = slota("ags", [128, 512], F32, 2)
            ybc_ring = slota("ybc", [128, 4, 128], BF16, 2)
            fm_ring = Ring([(sba("fm%d" % i, [128, 128], F32), Buf("fm%d" % i)) for i in range(6)])
            sm_ring = Ring([(sba("sm%d" % i, [128, 256], F32), Buf("sm%d" % i)) for i in range(6)])
            smb_ring = Ring([(sba("smb%d" % i, [128, 512], BF16), Buf("smb%d" % i)) for i in range(6)])
            col_ring = Ring([(sba("col%d" % i, [128, 8], F32), Buf("col%d" % i)) for i in range(6)])
            clr_ring = Ring([(sba("clr%d" % i, [16, 128], BF16), Buf("clr%d" % i)) for i in range(2)])

            def proj_tm(xt, xtB, c0, cw):
                pA, pAB = pA_ring.next()
                for kc in range(8):
                    S.op("pe", lambda h, kc=kc, pA=pA: h.matmul(pA[:, 0:cw], lhsT=xt[:, kc, :], rhs=wbf[:, kc, c0:c0 + cw],
                                                                start=(kc == 0), stop=(kc == 7)),
                         reads=[xtB, B_w], writes=[pAB] if kc == 0 else [], appends=[pAB] if kc else [])
                return pA, pAB

            def proj_fm(xt, xtB, c0, cw):
                pA, pAB = pA_ring.next()
                for kc in range(8):
                    S.op("pe", lambda h, kc=kc, pA=pA: h.matmul(pA[0:cw, 0:128], lhsT=wbf[:, kc, c0:c0 + cw], rhs=xt[:, kc, :],
                                                                start=(kc == 0), stop=(kc == 7)),
                         reads=[xtB, B_w], writes=[pAB] if kc == 0 else [], appends=[pAB] if kc else [])
                return pA, pAB

            def rope_tm(src, srcB, csl, cslB, nh, dst, dstB, first=True):
                sv = src.rearrange("p (h d) -> p h d", d=64)
                dv = dst.rearrange("p (h d) -> p h d", d=64)
                C = csl[:, 0:nh * 32].rearrange("p (h d) -> p h d", d=32)
                Sn = csl[:, 256:256 + nh * 32].rearrange("p (h d) -> p h d", d=32)
                t1, t1B = t_ring.next()
                t2, t2B = t_ring.next()
                a1 = t1[:, 0:nh * 32].rearrange("p (h d) -> p h d", d=32)
                a2 = t2[:, 0:nh * 32].rearrange("p (h d) -> p h d", d=32)
                S.op("dve", lambda h: h.tensor_tensor(out=a1, in0=sv[:, :, 0:32], in1=C, op=ALU.mult), reads=[srcB, cslB], writes=[t1B])
                S.op("pool", lambda h: h.tensor_tensor(out=a2, in0=sv[:, :, 32:64], in1=Sn, op=ALU.mult), reads=[srcB, cslB], writes=[t2B])
                S.op("dve", lambda h: h.tensor_tensor(out=dv[:, :, 0:32], in0=a1, in1=a2, op=ALU.subtract), reads=[t1B, t2B],
                     writes=[dstB] if first else [], appends=[] if first else [dstB])
                t3, t3B = t_ring.next()
                t4, t4B = t_ring.next()
                a3 = t3[:, 0:nh * 32].rearrange("p (h d) -> p h d", d=32)
                a4 = t4[:, 0:nh * 32].rearrange("p (h d) -> p h d", d=32)
                S.op("dve", lambda h: h.tensor_tensor(out=a3, in0=sv[:, :, 32:64], in1=C, op=ALU.mult), reads=[srcB, cslB], writes=[t3B])
                S.op("pool", lambda h: h.tensor_tensor(out=a4, in0=sv[:, :, 0:32], in1=Sn, op=ALU.mult), reads=[srcB, cslB], writes=[t4B])
                S.op("dve", lambda h: h.tensor_tensor(out=dv[:, :, 32:64], in0=a3, in1=a4, op=ALU.add), reads=[t3B, t4B], appends=[dstB])

            def evac(pA, pAB, cw, rows=128):
                ev, evB = ev_ring.next()
                S.op("act", lambda h: h.activation(out=ev[0:rows, 0:cw], in_=pA[0:rows, 0:cw], func=AF.Copy), reads=[pAB], writes=[evB])
                return ev, evB

            def tr_to(dst3, dstB, srcb, srcB, ntile, first=True):
                pT, pTB = pT_ring.next()
                for i in range(ntile):
                    S.op("pe", lambda h, i=i: h.transpose(out=pT[:, i * 128:(i + 1) * 128], in_=srcb[:, i * 128:(i + 1) * 128], identity=identb[:]),
                         reads=[srcB] + CONSTS, writes=[pTB] if i == 0 else [], appends=[pTB] if i else [])
                S.op("dve", lambda h: h.tensor_copy(out=dst3, in_=pT[:, 0:ntile * 128]), reads=[pTB],
                     writes=[dstB] if first else [], appends=[] if first else [dstB])

            for blk in range(NT if SL >= 1 else 0):
                t0 = blk * 128
                xt, xtB, xtD = xt_ring.next()
                dma("sp", xt[:], xT_v[:, :, t0:t0 + 128], [DB("xT", blk)], [xtB], xtD)
                csl, cslB, cslD = csl_ring.next()
                dma("sp", csl[:], cs_d[t0:t0 + 128, :], [DB("cs", blk)], [cslB], cslD)

                for (c0, dst_d, nm) in ((0, qT_d, "qT"), (512, kT_d, "kT"), (2048, qiT_d, "qiT")):
                    pA, pAB = proj_tm(xt, xtB, c0, 512)
                    ev, evB = evac(pA, pAB, 512)
                    rb, rbB = rb_ring.next()
                    rope_tm(ev[:, 0:512], evB, csl, cslB, 8, rb[:, 0:512], rbB)
                    qts, qtsB, qtsD = qts_ring.next()
                    tr_to(qts[:].rearrange("p a b -> p (a b)"), qtsB, rb, rbB, 4)
                    dma("sp", dst_d[:, :, t0:t0 + 128].rearrange("a p t -> p a t"), qts[:], [qtsB], [DB(nm, blk)], qtsD)
                if SL < 2:
                    continue
                pA, pAB = proj_tm(xt, xtB, 2560, 72)
                ev, evB = evac(pA, pAB, 72)
                rb, rbB = rb_ring.next()
                rope_tm(ev[:, 0:64], evB, csl, cslB, 1, rb[:, 0:64], rbB)
                pT, pTB = pT_ring.next()
                S.op("pe", lambda h: h.transpose(out=pT[0:64, 0:128], in_=rb[:, 0:64], identity=identb[:]), reads=[rbB] + CONSTS, writes=[pTB])
                kis, kisB, kisD = kis_ring.next()
                S.op("dve", lambda h: h.tensor_copy(out=kis[:], in_=pT[0:64, 0:128]), reads=[pTB], writes=[kisB])
                dma("sp", kiT_d[:, t0:t0 + 128], kis[:], [kisB], [DB("kiT", blk)], kisD)
                wis, wisB, wisD = wis_ring.next()
                S.op("dve", lambda h: h.tensor_scalar(out=wis[:], in0=ev[:, 64:72], scalar1=(8 ** -0.5) * (64 ** -0.5), scalar2=None, op0=ALU.mult),
                     reads=[evB], writes=[wisB])
                dma("sp", wi_d[t0:t0 + 128, :], wis[:], [wisB], [DB("wi", blk)], wisD)
                if SL < 3:
                    continue
                pA, pAB = proj_tm(xt, xtB, 1024, 512)
                vs, vsB, vsD = vs_ring.next()
                S.op("act", lambda h: h.activation(out=vs[:, :, 0:64], in_=pA[:, 0:512].rearrange("p (h d) -> p h d", d=64), func=AF.Copy),
                     reads=[pAB], writes=[vsB])
                S.op("pool", lambda h: h.memset(vs[:, :, 64:65], 1.0), appends=[vsB])
                dma("sp", V_d[t0:t0 + 128, :], vs[:].rearrange("p a b -> p (a b)"), [vsB], [DB("V", blk)], vsD)
                pA, pAB = proj_tm(xt, xtB, 1536, 512)
                ags, agsB, agsD = ags_ring.next()
                S.op("act", lambda h: h.activation(out=ags[:], in_=pA[:, 0:512], func=AF.Silu), reads=[pAB], writes=[agsB])
                dma("sp", ag_d[t0:t0 + 128, :], ags[:], [agsB], [DB("ag", blk)], agsD)

                if SL < 4:
                    continue
                ybc, ybcB, ybcD = ybc_ring.next()
                for ct in range(2):
                    pv_, pvB = proj_fm(xt, xtB, 2632 + ct * 128, 128)
                    pg_, pgB = proj_fm(xt, xtB, 2888 + ct * 128, 128)
                    sg, sgB = fm_ring.next()
                    S.op("act", lambda h, sg=sg, pg_=pg_: h.activation(out=sg[:], in_=pg_[:, 0:128], func=AF.Sigmoid), reads=[pgB], writes=[sgB])
                    S.op("dve", lambda h, sg=sg, pv_=pv_, ct=ct: h.tensor_tensor(out=hT[:, ct, 32 + t0:32 + t0 + 128], in0=pv_[:, 0:128], in1=sg[:], op=ALU.mult),
                         reads=[pvB, sgB], writes=[hB[blk]] if ct == 0 else [], appends=[hB[blk]] if ct else [])
                if SUB < 1:
                    continue
                hdeps = [hB[blk]] + ([hB[blk - 1]] if blk else [B_h0])
                cv = []
                for ct in range(2):
                    pC, pCB = pA_ring.next()
                    for j in range(CONVW):
                        S.op("pe", lambda h, ct=ct, j=j, pC=pC: h.matmul(pC[:, 0:128], lhsT=diag[:, ct, j, :],
                                                                         rhs=hT[:, ct, 32 + t0 - 30 + j:32 + t0 - 30 + j + 128],
                                                                         start=(j == 0), stop=(j == CONVW - 1)),
                             reads=hdeps + LW, writes=[pCB] if j == 0 else [], appends=[pCB] if j else [])
                    cvt, cvB = fm_ring.next()
                    S.op("act", lambda h, cvt=cvt, pC=pC, ct=ct: h.activation(out=cvt[:], in_=pC[:, 0:128], func=AF.Identity,
                                                                             bias=convb_sb[:, l, ct:ct + 1]), reads=[pCB] + LW, writes=[cvB])
                    cv.append((cvt, cvB))
                if SUB < 2:
                    continue
                cb16, cb16B = smb_ring.next()
                sq16, sq16B = smb_ring.next()
                for ct in range(2):
                    cvt, cvB = cv[ct]
                    S.op("dve", lambda h, cvt=cvt, ct=ct: h.tensor_copy(out=cb16[:, ct * 256:ct * 256 + 128], in_=cvt[:]), reads=[cvB],
                         writes=[cb16B] if ct == 0 else [], appends=[cb16B] if ct else [])
                    S.op("dve", lambda h, cvt=cvt, ct=ct: h.tensor_tensor(out=cb16[:, ct * 256 + 128:ct * 256 + 256], in0=cvt[:],
                                                                         in1=cb16[:, ct * 256:ct * 256 + 128], op=ALU.subtract),
                         reads=[cvB, cb16B], appends=[cb16B])
                    sqf, sqfB = fm_ring.next()
                    S.op("pool", lambda h, cvt=cvt, sqf=sqf: h.tensor_tensor(out=sqf[:], in0=cvt[:], in1=cvt[:], op=ALU.mult), reads=[cvB], writes=[sqfB])
                    S.op("dve", lambda h, sqf=sqf, ct=ct: h.tensor_copy(out=sq16[:, ct * 256:ct * 256 + 128], in_=sqf[:]), reads=[sqfB],
                         writes=[sq16B] if ct == 0 else [], appends=[sq16B] if ct else [])
                    S.op("dve", lambda h, sqf=sqf, ct=ct: h.tensor_tensor(out=sq16[:, ct * 256 + 128:ct * 256 + 256], in0=sqf[:],
                                                                         in1=sq16[:, ct * 256:ct * 256 + 128], op=ALU.subtract),
                         reads=[sqfB, sq16B], appends=[sq16B])
                pM, pMB = pS_ring.next()
                n = 0
                for ct in range(2):
                    for part in range(2):
                        S.op("pe", lambda h, ct=ct, part=part, n=n: h.matmul(pM[:, 0:128], lhsT=onesm[:], rhs=cb16[:, ct * 256 + part * 128:ct * 256 + part * 128 + 128],
                                                                             start=(n == 0), stop=(n == 3)),
                             reads=[cb16B, B_lw2], writes=[pMB] if n == 0 else [], appends=[pMB] if n else [])
                        n += 1
                n = 0
                for ct in range(2):
                    for part in range(2):
                        S.op("pe", lambda h, ct=ct, part=part, n=n: h.matmul(pM[:, 128:256], lhsT=onesm[:], rhs=sq16[:, ct * 256 + part * 128:ct * 256 + part * 128 + 128],
                                                                             start=(n == 0), stop=(n == 3)),
                             reads=[sq16B, B_lw2], appends=[pMB])
                        n += 1
                if SUB < 3:
                    continue
                st, stB = sm_ring.next()
                S.op("act", lambda h: h.activation(out=st[:, 0:256], in_=pM[:, 0:256], func=AF.Copy), reads=[pMB], writes=[stB])
                m2, m2B = fm_ring.next()
                S.op("dve", lambda h: h.tensor_tensor(out=m2[:], in0=st[:, 0:128], in1=st[:, 0:128], op=ALU.mult), reads=[stB], writes=[m2B])
                S.op("dve", lambda h: h.tensor_tensor(out=m2[:], in0=st[:, 128:256], in1=m2[:], op=ALU.subtract), reads=[stB, m2B], writes=[m2B])
                S.op("dve", lambda h: h.tensor_scalar(out=m2[:], in0=m2[:], scalar1=EPS, scalar2=None, op0=ALU.add), reads=[m2B], writes=[m2B])
                S.op("act", lambda h: h.activation(out=m2[:], in_=m2[:], func=AF.Sqrt), reads=[m2B], writes=[m2B])
                S.op("dve", lambda h: h.reciprocal(out=st[:, 128:256], in_=m2[:]), reads=[m2B, stB], writes=[stB])
                hs16, hs16B = smb_ring.next()
                for ct in range(2):
                    cvt, cvB = cv[ct]
                    S.op("dve", lambda h, cvt=cvt: h.tensor_tensor(out=cvt[:], in0=cvt[:], in1=st[:, 0:128], op=ALU.subtract), reads=[cvB, stB], writes=[cvB])
                    S.op("dve", lambda h, cvt=cvt: h.tensor_tensor(out=cvt[:], in0=cvt[:], in1=st[:, 128:256], op=ALU.mult), reads=[cvB, stB], writes=[cvB])
                    S.op("act", lambda h, cvt=cvt, ct=ct: h.activation(out=hs16[:, ct * 128:(ct + 1) * 128], in_=cvt[:], func=AF.Silu,
                                                                      scale=clng_sb[:, l, ct:ct + 1], bias=clnb_sb[:, l, ct:ct + 1]),
                         reads=[cvB] + LW, writes=[hs16B] if ct == 0 else [], appends=[hs16B] if ct else [])
                if SUB < 4:
                    continue
                for co in range(2):
                    pP, pPB = pA_ring.next()
                    for ct in range(2):
                        S.op("pe", lambda h, co=co, ct=ct, pP=pP: h.matmul(pP[:, 0:128], lhsT=pwb16[:, ct, co * 128:(co + 1) * 128],
                                                                           rhs=hs16[:, ct * 128:(ct + 1) * 128], start=(ct == 0), stop=(ct == 1)),
                             reads=[hs16B] + LW, writes=[pPB] if ct == 0 else [], appends=[pPB] if ct else [])
                    pG, pGB = proj_fm(xt, xtB, 3144 + co * 128, 128)
                    sgl, sglB = fm_ring.next()
                    S.op("act", lambda h, sgl=sgl, pG=pG: h.activation(out=sgl[:], in_=pG[:, 0:128], func=AF.Silu), reads=[pGB], writes=[sglB])
                    yb, ybB_ = fm_ring.next()
                    S.op("act", lambda h, yb=yb, pP=pP, co=co: h.activation(out=yb[:], in_=pP[:, 0:128], func=AF.Identity, bias=pwb_sb[:, l, co:co + 1]),
                         reads=[pPB] + LW, writes=[ybB_])
                    S.op("dve", lambda h, yb=yb, sgl=sgl, co=co: h.tensor_tensor(out=ybc[:, co, :], in0=yb[:], in1=sgl[:], op=ALU.mult),
                         reads=[ybB_, sglB], writes=[ybcB] if co == 0 else [], appends=[ybcB] if co else [])

                if SL < 5:
                    continue
                pK, pKB = proj_tm(xt, xtB, 3528, 384)
                kv, kvB = evac(pK, pKB, 384)
                vb, vbB = smb_ring.next()
                S.op("dve", lambda h: h.tensor_copy(out=vb[:, 0:256], in_=kv[:, 128:384]), reads=[kvB], writes=[vbB])
                pQ, pQB = proj_tm(xt, xtB, 3400, 128)
                qf, qfB = fm_ring.next()
                S.op("act", lambda h: h.activation(out=qf[:], in_=pQ[:, 0:128], func=AF.Copy, scale=32 ** -0.5), reads=[pQB], writes=[qfB])
                pL, pLB = proj_fm(xt, xtB, 4168, 16)
                clr, clrB = clr_ring.next()
                S.op("act", lambda h: h.activation(out=clr[:], in_=pL[0:16, 0:128], func=AF.Copy), reads=[pLB], writes=[clrB])
                if G < 1:
                    continue
                pZ, pZB = pS_ring.next()
                S.op("pe", lambda h: h.matmul(pZ[:, 0:128], lhsT=clr[:], rhs=gw2b[:], start=True, stop=False), reads=[clrB, B_lw2], writes=[pZB])
                S.op("pe", lambda h: h.matmul(pZ[:, 0:128], lhsT=onesb[:], rhs=gbb[:], start=False, stop=True), reads=[B_lw2], appends=[pZB])
                la, laB = fm_ring.next()
                S.op("act", lambda h: h.activation(out=la[:], in_=pZ[:, 0:128], func=AF.Exp, scale=-1.0), reads=[pZB], writes=[laB])
                S.op("act", lambda h: h.activation(out=la[:], in_=la[:], func=AF.Ln, bias=1.0), reads=[laB], writes=[laB])
                S.op("dve", lambda h: h.tensor_scalar(out=la[:], in0=la[:], scalar1=-1.0 / 16.0, scalar2=None, op0=ALU.mult), reads=[laB], writes=[laB])
                if G < 2:
                    continue
                pB_, pBB = pS_ring.next()
                S.op("pe", lambda h: h.matmul(pB_[:, 0:128], lhsT=trif[:], rhs=la[:], start=True, stop=True), reads=[laB] + CONSTS, writes=[pBB])
                S.op("pe", lambda h: h.matmul(pB_[:, 128:256], lhsT=onesf[:], rhs=la[:], start=True, stop=True), reads=[laB] + CONSTS, appends=[pBB])
                S.op("pe", lambda h: h.matmul(pB_[:, 256:257], lhsT=la[:], rhs=onesf[:, 0:1], start=True, stop=True), reads=[laB] + CONSTS, appends=[pBB])
                bb, bbB = sm_ring.next()
                S.op("act", lambda h: h.activation(out=bb[:, 0:256], in_=pB_[:, 0:256], func=AF.Copy), reads=[pBB], writes=[bbB])
                dec, decB = col_ring.next()
                S.op("act", lambda h: h.activation(out=dec[:, 0:1], in_=pB_[:, 256:257], func=AF.Exp), reads=[pBB], writes=[decB])
                eb, ebB = sm_ring.next()
                S.op("act", lambda h: h.activation(out=eb[:, 0:128], in_=bb[:, 0:128], func=AF.Exp), reads=[bbB], writes=[ebB])
                S.op("act", lambda h: h.activation(out=eb[:, 128:256], in_=bb[:, 0:128], func=AF.Exp, scale=-1.0), reads=[bbB], appends=[ebB])
                el, elB = fm_ring.next()
                S.op("dve", lambda h: h.tensor_tensor(out=el[:], in0=bb[:, 128:256], in1=bb[:, 0:128], op=ALU.subtract), reads=[bbB], writes=[elB])
                S.op("act", lambda h: h.activation(out=el[:], in_=el[:], func=AF.Exp), reads=[elB], writes=[elB])
                if G < 3:
                    continue
                qk16, qk16B = smb_ring.next()
                S.op("dve", lambda h: h.tensor_tensor(out=qk16[:, 0:128], in0=qf[:], in1=eb[:, 0:128], op=ALU.mult), reads=[qfB, ebB], writes=[qk16B])
                S.op("dve", lambda h: h.tensor_tensor(out=qk16[:, 128:256], in0=kv[:, 0:128], in1=eb[:, 128:256], op=ALU.mult), reads=[kvB, ebB], appends=[qk16B])
                Kbd, KbdB = Kbd_ring.next()
                for hh in range(4):
                    S.op("dve", lambda h, hh=hh: h.tensor_tensor(out=Kbd[:, hh, hh * 32:(hh + 1) * 32], in0=kv[:, hh * 32:(hh + 1) * 32],
                                                                 in1=el[:, hh * 32:(hh + 1) * 32], op=ALU.mult),
                         reads=[kvB, elB], writes=[KbdB] if hh == 0 else [], appends=[KbdB] if hh else [])
                pT, pTB = pT_ring.next()
                S.op("pe", lambda h: h.transpose(out=pT[:, 0:128], in_=qk16[:, 0:128], identity=identb[:]), reads=[qk16B] + CONSTS, writes=[pTB])
                S.op("pe", lambda h: h.transpose(out=pT[:, 128:256], in_=qk16[:, 128:256], identity=identb[:]), reads=[qk16B] + CONSTS, appends=[pTB])
                ktT, ktTB = smb_ring.next()
                S.op("act", lambda h: h.activation(out=ktT[:, 0:128], in_=pT[:, 128:256], func=AF.Copy), reads=[pTB], writes=[ktTB])
                Qbd, QbdB = smb_ring.next()
                for hh in range(4):
                    S.op("dve", lambda h, hh=hh: h.tensor_scalar(out=Qbd[:, hh * 128:(hh + 1) * 128], in0=pT[:, 0:128], scalar1=hmask[:, hh:hh + 1],
                                                                 scalar2=None, op0=ALU.mult),
                         reads=[pTB] + CONSTS, writes=[QbdB] if hh == 0 else [], appends=[QbdB] if hh else [])
                if G < 4:
                    continue
                pAt, pAtB = pA_ring.next()
                S.op("pe", lambda h: h.matmul(pAt[:, 0:512], lhsT=ktT[:, 0:128], rhs=Qbd[:, 0:512], start=True, stop=True), reads=[ktTB, QbdB], writes=[pAtB])
                AT, ATB = smb_ring.next()
                S.op("dve", lambda h: h.tensor_tensor(out=AT[:, 0:512], in0=pAt[:, 0:512], in1=cmaskb[:].rearrange("p a b -> p (a b)"), op=ALU.mult),
                     reads=[pAtB] + CONSTS, writes=[ATB])
                pO, pOB = pS_ring.next()
                for hh in range(4):
                    S.op("pe", lambda h, hh=hh: h.matmul(pO[:, hh * 64:(hh + 1) * 64], lhsT=AT[:, hh * 128:(hh + 1) * 128], rhs=vb[:, hh * 64:(hh + 1) * 64],
                                                         start=True, stop=False), reads=[ATB, vbB], writes=[pOB] if hh == 0 else [], appends=[pOB] if hh else [])
                    S.op("pe", lambda h, hh=hh: h.matmul(pO[:, hh * 64:(hh + 1) * 64], lhsT=Qbd[:, hh * 128:(hh + 1) * 128], rhs=Sbf[:],
                                                         start=False, stop=True), reads=[QbdB, B_Sb], appends=[pOB])
                pN, pNB = pS_ring.next()
                for hh in range(4):
                    S.op("pe", lambda h, hh=hh: h.matmul(pN[:, 0:64], lhsT=Kbd[:, hh, :], rhs=vb[:, hh * 64:(hh + 1) * 64], start=(hh == 0), stop=(hh == 3)),
                         reads=[KbdB, vbB], writes=[pNB] if hh == 0 else [], appends=[pNB] if hh else [])
                S.op("dve", lambda h: h.scalar_tensor_tensor(out=Sst[:], in0=Sst[:], scalar=dec[:, 0:1], in1=pN[:, 0:64], op0=ALU.mult, op1=ALU.add),
                     reads=[decB, pNB, B_S], writes=[B_S])
                S.op("dve", lambda h: h.tensor_copy(out=Sbf[:], in_=Sst[:]), reads=[B_S], writes=[B_Sb])
                if G < 5:
                    continue
                of, ofB = sm_ring.next()
                S.op("act", lambda h: h.activation(out=of[:, 0:256], in_=pO[:, 0:256], func=AF.Copy), reads=[pOB], writes=[ofB])
                osq, osqB = sm_ring.next()
                S.op("pool", lambda h: h.tensor_tensor(out=osq[:, 0:256], in0=of[:, 0:256], in1=of[:, 0:256], op=ALU.mult), reads=[ofB], writes=[osqB])
                ss, ssB = col_ring.next()
                S.op("dve", lambda h: h.tensor_reduce(out=ss[:, 0:4], in_=osq[:, 0:256].rearrange("p (h e) -> p h e", e=64), axis=AX.X, op=ALU.add),
                     reads=[osqB], writes=[ssB])
                S.op("dve", lambda h: h.tensor_scalar(out=ss[:, 0:4], in0=ss[:, 0:4], scalar1=1.0 / 64.0, scalar2=EPS, op0=ALU.mult, op1=ALU.add),
                     reads=[ssB], writes=[ssB])
                S.op("act", lambda h: h.activation(out=ss[:, 0:4], in_=ss[:, 0:4], func=AF.Sqrt), reads=[ssB], writes=[ssB])
                S.op("dve", lambda h: h.reciprocal(out=ss[:, 0:4], in_=ss[:, 0:4]), reads=[ssB], writes=[ssB])
                for hh in range(4):
                    S.op("dve", lambda h, hh=hh: h.tensor_scalar(out=of[:, hh * 64:(hh + 1) * 64], in0=of[:, hh * 64:(hh + 1) * 64], scalar1=ss[:, hh:hh + 1],
                                                                 scalar2=None, op0=ALU.mult), reads=[ofB, ssB], writes=[ofB])
                S.op("dve", lambda h: h.tensor_tensor(out=of[:, 0:256], in0=of[:, 0:256], in1=gng[:], op=ALU.mult), reads=[ofB] + LW, writes=[ofB])
                pCg, pCgB = proj_tm(xt, xtB, 3912, 256)
                cgs, cgsB = sm_ring.next()
                S.op("act", lambda h: h.activation(out=cgs[:, 0:256], in_=pCg[:, 0:256], func=AF.Silu), reads=[pCgB], writes=[cgsB])
                yc16, yc16B = smb_ring.next()
                S.op("dve", lambda h: h.tensor_tensor(out=yc16[:, 0:256], in0=of[:, 0:256], in1=cgs[:, 0:256], op=ALU.mult), reads=[ofB, cgsB], writes=[yc16B])
                tr_to(ybc[:, 2:4, :].rearrange("p a b -> p (a b)"), ybcB, yc16, yc16B, 2, first=False)
                dma("sp", ybc_d[:, :, t0:t0 + 128].rearrange("a p t -> p a t"), ybc[:], [ybcB], [DB("ybc", blk)], ybcD)

        S.barrier()
        if SL < 6:
            continue
        with ExitStack() as pd:
            def sbd(name, shape, dt):
                return pd.enter_context(nc.sbuf_tensor("%s_d%d" % (name, l), list(shape), dt))

            def slotd(name, shape, dt, n=2):
                items = []
                for i in range(n):
                    t = sbd("%s_%d" % (name, i), shape, dt)
                    items.append((t, Buf("%s_%d" % (name, i)), S.dsem("d%s_%d" % (name, i))))
                return Ring(items)

            kiT = sbd("kiT_sb", [128, T], BF16)
            kiB = Buf("kiT_sb")
            dma_multi("sp", [(kiT[0:64, :], kiT_d[:, :]), (kiT[64:128, :], kiT_d[:, :])],
                      [DB("kiT", b) for b in range(NT)], [kiB], S.dsem("kiT_sb"))
            wob = sbd("wob", [128, 8, 1024], BF16)
            B_wo = Buf("wob")
            rl_ring = Ring([(sbd("rl%d" % i, [128, 1024], F32), Buf("rl%d" % i), S.dsem("drl%d" % i)) for i in range(3)])
            wo_v = w_out[l].rearrange("(kc p) n -> p kc n", p=128)
            for hf in range(8):
                wos, wosB, wosD = rl_ring.next()
                wv = wos[:].rearrange("p (a b) -> p a b", b=128)
                dma("sp", wv, wo_v[:, :, hf * 128:(hf + 1) * 128], [], [wosB], wosD)
                if hf % 2:
                    S.op("act", lambda h: h.activation(out=wob[:, :, hf * 128:(hf + 1) * 128], in_=wv, func=AF.Copy), reads=[wosB], appends=[B_wo])
                else:
                    S.op("dve", lambda h: h.tensor_copy(out=wob[:, :, hf * 128:(hf + 1) * 128], in_=wv), reads=[wosB], appends=[B_wo])
            rl_ring = Ring([(a, b) for (a, b, c) in rl_ring.items])
            lng = sbd("lng", [128, 1024], F32)
            lnb = sbd("lnb", [128, 1024], F32)
            B_ln = Buf("ln")
            dln = S.dsem("ln")
            dma("sp", lng[:], ln_g[l], [], [], dln, appends=[B_ln])
            dma("sp", lnb[:], ln_b[l], [], [], dln, appends=[B_ln])

            score = sbd("score", [128, T], F32)
            scB = Buf("score")
            m01 = sbd("m01", [128, T], BF16)
            m01B = Buf("m01")
            junk = m01
            jkB = m01B
            mT_ring = Ring([(sbd("mT%d" % i, [128, NT, 128], BF16), Buf("mT%d" % i)) for i in range(2)])
            q_ring = slotd("qd", [128, 4, 128], BF16, 2)
            qi_ring = slotd("qid", [128, 4, 128], BF16, 2)
            wi_ring = slotd("wid", [128, 8], F32, 2)
            kv_ring = slotd("kvd", [128, 4, 512 + 520], BF16, 3)
            pt_ring = Ring([(sbd("pt%d" % i, [128, 512], BF16), Buf("pt%d" % i)) for i in range(4)])
            bis = sbd("bis", [128, 8], F32)
            bisB = Buf("bis")
            bisA = sbd("bisA", [128, 2], F32)
            bisAB = Buf("bisA")
            jkAB = Buf("junkA")
            junkA = sbd("junkA", [128, (T * 3) // 10 + 128], BF16)
            hs = sbd("hs", [128, NBIS + 1], F32)
            hsB = Buf("hs")
            ya = sbd("ya", [128, 8, 64], F32)
            yaB = Buf("ya")
            agl_ring = slotd("agl", [128, 512], F32, 1)
            ya16 = sbd("ya16", [128, 512], BF16)
            ya16B = Buf("ya16")
            mixT = sbd("mixT", [128, 4, 128], BF16)
            mixB = Buf("mixT")
            ybl_ring = slotd("ybl", [128, 4, 128], BF16, 2)
            xl2_ring = slotd("xl2", [128, 1024], F32, 1)
            z_ring = slotd("z", [128, 1024], F32, 2)
            stat = sbd("stat", [128, 2, 6], F32)
            statB = Buf("stat")
            mv = sbd("mv", [128, 4], F32)
            mvB = Buf("mv")
            rcp = sbd("rcp", [128, 8], F32)
            rcpB = Buf("rcp")
            pOa, pOaB = (pS_ring.items[0][0][:, 0:260].rearrange("p (a b) -> p a b", b=65), pS_ring.items[0][1])
            pOb, pObB = (pS_ring.items[1][0][:, 0:260].rearrange("p (a b) -> p a b", b=65), pS_ring.items[1][1])
            ST = {}

            def stage1a(qb):
                t0 = qb * 128
                n = t0 + 128
                qd, qdB, qdD = q_ring.next()
                dma("sp", qd[:], qT_d[:, :, t0:t0 + 128].rearrange("a p t -> p a t"), [DB("qT", qb)], [qdB], qdD)
                qid, qidB, qidD = qi_ring.next()
                dma("sp", qid[:], qiT_d[:, :, t0:t0 + 128].rearrange("a p t -> p a t"), [DB("qiT", qb)], [qidB], qidD)
                wid, widB, widD = wi_ring.next()
                dma("sp", wid[:], wi_d[t0:t0 + 128, :], [DB("wi", qb)], [widB], widD)
                ST[qb] = (qd, qdB)
                nch = (n + 511) // 512
                steps = []

                def idx_step(hh, c2):
                    def run():
                        rl, rlB = rl_ring.next()
                        wtot = 0
                        for c in range(c2, min(c2 + 2, nch)):
                            k0 = c * 512
                            kw = min(512, n - k0)
                            pA, pAB = pA_ring.next()
                            pb = (hh % 2) * 64
                            S.op("pe", lambda h: h.matmul(pA[:, 0:kw], lhsT=qid[pb:pb + 64, hh // 2, :], rhs=kiT[pb:pb + 64, k0:k0 + kw],
                                                          start=True, stop=True), reads=[qidB, kiB], writes=[pAB])
                            off = (c - c2) * 512
                            S.op("act", lambda h: h.activation(out=rl[:, off:off + kw], in_=pA[:, 0:kw], func=AF.Relu),
                                 reads=[pAB], writes=[rlB] if c == c2 else [], appends=[rlB] if c != c2 else [])
                            wtot += kw
                        k0 = c2 * 512
                        if hh == 0:
                            S.op("dve", lambda h: h.tensor_scalar(out=score[:, k0:k0 + wtot], in0=rl[:, 0:wtot], scalar1=wid[:, 0:1],
                                                                  scalar2=None, op0=ALU.mult),
                                 reads=[rlB, widB], writes=[scB] if c2 == 0 else [], appends=[scB] if c2 else [])
                        else:
                            S.op("dve", lambda h: h.scalar_tensor_tensor(out=score[:, k0:k0 + wtot], in0=rl[:, 0:wtot],
                                                                         scalar=wid[:, hh:hh + 1], in1=score[:, k0:k0 + wtot],
                                                                         op0=ALU.mult, op1=ALU.add),
                                 reads=[rlB, widB, scB], writes=[scB])
                    return run

                for hh in range(8):
                    for c2 in range(0, nch, 2):
                        steps.append(idx_step(hh, c2))
                steps.append(lambda: bis_step(qb, n))
                return steps

            def bis_step(qb, n):
                S.op("dve", lambda h: h.tensor_reduce(out=bis[:, 0:1], in_=score[:, 0:n], axis=AX.X, op=ALU.max, apply_absolute_value=True),
                     reads=[scB], writes=[bisB])
                S.op("dve", lambda h: h.tensor_scalar(out=bis[:, 0:1], in0=bis[:, 0:1], scalar1=1.0, scalar2=None, op0=ALU.add), reads=[bisB], writes=[bisB])
                S.op("dve", lambda h: h.tensor_scalar(out=hs[:], in0=pow2[:, 0:NBIS + 1], scalar1=bis[:, 0:1], scalar2=None, op0=ALU.mult),
                     reads=[bisB] + CONSTS, writes=[hsB])
                S.op("dve", lambda h: h.memset(bis[:, 3:4], 0.0), reads=[bisB], writes=[bisB])
                S.op("dve", lambda h: h.tensor_tensor(out=score[:, n - 128:n], in0=score[:, n - 128:n], in1=cbias[:], op=ALU.add), reads=[scB, bisB] + CONSTS, writes=[scB])
                nA = 0
                nD = n - nA
                for it in range(NBIS):
                    if nA:
                        S.op("act", lambda h: h.activation(out=junkA[:, 0:nA], in_=score[:, nD:n], func=AF.Sign, scale=-1.0, bias=bis[:, 3:4],
                                                           accum_out=bisA[:, 0:1]), reads=[scB, bisB], writes=[bisAB, jkAB])
                    S.op("dve", lambda h: h.tensor_scalar(out=junk[:, 0:nD], in0=score[:, 0:nD], scalar1=bis[:, 3:4], scalar2=None, op0=ALU.is_ge, op1=ALU.add,
                                                          accum_out=bis[:, 4:5]), reads=[scB, bisB], writes=[bisB, jkB])
                    if nA:
                        S.op("dve", lambda h: h.scalar_tensor_tensor(out=bis[:, 4:5], in0=bisA[:, 0:1], scalar=-0.5, in1=bis[:, 4:5], op0=ALU.mult, op1=ALU.add),
                             reads=[bisB, bisAB], writes=[bisB])
                    S.op("dve", lambda h: h.tensor_scalar(out=bis[:, 5:6], in0=bis[:, 4:5], scalar1=float(topk) - 0.5 - nA / 2.0, scalar2=hs[:, it:it + 1], op0=ALU.is_ge, op1=ALU.mult),
                         reads=[bisB, hsB], writes=[bisB])
                    S.op("dve", lambda h: h.scalar_tensor_tensor(out=bis[:, 3:4], in0=bis[:, 5:6], scalar=hs[:, it + 1:it + 2], in1=bis[:, 3:4], op0=ALU.subtract, op1=ALU.add),
                         reads=[bisB, hsB], writes=[bisB])
                S.op("dve", lambda h: h.tensor_tensor(out=bis[:, 1:2], in0=bis[:, 3:4], in1=hs[:, NBIS:NBIS + 1], op=ALU.subtract), reads=[bisB, hsB], writes=[bisB])
                S.op("dve", lambda h: h.tensor_scalar(out=m01[:, 0:n], in0=score[:, 0:n], scalar1=bis[:, 1:2], scalar2=None, op0=ALU.is_ge),
                     reads=[scB, bisB], writes=[m01B])

            def stage1b(qb):
                nkb = qb + 1
                mT, mTB = mT_ring.next()
                ST[qb] = ST[qb] + (mT, mTB)
                for kb8 in range(0, nkb, 8):
                    cnt = min(8, nkb - kb8)
                    pT, pTB = pT_ring.next()
                    for i in range(cnt):
                        kb = kb8 + i
                        S.op("pe", lambda h: h.transpose(out=pT[:, i * 128:(i + 1) * 128], in_=m01[:, kb * 128:(kb + 1) * 128], identity=identb[:]),
                             reads=[m01B] + CONSTS, writes=[pTB] if i == 0 else [], appends=[pTB] if i else [])
                    S.op("act", lambda h: h.activation(out=mT[:, kb8:kb8 + cnt, :].rearrange("p a b -> p (a b)"), in_=pT[:, 0:cnt * 128], func=AF.Copy),
                         reads=[pTB], writes=[mTB] if kb8 == 0 else [], appends=[mTB] if kb8 else [])

            def stage2(qb, steps):
                t0 = qb * 128
                qd, qdB, mT, mTB = ST.pop(qb)
                nkb = qb + 1
                ngr = (nkb + 3) // 4
                units = [(g, hh) for g in range(ngr) for hh in range(8)]
                ust = {}
                kvs = {}
                LOOK = 2

                def emit_qk(u):
                    g, hh = units[u]
                    kb0 = g * 4
                    nb = min(4, nkb - kb0)
                    if hh == 0:
                        kvt, kvB_, kvD = kv_ring.next()
                        pairs = [(kvt[:, :, 0:nb * 128], kT_d[:, :, kb0 * 128:(kb0 + nb) * 128].rearrange("a p t -> p a t"))]
                        for j in range(nb):
                            pairs.append((kvt[:, j, 512:512 + 520], V_d[(kb0 + j) * 128:(kb0 + j + 1) * 128, :]))
                        dma_multi("sp", pairs, [DB("kT", kb0 + j) for j in range(nb)] + [DB("V", kb0 + j) for j in range(nb)], [kvB_], kvD)
                        kvs[g] = (kvt, kvB_)
                    kvt, kvB_ = kvs[g]
                    pb = (hh % 2) * 64
                    pL_, pLB_ = pA_ring.next()
                    for j in range(nb):
                        S.op("pe", lambda h: h.matmul(pL_[:, j * 128:(j + 1) * 128], lhsT=kvt[pb:pb + 64, hh // 2, j * 128:(j + 1) * 128],
                                                      rhs=qd[pb:pb + 64, hh // 2, :], start=True, stop=True),
                             reads=[kvB_, qdB], writes=[pLB_] if j == 0 else [], appends=[pLB_] if j else [])
                    pt, ptB = pt_ring.next()
                    S.op("act", lambda h: h.activation(out=pt[:, 0:nb * 128], in_=pL_[:, 0:nb * 128], func=AF.Exp, scale=0.125),
                         reads=[pLB_], writes=[ptB])
                    S.op("pool", lambda h: h.tensor_tensor(out=pt[:, 0:nb * 128], in0=pt[:, 0:nb * 128],
                                                           in1=mT[:, kb0:kb0 + nb, :].rearrange("p a b -> p (a b)"), op=ALU.mult),
                         reads=[ptB, mTB], writes=[ptB])
                    ust[u] = (pt, ptB, kvt, kvB_, nb)

                def emit_pv(u):
                    g, hh = units[u]
                    pt, ptB, kvt, kvB_, nb = ust.pop(u)
                    po, poB = (pOa, pOaB) if hh < 4 else (pOb, pObB)
                    for j in range(nb):
                        first = (g == 0 and j == 0)
                        last = (g == ngr - 1 and j == nb - 1)
                        S.op("pe", lambda h: h.matmul(po[:, hh % 4, :], lhsT=pt[:, j * 128:(j + 1) * 128], rhs=kvt[:, j, 512 + hh * 65:512 + (hh + 1) * 65],
                                                      start=(first and hh % 4 == 0), stop=last, skip_group_check=True),
                             reads=[ptB, kvB_], writes=[poB] if (first and hh % 4 == 0) else [], appends=[] if (first and hh % 4 == 0) else [poB])

                nst = len(steps)
                done = 0
                for u in range(len(units) + LOOK):
                    want = min(nst, u + 1)
                    while done < want:
                        steps[done]()
                        done += 1
                    if u < len(units):
                        emit_qk(u)
                    if u - LOOK >= 0:
                        emit_pv(u - LOOK)
                while done < nst:
                    steps[done]()
                    done += 1
                for hd in range(8):
                    po, poB = (pOa, pOaB) if hd < 4 else (pOb, pObB)
                    S.op("dve", lambda h: h.reciprocal(out=rcp[:, hd:hd + 1], in_=po[:, hd % 4, 64:65]), reads=[poB], writes=[rcpB] if hd == 0 else [],
                         appends=[rcpB] if hd else [])
                for hd in range(8):
                    po, poB = (pOa, pOaB) if hd < 4 else (pOb, pObB)
                    S.op("dve", lambda h: h.tensor_scalar(out=ya[:, hd, :], in0=po[:, hd % 4, 0:64], scalar1=rcp[:, hd:hd + 1], scalar2=None, op0=ALU.mult),
                         reads=[poB, rcpB], writes=[yaB] if hd == 0 else [], appends=[yaB] if hd else [])
                agl, aglB, aglD = agl_ring.next()
                dma("sp", agl[:], ag_d[t0:t0 + 128, :], [DB("ag", qb)], [aglB], aglD)
                S.op("dve", lambda h: h.tensor_tensor(out=ya16[:], in0=ya[:].rearrange("p a b -> p (a b)"), in1=agl[:], op=ALU.mult), reads=[yaB, aglB], writes=[ya16B])
                def t_step1():
                    pT, pTB = pT_ring.next()
                    for i in range(4):
                        S.op("pe", lambda h: h.transpose(out=pT[:, i * 128:(i + 1) * 128], in_=ya16[:, i * 128:(i + 1) * 128], identity=identb[:]),
                             reads=[ya16B] + CONSTS, writes=[pTB] if i == 0 else [], appends=[pTB] if i else [])
                    S.op("dve", lambda h: h.tensor_copy(out=mixT[:].rearrange("p a b -> p (a b)"), in_=pT[:, 0:512]), reads=[pTB], writes=[mixB])

                tst = {}

                def t_step2():
                    ybl, yblB, yblD = ybl_ring.next()
                    dma("sp", ybl[:], ybc_d[:, :, t0:t0 + 128].rearrange("a p t -> p a t"), [DB("ybc", qb)], [yblB], yblD)
                    xl2, xl2B, xl2D = xl2_ring.next()
                    dma("sp", xl2[:], x_src[t0:t0 + 128, :], [DB("x", qb)], [xl2B], xl2D)
                    z, zB, zD = z_ring.next()
                    tst["z"] = (z, zB, zD)
                    for hf in range(2):
                        pY, pYB = pA_ring.next()
                        for kc in range(8):
                            S.op("pe", lambda h: h.matmul(pY[:, 0:512], lhsT=(mixT[:, kc, :] if kc < 4 else ybl[:, kc - 4, :]),
                                                          rhs=wob[:, kc, hf * 512:(hf + 1) * 512], start=(kc == 0), stop=(kc == 7)),
                                 reads=[mixB, yblB, B_wo], writes=[pYB] if kc == 0 else [], appends=[pYB] if kc else [])
                        S.op("dve", lambda h: h.scalar_tensor_tensor(out=z[:, hf * 512:(hf + 1) * 512], in0=xl2[:, hf * 512:(hf + 1) * 512], scalar=alpha,
                                                                     in1=pY[:, 0:512], op0=ALU.mult, op1=ALU.add),
                             reads=[pYB, xl2B], writes=[zB] if hf == 0 else [], appends=[zB] if hf else [])
                        S.op("dve", lambda h: h.bn_stats(out=stat[:, hf, :], in_=z[:, hf * 512:(hf + 1) * 512]), reads=[zB], writes=[statB] if hf == 0 else [],
                             appends=[statB] if hf else [])

                def t_step3():
                    z, zB, zD = tst["z"]
                    S.op("dve", lambda h: h.bn_aggr(out=mv[:, 0:2], in_=stat[:].rearrange("p a b -> p (a b)")), reads=[statB], writes=[mvB])
                    S.op("dve", lambda h: h.tensor_scalar(out=mv[:, 2:3], in0=mv[:, 1:2], scalar1=EPS, scalar2=None, op0=ALU.add), reads=[mvB], writes=[mvB])
                    S.op("act", lambda h: h.activation(out=mv[:, 2:3], in_=mv[:, 2:3], func=AF.Sqrt), reads=[mvB], writes=[mvB])
                    S.op("dve", lambda h: h.reciprocal(out=mv[:, 3:4], in_=mv[:, 2:3]), reads=[mvB], writes=[mvB])
                    S.op("dve", lambda h: h.tensor_scalar(out=z[:], in0=z[:], scalar1=mv[:, 0:1], scalar2=mv[:, 3:4], op0=ALU.subtract, op1=ALU.mult), reads=[zB, mvB], writes=[zB])
                    S.op("dve", lambda h: h.tensor_tensor(out=z[:], in0=z[:], in1=lng[:], op=ALU.mult), reads=[zB, B_ln], writes=[zB])
                    S.op("dve", lambda h: h.tensor_tensor(out=z[:], in0=z[:], in1=lnb[:], op=ALU.add), reads=[zB, B_ln], writes=[zB])
                    dma("sp", x_dst[t0:t0 + 128, :], z[:], [zB], [DB("x", qb)], zD)
                    if l < DEPTH - 1:
                        emit_xT(qb, z[:], zB)

                return [t_step1, t_step2, t_step3]

            for st_ in stage1a(0):
                st_()
            stage1b(0)
            pend = []
            for qb in range(NT):
                pend = stage2(qb, pend + (stage1a(qb + 1) if qb + 1 < NT else []))
                if qb + 1 < NT:
                    stage1b(qb + 1)
            for st_ in pend:
                st_()

    S.barrier()
    print('KERNEL nops', S.nops, 'sems', len(S.dsems) + 5, flush=True)
    S.emit()
    es.close()
    return nc


def make_consts():
    ident = np.eye(128, dtype=np.float32)
    j = np.arange(128)
    tri = (j[:, None] <= j[None, :]).astype(np.float32)
    cb = np.where(j[None, :] <= j[:, None], 0.0, -1e30).astype(np.float32)
    hm = np.zeros((128, 4), np.float32)
    for h in range(4):
        hm[h * 32:(h + 1) * 32, h] = 1.0
    inv = (10000.0 ** (-np.arange(0, 64, 2, dtype=np.float32) / 64.0)).astype(np.float32)
    inv8 = np.tile((inv / np.float32(2.0 * math.pi)).astype(np.float32), 8)[None, :].repeat(128, 0).astype(np.float32)
    p2 = np.broadcast_to((2.0 ** (-np.arange(32, dtype=np.float32)))[None, :], (128, 32)).astype(np.float32)
    return {"c_pow2": np.ascontiguousarray(p2), "c_ident": ident, "c_tri": tri, "c_cbias": cb, "c_hmask": hm, "c_inv8": np.ascontiguousarray(inv8)}


_CACHE = {}
_NCORES = [8]
_DBG = [False]
_LAST = [None]


def kernel(x, positions, w_in, conv_w, conv_b, cln_g, cln_b, pw_w, pw_b, gate_w2, gate_b, gnorm_g, w_out, ln_g, ln_b):
    x = np.asarray(x)
    B, T, _ = x.shape
    DEPTH = int(np.asarray(w_in).shape[0])
    key = (T, DEPTH)
    if key not in _CACHE:
        _CACHE[key] = build_program(T, DEPTH, dbg=_DBG[0])
    nc = _CACHE[key]
    consts = make_consts()
    shared = {"w_in": w_in, "conv_w": conv_w, "conv_b": conv_b, "cln_g": cln_g, "cln_b": cln_b, "pw_w": pw_w, "pw_b": pw_b,
              "gate_w2": gate_w2, "gate_b": gate_b, "gnorm_g": gnorm_g, "w_out": w_out, "ln_g": ln_g, "ln_b": ln_b}
    shared = {k: np.asarray(v, dtype=np.float32) for k, v in shared.items()}
    shared["conv_w"] = shared["conv_w"].reshape(DEPTH, CONVW, 2, 128).transpose(0, 2, 3, 1)
    for nm in ("conv_b", "cln_g", "cln_b", "pw_b"):
        shared[nm] = shared[nm].reshape(DEPTH, 2, 128).transpose(0, 2, 1)
    for nm in ("gnorm_g", "ln_g", "ln_b"):
        shared[nm] = np.broadcast_to(shared[nm][:, None, :], (DEPTH, 128, shared[nm].shape[-1]))
    shared = {k: np.ascontiguousarray(v) for k, v in shared.items()}
    n_cores = _NCORES[0]
    in_maps = []
    for c in range(n_cores):
        b = c % B
        m = {"x": np.ascontiguousarray(x[b]), "positions": np.ascontiguousarray(np.asarray(positions)[b].astype(np.int32).reshape(T // 128, 128).T)}
        m.update(shared)
        m.update(consts)
        in_maps.append(m)
    res = run_bass_kernel_spmd(nc, in_maps, core_ids=list(range(n_cores)))
    _LAST[0] = res
    return np.stack([np.asarray(res.results[b]["out"]) for b in range(min(B, n_cores))], axis=0).astype(np.float32)
```
